# Optimizing a Trainium2 kernel written in Bass

```python
import jax, jax.numpy as jnp
from jax import lax
import numpy as np

D_MODEL = 2048
BATCH = 4
SEQ = 4096
DEPTH = 2

GRID_W = 64
CTX_LEN = 256
N_MIXERS = 2
N_SSD_LAYERS = (DEPTH + 1) // 2
N_SMLP_LAYERS = DEPTH // 2
EPS = 1e-6

SSD_EXPAND = 2
SSD_D_INNER = SSD_EXPAND * D_MODEL
SSD_HEAD_DIM = 64
SSD_N_HEADS = SSD_D_INNER // SSD_HEAD_DIM
SSD_N_GROUPS = 8
SSD_HEADS_PER_GROUP = SSD_N_HEADS // SSD_N_GROUPS
SSD_D_STATE = 128
SSD_CONV_CAUSAL_W = 4
SSD_CONV_W = 2 * (SSD_CONV_CAUSAL_W - 1) + 1
SSD_CHUNK = 128
SSD_CONV_DIM = SSD_D_INNER + 2 * SSD_N_GROUPS * SSD_D_STATE
SSD_IN_DIM = SSD_D_INNER + SSD_CONV_DIM + 2 * SSD_N_HEADS

SMLP_EXPAND = 2
SMLP_D_INNER = SMLP_EXPAND * D_MODEL
SMLP_CHUNK = 128
SMLP_N_GROUPS = 16
SMLP_GROUP_W = SMLP_D_INNER // SMLP_N_GROUPS

kernel_name = 'hybrid_ssd_chunkmlp_prefix_dit'


def rms_norm(x, gain=None):
    xf = x.astype(jnp.float32)
    y = xf * lax.rsqrt(jnp.mean(xf * xf, axis=-1, keepdims=True) + EPS)
    if gain is not None:
        y = y * gain.astype(jnp.float32)
    return y.astype(x.dtype)


def layer_norm(x, gain, bias):
    xf = x.astype(jnp.float32)
    mu = jnp.mean(xf, axis=-1, keepdims=True)
    var = jnp.mean(jnp.square(xf - mu), axis=-1, keepdims=True)
    y = (xf - mu) * lax.rsqrt(var + EPS) * gain.astype(jnp.float32) + bias.astype(jnp.float32)
    return y.astype(x.dtype)


def adaln(cvec, w_mod, b_mod):
    m = jax.nn.silu(cvec) @ w_mod + b_mod
    return jnp.split(m, 3, axis=-1)


def modulate(x, shift, scale):
    return rms_norm(x) * (1 + scale) + shift


def depthwise_conv_centred(x, w, b):
    ch = x.shape[-1]
    y = lax.conv_general_dilated(x, w[:, None, :], window_strides=(1,), padding='SAME',
                                 dimension_numbers=('NWC', 'WIO', 'NWC'),
                                 feature_group_count=ch)
    return y + b


def ssd_chunked(x, dt, a_head, Bm, Cm, state0):
    b, L, H, P = x.shape
    G, N = Bm.shape[-2:]
    R = H // G
    Q = SSD_CHUNK
    nc = L // Q
    dtype = x.dtype
    a = (dt * a_head).reshape(b, nc, Q, G, R)
    a_cum = jnp.cumsum(a, axis=2)
    xdt = (x * dt[..., None].astype(dtype)).reshape(b, nc, Q, G, R, P)
    Bc = Bm.reshape(b, nc, Q, G, N)
    Cc = Cm.reshape(b, nc, Q, G, N)
    diff = a_cum[:, :, :, None] - a_cum[:, :, None]
    mask = jnp.tril(jnp.ones((Q, Q), dtype=bool))[:, :, None, None]
    decay = jnp.exp(jnp.where(mask, diff, -jnp.inf)).astype(dtype)
    cb = jnp.einsum('bcqgn,bckgn->bcqkg', Cc, Bc)
    y_diag = jnp.einsum('bcqkg,bcqkgr,bckgrp->bcqgrp', cb, decay, xdt)
    decay_to_end = jnp.exp(a_cum[:, :, -1:] - a_cum).astype(dtype)
    chunk_states = jnp.einsum('bckgn,bckgr,bckgrp->bcgrpn', Bc, decay_to_end, xdt)
    chunk_decay = jnp.exp(a_cum[:, :, -1])

    def step(s, inp):
        dec, st = inp
        s_next = s * dec[..., None, None] + st.astype(jnp.float32)
        return s_next, s

    s_final, s_in = lax.scan(step, state0.reshape(b, G, R, P, N),
                             (jnp.moveaxis(chunk_decay, 1, 0), jnp.moveaxis(chunk_states, 1, 0)))
    s_in = jnp.moveaxis(s_in, 0, 1).astype(dtype)
    y_off = jnp.einsum('bcqgn,bcgrpn,bcqgr->bcqgrp', Cc, s_in, jnp.exp(a_cum).astype(dtype))
    y = (y_diag + y_off).reshape(b, L, H, P)
    return y, s_final.reshape(b, H, P, N)


def ssd_mixer(h_lat, h_ctx, w_in, conv_w, conv_b, dt_bias, a_log, d_skip, norm_g, w_out, ctx_out):
    H, P, G, N = SSD_N_HEADS, SSD_HEAD_DIM, SSD_N_GROUPS, SSD_D_STATE

    def project(h):
        b, L, _ = h.shape
        z, xbc, dt_raw = jnp.split(h @ w_in, [SSD_D_INNER, SSD_D_INNER + SSD_CONV_DIM], axis=-1)
        xbc = jax.nn.silu(depthwise_conv_centred(xbc, conv_w, conv_b))
        xs, Bm, Cm = jnp.split(xbc, [SSD_D_INNER, SSD_D_INNER + G * N], axis=-1)
        dt = jax.nn.softplus(dt_raw.astype(jnp.float32).reshape(b, L, 2, H)
                             + dt_bias.astype(jnp.float32))
        return (z, xs.reshape(b, L, H, P), Bm.reshape(b, L, G, N), Cm.reshape(b, L, G, N), dt)

    def flip(t):
        return jnp.flip(t, axis=1)

    def finish(z, xs, y_f, y_b_flipped):
        b, L = xs.shape[:2]
        y = y_f + flip(y_b_flipped) + d_skip[:, None].astype(xs.dtype) * xs
        y = y.reshape(b, L, SSD_D_INNER) * jax.nn.silu(z)
        return rms_norm(y, norm_g) @ w_out

    a_head = -jnp.exp(a_log.astype(jnp.float32))
    zc, xc, Bc, Cc, dtc = project(h_ctx)
    zl, xl, Bl, Cl, dtl = project(h_lat)
    zero = jnp.zeros((h_ctx.shape[0], H, P, N), jnp.float32)
    yc_f, sc_f = ssd_chunked(xc, dtc[:, :, 0], a_head[0], Bc, Cc, zero)
    yl_f, _ = ssd_chunked(xl, dtl[:, :, 0], a_head[0], Bl, Cl, sc_f)
    yc_b, sc_b = ssd_chunked(flip(xc), flip(dtc[:, :, 1]), a_head[1], flip(Bc), flip(Cc), zero)
    yl_b, _ = ssd_chunked(flip(xl), flip(dtl[:, :, 1]), a_head[1], flip(Bl), flip(Cl), sc_b)
    out_l = finish(zl, xl, yl_f, yl_b)
    out_c = finish(zc, xc, yc_f, yc_b) if ctx_out else None
    return out_l, out_c


def chunk_mlp(h, w_in, ln_g, ln_b, w_s, b_s, w_out):
    b, L, _ = h.shape
    u, v, g = jnp.split(h @ w_in, 3, axis=-1)
    u = jax.nn.gelu(u)
    v = layer_norm(jax.nn.gelu(v), ln_g, ln_b)
    nc = L // SMLP_CHUNK
    v = v.reshape(b, nc, SMLP_CHUNK, SMLP_N_GROUPS, SMLP_GROUP_W)
    v = jnp.einsum('gqk,bckgw->bcqgw', w_s, v) + b_s.T[None, None, :, :, None]
    s = u * v.reshape(b, L, SMLP_D_INNER)
    return (s * jax.nn.silu(g)) @ w_out


def setup_inputs(seed: int = 0) -> dict:
    key = jax.random.key(seed)
    ks = jax.random.split(key, 24)
    f32 = jnp.float32
    D, H, E = D_MODEL, SSD_N_HEADS, SMLP_D_INNER
    dt0 = jnp.exp(jax.random.uniform(ks[8], (N_SSD_LAYERS, 2, H), f32, np.log(1e-3), np.log(1e-1)))
    return {
        'x': jax.random.normal(ks[0], (BATCH, SEQ, D), f32),
        'c': jax.random.normal(ks[1], (BATCH, D), f32),
        'ctx': jax.random.normal(ks[2], (BATCH, CTX_LEN, D), f32),
        'c_ctx': jax.random.normal(ks[3], (D,), f32),
        'mod_w': jax.random.normal(ks[4], (DEPTH, D, 3 * D), f32) * (0.5 * D ** -0.5),
        'mod_b': jax.random.normal(ks[5], (DEPTH, 3 * D), f32) * 0.01,
        'ssd_w_in': jax.random.normal(ks[6], (N_SSD_LAYERS, D, SSD_IN_DIM), f32) * D ** -0.5,
        'ssd_conv_w': jax.random.normal(ks[7], (N_SSD_LAYERS, SSD_CONV_W, SSD_CONV_DIM), f32) * SSD_CONV_W ** -0.5,
        'ssd_conv_b': jax.random.normal(ks[9], (N_SSD_LAYERS, SSD_CONV_DIM), f32) * 0.01,
        'ssd_dt_bias': dt0 + jnp.log(-jnp.expm1(-dt0)),
        'ssd_a_log': jnp.log(jax.random.uniform(ks[10], (N_SSD_LAYERS, 2, H), f32, 1.0, 16.0)),
        'ssd_d': 1.0 + 0.1 * jax.random.normal(ks[11], (N_SSD_LAYERS, H), f32),
        'ssd_norm_g': 1.0 + 0.1 * jax.random.normal(ks[12], (N_SSD_LAYERS, SSD_D_INNER), f32),
        'ssd_w_out': jax.random.normal(ks[13], (N_SSD_LAYERS, SSD_D_INNER, D), f32) * SSD_D_INNER ** -0.5,
        'smlp_w_in': jax.random.normal(ks[14], (N_SMLP_LAYERS, D, 3 * E), f32) * D ** -0.5,
        'smlp_ln_g': 1.0 + 0.1 * jax.random.normal(ks[15], (N_SMLP_LAYERS, E), f32),
        'smlp_ln_b': 0.01 * jax.random.normal(ks[16], (N_SMLP_LAYERS, E), f32),
        'smlp_w_s': jax.random.normal(ks[17], (N_SMLP_LAYERS, SMLP_N_GROUPS, SMLP_CHUNK, SMLP_CHUNK), f32) * SMLP_CHUNK ** -0.5,
        'smlp_b_s': 1.0 + 0.1 * jax.random.normal(ks[18], (N_SMLP_LAYERS, SMLP_N_GROUPS, SMLP_CHUNK), f32),
        'smlp_w_out': jax.random.normal(ks[19], (N_SMLP_LAYERS, E, D), f32) * E ** -0.5,
        'final_norm_g': 1.0 + 0.1 * jax.random.normal(ks[20], (D,), f32),
    }


def reference(x, c, ctx, c_ctx, mod_w, mod_b, ssd_w_in, ssd_conv_w, ssd_conv_b, ssd_dt_bias,
              ssd_a_log, ssd_d, ssd_norm_g, ssd_w_out, smlp_w_in, smlp_ln_g, smlp_ln_b,
              smlp_w_s, smlp_b_s, smlp_w_out, final_norm_g):
    mixer_of = [i % N_MIXERS for i in range(DEPTH)]
    h_ctx = ctx
    for i in range(DEPTH):
        k = i // N_MIXERS
        ctx_later = any(mixer_of[j] == 0 for j in range(i + 1, DEPTH))
        shift, scale, gate = adaln(c, mod_w[i], mod_b[i])
        hl = modulate(x, shift[:, None], scale[:, None])
        if mixer_of[i] == 0 or ctx_later:
            shift_c, scale_c, gate_c = adaln(c_ctx, mod_w[i], mod_b[i])
            hc = modulate(h_ctx, shift_c, scale_c)
        if mixer_of[i] == 0:
            out_l, out_c = ssd_mixer(hl, hc, ssd_w_in[k], ssd_conv_w[k], ssd_conv_b[k],
                                     ssd_dt_bias[k], ssd_a_log[k], ssd_d[k], ssd_norm_g[k],
                                     ssd_w_out[k], ctx_later)
        else:
            sm = (smlp_w_in[k], smlp_ln_g[k], smlp_ln_b[k], smlp_w_s[k], smlp_b_s[k], smlp_w_out[k])
            out_l = chunk_mlp(hl, *sm)
            out_c = chunk_mlp(hc, *sm) if ctx_later else None
        x = x + gate[:, None] * out_l
        if ctx_later:
            h_ctx = h_ctx + gate_c * out_c
    return rms_norm(x, final_norm_g)
```

```python
import numpy as np
import ml_dtypes
import concourse.bass as bass
import concourse.mybir as mybir
from concourse.bass_utils import run_bass_kernel_spmd

F32 = mybir.dt.float32
BF16 = mybir.dt.bfloat16
AF = mybir.ActivationFunctionType
ALU = mybir.AluOpType
AX = mybir.AxisListType

D = 2048
KT = 16
T = 2048
NCH = 16
DI = 4096
H = 64
EPS = 1e-6
W_IN0 = 10368
OW = 2320
CTX0 = 2056


class Buf:
    __slots__ = ("name", "w", "r")

    def __init__(self, name):
        self.name = name
        self.w = None
        self.r = {}


class Eng:
    def __init__(self, name, h, sem, same_engine_sync=True):
        self.name = name
        self.h = h
        self.sem = sem
        self.cnt = 0
        self.waited = {}
        self.same = same_engine_sync


class Sched:
    def __init__(self, nc, sems):
        self.nc = nc
        it = iter(sems)
        self.pe = Eng("pe", nc.tensor, next(it), same_engine_sync=False)
        self.act = Eng("act", nc.scalar, next(it))
        self.dve = Eng("dve", nc.vector, next(it))
        self.pool = Eng("pool", nc.gpsimd, next(it))
        self.sp = Eng("sp", nc.sync, next(it))
        self.engs = [self.pe, self.act, self.dve, self.pool, self.sp]
        self.rings = {}
        for q in (self.sp, self.pool):
            self.rings[q.name] = {"sems": [next(it) for _ in range(12)], "vals": [0] * 12, "i": 0}
        self.n_ins = 0

    def _need(self, eng, deps):
        out = []
        best = {}
        for (sem, val, src) in deps:
            if src is eng and not eng.same:
                continue
            k = id(sem)
            if eng.waited.get(k, 0) >= val:
                continue
            if k not in best or best[k][1] < val:
                best[k] = (sem, val)
        for k, (sem, val) in best.items():
            eng.waited[k] = val
            out.append((sem, val))
        return out

    def _deps(self, reads, writes):
        deps = []
        for b in reads:
            if b.w is not None:
                deps.append(b.w)
        for b in writes:
            if b.w is not None:
                deps.append(b.w)
            deps.extend(b.r.values())
        return deps

    def _emit(self, eng, fn, waits):
        for (sem, val) in waits[1:]:
            eng.h.wait_ge(sem, val)
            self.n_ins += 1
        ins = fn()
        self.n_ins += 1
        if waits:
            ins._wait_ge(waits[0][0], waits[0][1])
        return ins

    def op(self, eng, fn, reads=(), writes=(), inc=True):
        waits = self._need(eng, self._deps(reads, writes))
        ins = self._emit(eng, fn, waits)
        if inc:
            eng.cnt += 1
            ins.then_inc(eng.sem, 1)
            ev = (eng.sem, eng.cnt, eng)
        else:
            ev = (eng.sem, eng.cnt + 1, eng)
        for b in reads:
            b.r[id(eng.sem)] = ev
        for b in writes:
            b.w = ev
            b.r = {}
        return ins

    def dma(self, q, out, in_, reads=(), writes=()):
        ring = self.rings[q.name]
        i = ring["i"]
        ring["i"] = (i + 1) % len(ring["sems"])
        sem = ring["sems"][i]
        deps = self._deps(reads, writes)
        if ring["vals"][i] > 0:
            deps.append((sem, ring["vals"][i], None))
        waits = self._need(q, deps)
        ins = self._emit(q, lambda: q.h.dma_start(out=out, in_=in_), waits)
        ring["vals"][i] += 16
        ins.then_inc(sem, 16)
        ev = (sem, ring["vals"][i], None)
        for b in reads:
            b.r[id(sem)] = ev
        for b in writes:
            b.w = ev
            b.r = {}
        return ins

    def barrier(self):
        evs = []
        for e in self.engs:
            if e.cnt > 0:
                evs.append((e.sem, e.cnt, None))
        for r in self.rings.values():
            for s, v in zip(r["sems"], r["vals"]):
                if v > 0:
                    evs.append((s, v, None))
        for e in self.engs:
            for (sem, val) in self._need(e, evs):
                e.h.wait_ge(sem, val)
                self.n_ins += 1


def build(stage=99, dbg=False):
    nc = bass.Bass("TRN2", target_bir_lowering=False)
    from contextlib import ExitStack
    es = ExitStack()

    def din(name, shape, dt=F32):
        return nc.dram_tensor(name, list(shape), dt, kind="ExternalInput").ap()

    def dscr(name, shape, dt):
        return nc.dram_tensor(name, list(shape), dt, kind="Internal").ap()

    x_own = din("x_own", [T, D])
    x_oth = din("x_oth", [T, D])
    x_ctx = din("x_ctx", [256, D])
    c2_d = din("c2", [128, KT, 2])
    mod_w = din("mod_w", [2, 48, 128, KT, 128])
    modb_col_d = din("modb_col", [128, 2, 32])
    modb_gate_d = din("modb_gate", [1, 2, D])
    w_in0 = din("w_in0", [81, 128, KT, 128])
    w_dt = din("w_dt", [128, KT, 128])
    dtp_d = din("dtp", [128, 2])
    convp_d = din("convp", [128, 48, 8])
    drep_d = din("drep", [128, DI])
    ng_d = din("ng", [128, 32])
    w_out0 = din("w_out0", [4, 128, 32, 512])
    w_in1 = din("w_in1", [96, 128, KT, 128])
    lng_d = din("lng", [128, 32])
    lnb_d = din("lnb", [128, 32])
    wsT_d = din("wsT", [128, 16, 128])
    bs_d = din("bs", [1, 16 * 128])
    w_out1 = din("w_out1", [4, 128, 32, 512])
    fng_d = din("fng", [1, D])
    consts_d = din("consts", [128, 640])
    out_d = nc.dram_tensor("out", [T, D], F32, kind="ExternalOutput").ap()
    dbg_d = nc.dram_tensor("dbg", [128, 4096], F32, kind="ExternalOutput").ap() if dbg else None

    tok_own = dscr("tok_own", [T, 8, 640], BF16)
    tok_oth = dscr("tok_oth", [OW, 8, 640], BF16)
    featBC = dscr("featBC", [8, 2, 128, T], BF16)
    zT_s = dscr("zT_s", [DI, T], BF16)
    rscr = dscr("rscr", [NCH, 3, 128, 128], BF16)
    sbin = dscr("sbin", [NCH, 128, 8, 512], BF16)
    ygT_s = dscr("ygT_s", [DI, T], BF16)
    x1_s = dscr("x1_s", [T, D], F32)
    gv_s = dscr("gv_s", [T, DI], BF16)
    sT_s = dscr("sT_s", [DI, T], BF16)

    sems = [es.enter_context(nc.semaphore(f"s{i}")) for i in range(5 + 24)]
    S = Sched(nc, sems)
    PE, ACT, DVE, POOL, SP = S.pe, S.act, S.dve, S.pool, S.sp

    uniq = {"n": 0}

    def sb(name, shape, dt, stack=None):
        uniq["n"] += 1
        t = (stack or es).enter_context(nc.sbuf_tensor(f"sb{uniq['n']}_{name}", list(shape), dt))
        return t

    pf = [es.enter_context(nc.psum_tensor(f"pf{i}", [128, 512], F32)) for i in range(6)]
    pb = [es.enter_context(nc.psum_tensor(f"pb{i}", [128, 1024], BF16)) for i in range(2)]
    pfB = [Buf(f"pf{i}") for i in range(6)]
    pbB = [Buf(f"pb{i}") for i in range(2)]

    consts = sb("consts", [128, 640], F32)
    cB = Buf("consts")
    ident_f = consts[:, 0:128]
    tri_f = consts[:, 128:256]
    tri_b = consts[:, 256:384]
    e_last = consts[:, 384:512]
    e_first = consts[:, 512:640]
    ident_b = sb("ident_b", [128, 128], BF16)
    ones_b = sb("ones_b", [128, 128], BF16)
    ones_f = sb("ones_f", [128, 128], F32)
    modcol = sb("modcol", [128, 2, 32, 2], F32)
    mcB = Buf("modcol")
    gate_rep = sb("gate_rep", [128, 2, D], F32)
    grB = Buf("gate_rep")
    fng_rep = sb("fng_rep", [128, D], F32)
    rstd_y = sb("rstd_y", [128, NCH], F32)
    ryB = Buf("rstd_y")
    small = sb("small", [128, 64], F32)
    smB = Buf("small")

    S.dma(SP, consts[:], consts_d[:, :], writes=[cB])
    S.op(DVE, lambda: nc.vector.tensor_copy(ident_b[:], ident_f), reads=[cB], writes=[cB])
    S.op(DVE, lambda: nc.vector.memset(ones_b[:], 1.0), writes=[cB])
    S.op(DVE, lambda: nc.vector.memset(ones_f[:], 1.0), writes=[cB])

    rr = {"ev": 0}

    def evac_copy(out, in_, reads, writes, eng=None):
        rr["ev"] += 1
        if eng == "act" or (eng is None and rr["ev"] % 2 == 0):
            S.op(ACT, lambda: nc.scalar.copy(out, in_), reads=reads, writes=writes)
        else:
            S.op(DVE, lambda: nc.vector.tensor_copy(out, in_), reads=reads, writes=writes)

    NW = 5
    wt = [sb(f"wt{i}", [128, KT, 128], BF16) for i in range(NW)]
    wtB = [Buf(f"wt{i}") for i in range(NW)]
    wstate = {"i": 0}

    def load_w(src_ap):
        i = wstate["i"]
        wstate["i"] = (i + 1) % NW
        S.dma(POOL, wt[i][:], src_ap, writes=[wtB[i]])
        return wt[i], wtB[i]

    with ExitStack() as ph:
        c2 = sb("c2", [128, KT, 2], F32, ph)
        cs = sb("cs", [128, KT, 2], BF16, ph)
        modb_col = sb("modb_col", [128, 2, 32], F32, ph)
        grow = sb("grow", [1, 2, D], F32, ph)
        mbg = sb("mbg", [1, 2, D], F32, ph)
        fng = sb("fng", [1, D], F32, ph)
        aB = Buf("phA")
        growB = Buf("grow")
        S.dma(SP, c2[:], c2_d[:, :, :], writes=[aB])
        S.dma(SP, modb_col[:], modb_col_d[:, :, :], writes=[aB])
        S.dma(SP, mbg[:], modb_gate_d[:, :, :], writes=[aB])
        S.dma(SP, fng[:], fng_d[:, :], writes=[aB])
        csB = Buf("cs")
        S.op(ACT, lambda: nc.scalar.activation(cs[:], c2[:], AF.Silu), reads=[aB], writes=[csB])
        pend = None
        for layer in range(2):
            tiles = list(range(48))
            if layer == 0:
                nxt = load_w(mod_w[layer, 0])
            for nt in tiles:
                wtile, wB = nxt
                if nt + 1 < 48:
                    nxt = load_w(mod_w[layer, nt + 1])
                elif layer == 0:
                    nxt = load_w(mod_w[1, 0])
                bank = nt % 2
                if nt < 32:
                    for kt in range(KT):
                        S.op(PE, lambda kt=kt: nc.tensor.matmul(pf[bank][:, 0:2], wtile[:, kt, :], cs[:, kt, :],
                                                                  start=(kt == 0), stop=(kt == KT - 1)),
                             reads=[wB, csB], writes=[pfB[bank]], inc=(kt == KT - 1))
                    S.op(DVE, lambda: nc.vector.tensor_scalar(modcol[:, layer, nt, :], pf[bank][:, 0:2],
                                                              modb_col[:, layer, nt:nt + 1],
                                                              1.0 if nt >= 16 else 0.0, ALU.add, ALU.add),
                         reads=[pfB[bank], aB], writes=[mcB])
                else:
                    for kt in range(KT):
                        S.op(PE, lambda kt=kt: nc.tensor.matmul(pf[bank][0:2, 0:128], cs[:, kt, :], wtile[:, kt, :],
                                                                  start=(kt == 0), stop=(kt == KT - 1)),
                             reads=[wB, csB], writes=[pfB[bank]], inc=(kt == KT - 1))
                    c0 = (nt - 32) * 128
                    S.op(DVE, lambda: nc.vector.tensor_tensor(grow[0:1, layer, c0:c0 + 128], pf[bank][0:1, 0:128],
                                                              mbg[0:1, layer, c0:c0 + 128], ALU.add),
                         reads=[pfB[bank], aB], writes=[growB])
        for layer in range(2):
            for blk in range(4):
                bank = blk % 2
                S.op(PE, lambda: nc.tensor.matmul(pf[bank][:, :], ones_f[0:1, :], grow[0:1, layer, blk * 512:(blk + 1) * 512],
                                                  start=True, stop=True),
                     reads=[growB, cB], writes=[pfB[bank]])
                evac_copy(gate_rep[:, layer, blk * 512:(blk + 1) * 512], pf[bank][:, :], [pfB[bank]], [grB])
        for blk in range(4):
            bank = blk % 2
            S.op(PE, lambda: nc.tensor.matmul(pf[bank][:, :], ones_f[0:1, :], fng[0:1, blk * 512:(blk + 1) * 512],
                                              start=True, stop=True),
                 reads=[aB, cB], writes=[pfB[bank]])
            evac_copy(fng_rep[:, blk * 512:(blk + 1) * 512], pf[bank][:, :], [pfB[bank]], [grB])
        if dbg and stage == 1:
            S.barrier()
            S.dma(SP, dbg_d[:, 0:128], modcol[:].rearrange("p a b c -> p (a b c)"), reads=[mcB])
            S.dma(SP, dbg_d[:, 128:128 + 2048], gate_rep[:, 0, :], reads=[grB])
            S.dma(SP, dbg_d[:, 2176:2176 + 1920], gate_rep[:, 1, 0:1920], reads=[grB])
        S.barrier()
    if stage == 1:
        return finish(nc, S, es, out_d)

    def token_prep(ph, hlt, hltB, jobs, layer):
        xt = [sb(f"xt{i}", [128, D], F32, ph) for i in range(2)]
        xtB = [Buf(f"xt{i}") for i in range(2)]
        xn = [sb(f"xn{i}", [128, D], BF16, ph) for i in range(2)]
        xnB = [Buf(f"xn{i}") for i in range(2)]
        junk = sb("junk", [128, D], BF16, ph)
        jB = Buf("junk")
        st = sb("st", [128, 4, 2], F32, ph)
        stB = [Buf("st0"), Buf("st1")]
        if jobs:
            S.dma(SP, xt[0][:], jobs[0][0], writes=[xtB[0]])
        for n, (src, modj, dcol, t0, ntok) in enumerate(jobs):
            i = n % 2
            if n + 1 < len(jobs):
                S.dma(SP, xt[1 - i][:], jobs[n + 1][0], writes=[xtB[1 - i]])
            S.op(ACT, lambda: nc.scalar.activation(junk[:], xt[i][:], AF.Square, accum_out=st[:, i, 0:1]),
                 reads=[xtB[i]], writes=[jB, stB[i]])
            S.op(ACT, lambda: nc.scalar.activation(st[:, i, 1:2], st[:, i, 0:1], AF.Sqrt, bias=EPS, scale=1.0 / D),
                 reads=[stB[i]], writes=[stB[i]])
            S.op(DVE, lambda: nc.vector.reciprocal(st[:, i, 1:2], st[:, i, 1:2]),
                 reads=[stB[i]], writes=[stB[i]])
            S.op(ACT, lambda: nc.scalar.activation(xn[i][:], xt[i][:], AF.Copy, scale=st[:, i, 1:2]),
                 reads=[xtB[i], stB[i]], writes=[xnB[i]])
            for half in range(2):
                for k8 in range(8):
                    kt = half * 8 + k8
                    S.op(PE, lambda: nc.tensor.transpose(pb[half][:, k8 * 128:(k8 + 1) * 128],
                                                         xn[i][:, kt * 128:(kt + 1) * 128], ident_b[:]),
                         reads=[xnB[i], cB], writes=[pbB[half]], inc=(k8 == 7))
                for k8 in range(8):
                    kt = half * 8 + k8
                    src_ps = pb[half][:, k8 * 128 + t0:k8 * 128 + t0 + ntok]
                    dst = hlt[:, kt, dcol:dcol + ntok]
                    sc = modcol[:, layer, 16 + kt, modj:modj + 1]
                    sh = modcol[:, layer, kt, modj:modj + 1]
                    if half == 0:
                        S.op(DVE, lambda: nc.vector.tensor_scalar(dst, src_ps, sc, sh, ALU.mult, ALU.add),
                             reads=[pbB[half], mcB], writes=[hltB[kt]])
                    else:
                        S.op(ACT, lambda: nc.scalar.activation(dst, src_ps, AF.Identity, bias=sh, scale=sc),
                             reads=[pbB[half], mcB], writes=[hltB[kt]])

    def inproj(wsrc, hlt, hltB, blocks, evac, nxt_src=None, pre=None):
        wtile, wB = pre if pre is not None else load_w(wsrc)
        nxt = load_w(nxt_src) if nxt_src is not None else None
        for bi, (c0, n) in enumerate(blocks):
            bank = bi % 4
            for kt in range(KT):
                S.op(PE, lambda kt=kt: nc.tensor.matmul(pf[bank][:, 0:n], wtile[:, kt, :], hlt[:, kt, c0:c0 + n],
                                                          start=(kt == 0), stop=(kt == KT - 1)),
                     reads=[wB, hltB[kt]], writes=[pfB[bank]], inc=(kt == KT - 1))
            evac(pf[bank][:, 0:n], pfB[bank], c0, n)
        return nxt

    dtraw_own = sb("dtraw_own", [128, T], F32)
    dtraw_oth = sb("dtraw_oth", [128, OW], F32)
    dtB = Buf("dtraw")
    with ExitStack() as ph:
        hlt = sb("hlt", [128, KT, OW], BF16, ph)
        hltB = [Buf(f"hlt{k}") for k in range(KT)]
        convp = sb("convp", [128, 48, 8], F32, ph)
        cvB = Buf("convp")
        S.dma(SP, convp[:], convp_d[:, :, :], writes=[cvB])
        tcnt = {"n": 0}

        for pas in ("oth", "own"):
            with ExitStack() as ph2:
                if pas == "oth":
                    jobs = [(x_own[1920:2048, :], 0, 0, 125, 3)]
                    jobs += [(x_oth[i * 128:(i + 1) * 128, :], 0, 3 + i * 128, 0, 128) for i in range(16)]
                    jobs += [(x_ctx[i * 128:(i + 1) * 128, :], 1, CTX0 + i * 128, 0, 128) for i in range(2)]
                    blocks = [(0, 512), (512, 512), (1024, 512), (1536, 512), (2048, 3), (CTX0, 256)]
                    shift = 0
                    chunks = [("oth", c, 3 + c * 128) for c in range(16)] + [("ctx", c, CTX0 + c * 128) for c in range(2)]
                    conv_lo, conv_n = 3, 2312 - 3
                else:
                    jobs = [(x_own[i * 128:(i + 1) * 128, :], 0, i * 128, 0, 128) for i in range(16)]
                    jobs += [(x_oth[0:128, :], 0, 2048, 0, 3)]
                    blocks = [(0, 512), (512, 512), (1024, 512), (1536, 512), (2048, 3)]
                    shift = 3
                    chunks = [("own", c, c * 128) for c in range(16)]
                    conv_lo, conv_n = 3, 2048
                token_prep(ph2, hlt, hltB, jobs, 0)
                S.barrier()
            ph3 = ExitStack()
            pre_t = [sb(f"pre{i}", [128, OW], F32, ph3) for i in range(2)]
            preB = [Buf("pre0"), Buf("pre1")]
            acc = sb("acc", [128, OW], F32, ph3)
            accB = Buf("acc")
            acc2 = sb("acc2", [128, OW], F32, ph3)
            acc2B = Buf("acc2")
            ft = [sb(f"ft{i}", [128, OW], BF16, ph3) for i in range(2)]
            ftB = [Buf("ft0"), Buf("ft1")]
            tokg = sb("tokg", [128, 18, 640], BF16, ph3)
            tokgB = Buf("tokg")
            for i in range(2):
                S.op(DVE, lambda: nc.vector.memset(pre_t[i][:], 0.0), writes=[preB[i]])

            def cols_of(kind, g, j=0):
                if kind == "z":
                    return g * 512 + j * 128
                if kind == "xs":
                    return 4096 + g * 512 + j * 128
                if kind == "B":
                    return 8192 + g * 128
                if kind == "C":
                    return 9216 + g * 128
            tl = [("dt", 0, 0)]
            for g in range(8):
                tl.append(("B", g, 0))
                if pas == "own":
                    tl.append(("C", g, 0))
                for j in range(4):
                    tl.append(("xs", g, j))
                if pas == "own":
                    for j in range(4):
                        tl.append(("z", g, j))

            def src_of(t):
                kind, g, j = t
                if kind == "dt":
                    return w_dt[:, :, :]
                c0 = cols_of(kind, g, j)
                return w_in0[c0 // 128]

            nxt = load_w(src_of(tl[0]))
            pend = {"silu": None, "T": None}

            def flush():
                if pend["T"] is not None:
                    f_ = pend["T"]
                    pend["T"] = None
                    f_()

            def flush_silu():
                if pend["silu"] is not None:
                    f_, g_ = pend["silu"]
                    pend["silu"] = None
                    f_()
                    assert pend["T"] is None
                    pend["T"] = g_
            for ti, t in enumerate(tl):
                kind, g, j = t
                nsrc = src_of(tl[ti + 1]) if ti + 1 < len(tl) else None
                if kind == "dt":
                    dst = dtraw_oth if pas == "oth" else dtraw_own

                    def ev(ps, bB, c0, n, dst=dst):
                        if pas == "own" and c0 >= 2048:
                            return
                        evac_copy(dst[:, c0:c0 + n], ps, [bB], [dtB], eng="act")
                    nxt = inproj(None, hlt, hltB, blocks, ev, nsrc, pre=nxt)
                    flush()
                    flush_silu()
                    continue
                if kind == "z":
                    zi = tcnt["n"] % 2
                    tcnt["n"] += 1

                    def ev(ps, bB, c0, n, zi=zi):
                        if c0 >= 2048:
                            return
                        S.op(ACT, lambda: nc.scalar.activation(ft[zi][:, c0:c0 + n], ps, AF.Silu),
                             reads=[bB], writes=[ftB[zi]])
                    flush()
                    nxt = inproj(None, hlt, hltB, blocks[:4], ev, nsrc, pre=nxt)
                    flush_silu()
                    r0 = (g * 4 + j) * 128
                    S.dma(SP, zT_s[r0:r0 + 128, :], ft[zi][:, 0:T], reads=[ftB[zi]])
                    continue
                pi = tcnt["n"] % 2
                tcnt["n"] += 1
                cidx = {"xs": 0, "B": 32, "C": 40}[kind] + (g * 4 + j if kind == "xs" else g)

                def ev(ps, bB, c0, n, pi=pi):
                    evac_copy(pre_t[pi][:, c0 + shift:c0 + shift + n], ps, [bB], [preB[pi]], eng="act")
                nxt = inproj(None, hlt, hltB, blocks, ev, nsrc, pre=nxt)
                lo, n = conv_lo, conv_n
                def tap(k):
                    return pre_t[pi][:, lo - 3 + k:lo - 3 + k + n]
                accs = (acc, acc2)[pi]
                accsB = (accB, acc2B)[pi]
                flush()
                S.op(POOL, lambda: nc.gpsimd.tensor_tensor(accs[:, lo:lo + n], tap(0),
                                                           convp[:, cidx, 0:1].to_broadcast([128, n]), ALU.mult),
                     reads=[preB[pi], cvB], writes=[accsB])
                for k in range(1, 7):
                    S.op(DVE, lambda k=k: nc.vector.scalar_tensor_tensor(accs[:, lo:lo + n], tap(k), convp[:, cidx, k:k + 1],
                                                                         accs[:, lo:lo + n], ALU.mult, ALU.add),
                         reads=[preB[pi], cvB, accsB], writes=[accsB])
                fo = 0 if pas == "own" else lo

                def post_silu(pi=pi, accs=accs, accsB=accsB, cidx=cidx, fo=fo, lo=lo, n=n):
                    S.op(ACT, lambda: nc.scalar.activation(ft[pi][:, fo:fo + n], accs[:, lo:lo + n], AF.Silu,
                                                           bias=convp[:, cidx, 7:8]),
                         reads=[accsB, cvB], writes=[ftB[pi]])

                def post(kind=kind, g=g, j=j, pi=pi):
                    if kind in ("B", "C") and pas == "own":
                        S.dma(SP, featBC[g, 0 if kind == "B" else 1, :, :], ft[pi][:, 0:T], reads=[ftB[pi]])
                    if kind == "C":
                        return
                    dcol = 512 if kind == "B" else j * 128
                    for c8 in range(0, len(chunks), 8):
                        grp = chunks[c8:c8 + 8]
                        bank = (c8 // 8) % 2
                        for ci, (_, _, col0) in enumerate(grp):
                            S.op(PE, lambda: nc.tensor.transpose(pb[bank][:, ci * 128:(ci + 1) * 128],
                                                                 ft[pi][:, col0:col0 + 128], ident_b[:]),
                                 reads=[ftB[pi], cB], writes=[pbB[bank]], inc=(ci == len(grp) - 1))
                        ng = len(grp)
                        evac_copy(tokg[:, c8:c8 + ng, dcol:dcol + 128],
                                  pb[bank][:, 0:ng * 128].rearrange("p (c q) -> p c q", q=128),
                                  [pbB[bank]], [tokgB], eng="act")
                    if kind == "xs" and j == 3:
                        if pas == "own":
                            S.dma(SP, tok_own[:, g, :].rearrange("(c p) e -> p c e", p=128), tokg[:, 0:16, :], reads=[tokgB])
                        else:
                            S.dma(SP, tok_oth[3:3 + 2048, g, :].rearrange("(c p) e -> p c e", p=128), tokg[:, 0:16, :],
                                  reads=[tokgB])
                            S.dma(SP, tok_oth[CTX0:CTX0 + 256, g, :].rearrange("(c p) e -> p c e", p=128), tokg[:, 16:18, :],
                                  reads=[tokgB])
                flush_silu()
                pend["silu"] = (post_silu, post)
            flush()
            flush_silu()
            flush()
            S.barrier()
            ph3.close()
        if dbg and stage == 2:
            S.dma(POOL, dbg_d[:, 0:640], tok_own[0:128, 0, :])
            S.dma(POOL, dbg_d[:, 640:1280], tok_own[1920:2048, 7, :])
            S.dma(POOL, dbg_d[:, 1280:1920], tok_oth[3:131, 0, :])
            S.dma(POOL, dbg_d[:, 1920:2560], tok_oth[CTX0 + 128:CTX0 + 256, 3, :])
            S.dma(POOL, dbg_d[:, 2560:2688], featBC[2, 1, :, 0:128])
            S.dma(POOL, dbg_d[:, 2688:2816], zT_s[5 * 128:6 * 128, 128:256])
            S.dma(SP, dbg_d[:, 2816:3328], dtraw_own[:, 0:512], reads=[dtB])
            S.dma(SP, dbg_d[:, 3328:3840], dtraw_oth[:, 1808:2320], reads=[dtB])
            S.barrier()
    if stage == 2:
        return finish(nc, S, es, out_d)

    with ExitStack() as ph:
        dtp = sb("dtp", [128, 2], F32, ph)
        acol = sb("acol", [128, 1], F32, ph)
        drep = sb("drep", [128, DI], BF16, ph)
        ng = sb("ng", [128, 32], F32, ph)
        ones3 = sb("ones3", [3, 128], BF16, ph)
        pB = Buf("ssdparams")
        S.dma(SP, dtp[:], dtp_d[:, :], writes=[pB])
        S.dma(SP, ng[:], ng_d[:, :], writes=[pB])
        S.dma(POOL, drep[:], drep_d[:, :], writes=[pB])
        S.op(DVE, lambda: nc.vector.memset(ones3[:], 1.0), writes=[pB])
        S.op(ACT, lambda: nc.scalar.activation(acol[:], dtp[:, 1:2], AF.Exp), reads=[pB], writes=[pB])
        S.op(DVE, lambda: nc.vector.tensor_scalar(acol[:], acol[:], -1.0, None, ALU.mult), reads=[pB], writes=[pB])
        S_f = sb("S_f", [128, 8, 512], F32, ph)
        S_b = sb("S_b", [128, 8, 512], F32, ph)
        Sbf_f = sb("Sbf_f", [128, 8, 512], BF16, ph)
        SfB = [Buf(f"S_f{g}") for g in range(8)]
        SbB = [Buf(f"S_b{g}") for g in range(8)]
        SbfB = [Buf(f"Sbf_f{g}") for g in range(8)]
        S.op(DVE, lambda: nc.vector.memset(S_f[:], 0.0), writes=SfB)
        S.op(DVE, lambda: nc.vector.memset(S_b[:], 0.0), writes=SbB)
        S.op(DVE, lambda: nc.vector.memset(Sbf_f[:], 0.0), writes=SbfB)

        scs = []
        for i in range(2):
            scs.append(dict(
                at_lt=sb(f"at_lt{i}", [128, 256], F32, ph), ac=sb(f"ac{i}", [128, 128], F32, ph),
                acT=sb(f"acT{i}", [128, 128], F32, ph), cdb=sb(f"cdb{i}", [128, 128], F32, ph),
                wtk=sb(f"wtk{i}", [128, 128], F32, ph), biasL=sb(f"biasL{i}", [128, 128], F32, ph),
                dec=sb(f"dec{i}", [128, 128], F32, ph), r3=sb(f"r3{i}", [128, 3, 128], BF16, ph),
                tmpa=sb(f"tmpa{i}", [128, 128], F32, ph), B=Buf(f"sc{i}")))
        p0aB = Buf("pf0a")
        gramB = pbB[1]
        gram_ps = pb[1][:, 0:256].bitcast(F32)
        rscrB = [Buf(f"rscr{c}") for c in range(NCH)]

        def chunk_scalars(par, nm, col0, rchunk=None, full=True):
            sc = scs[par]
            B_ = sc["B"]
            a_src, l_src = aT[nm], ldT[nm]
            S.op(PE, lambda: nc.tensor.transpose(pf[0][:, 0:128], a_src[:, col0:col0 + 128], ident_f),
                 reads=[dt2B, cB], writes=[p0aB], inc=False)
            S.op(PE, lambda: nc.tensor.transpose(pf[0][:, 128:256], l_src[:, col0:col0 + 128], ident_f),
                 reads=[dt2B, cB], writes=[p0aB])
            S.op(DVE, lambda: nc.vector.tensor_copy(sc["at_lt"][:, :], pf[0][:, 0:256]), reads=[p0aB], writes=[B_])
            at = sc["at_lt"]
            S.op(PE, lambda: nc.tensor.matmul(pf[1][:, 0:64], tri_f, at[:, 0:64], start=True, stop=True),
                 reads=[B_, cB], writes=[pfB[1]], inc=False)
            S.op(PE, lambda: nc.tensor.matmul(pf[1][:, 64:128], tri_b, at[:, 64:128], start=True, stop=True),
                 reads=[B_, cB], writes=[pfB[1]], inc=False)
            S.op(PE, lambda: nc.tensor.matmul(pf[1][:, 128:256], at[:, 0:128], tri_f, start=True, stop=True),
                 reads=[B_, cB], writes=[pfB[1]], inc=False)
            S.op(PE, lambda: nc.tensor.matmul(pf[1][:, 256:384], at[:, 0:128], tri_b, start=True, stop=True),
                 reads=[B_, cB], writes=[pfB[1]])
            S.op(DVE, lambda: nc.vector.tensor_copy(sc["ac"][:, :], pf[1][:, 0:128]), reads=[pfB[1]], writes=[B_])
            if full:
                S.op(DVE, lambda: nc.vector.tensor_copy(sc["acT"][0:64, :], pf[1][0:64, 128:256]), reads=[pfB[1]], writes=[B_])
                S.op(DVE, lambda: nc.vector.tensor_copy(sc["acT"][64:128, :], pf[1][64:128, 256:384]), reads=[pfB[1]], writes=[B_])
            S.op(PE, lambda: nc.tensor.matmul(pf[1][:, 384:448], e_last, sc["ac"][:, 0:64], start=True, stop=True),
                 reads=[B_, cB], writes=[pfB[1]], inc=False)
            S.op(PE, lambda: nc.tensor.matmul(pf[1][:, 448:512], e_first, sc["ac"][:, 64:128], start=True, stop=True),
                 reads=[B_, cB], writes=[pfB[1]])
            S.op(ACT, lambda: nc.scalar.activation(sc["cdb"][:, :], pf[1][:, 384:512], AF.Exp), reads=[pfB[1]], writes=[B_])
            S.op(DVE, lambda: nc.vector.tensor_tensor(sc["tmpa"][:, :], pf[1][:, 384:512], sc["ac"][:, :], ALU.subtract),
                 reads=[pfB[1], B_], writes=[B_])
            S.op(DVE, lambda: nc.vector.tensor_tensor(sc["tmpa"][:, :], sc["tmpa"][:, :], at[:, 128:256], ALU.add),
                 reads=[B_], writes=[B_])
            S.op(ACT, lambda: nc.scalar.activation(sc["wtk"][:, :], sc["tmpa"][:, :], AF.Exp), reads=[B_], writes=[B_])
            if full:
                S.op(DVE, lambda: nc.vector.tensor_tensor(sc["biasL"][:, :], at[:, 128:256], sc["ac"][:, :], ALU.subtract),
                     reads=[B_], writes=[B_])
                S.op(ACT, lambda: nc.scalar.activation(sc["dec"][:, :], sc["ac"][:, :], AF.Exp), reads=[B_], writes=[B_])
            if rchunk is not None:
                r3 = sc["r3"]
                S.op(DVE, lambda: nc.vector.tensor_copy(r3[:, 0, :], sc["acT"][:, :]), reads=[B_], writes=[B_])
                S.op(DVE, lambda: nc.vector.tensor_tensor(sc["tmpa"][:, :], sc["acT"][:, :], r3[:, 0, :], ALU.subtract),
                     reads=[B_], writes=[B_])
                S.op(DVE, lambda: nc.vector.tensor_copy(r3[:, 1, :], sc["tmpa"][:, :]), reads=[B_], writes=[B_])
                S.op(DVE, lambda: nc.vector.tensor_tensor(sc["tmpa"][:, :], sc["tmpa"][:, :], r3[:, 1, :], ALU.subtract),
                     reads=[B_], writes=[B_])
                S.op(DVE, lambda: nc.vector.tensor_copy(r3[:, 2, :], sc["tmpa"][:, :]), reads=[B_], writes=[B_])
                S.dma(SP, rscr[rchunk].rearrange("j p q -> p j q"), r3[:, :, :], reads=[B_], writes=[rscrB[rchunk]])

        def bc8(tile_ap, c0):
            return tile_ap[:, c0:c0 + 8].unsqueeze(2).to_broadcast([128, 8, 64])

        def v3(ap2d):
            return ap2d.rearrange("p (a b) -> p a b", b=64)

        tk = [sb(f"tk{i}", [128, 640], BF16, ph) for i in range(2)]
        tkB = [Buf("tk0"), Buf("tk1")]
        xsw = [sb(f"xsw{i}", [128, 512], BF16, ph) for i in range(2)]
        xswB = [Buf("xsw0"), Buf("xsw1")]

        def state_prep(par, d, g, tkt, tkb, Sd, SdB, k, swap=False):
            sc = scs[par]
            S.op(POOL, lambda: nc.gpsimd.tensor_tensor(v3(xsw[k][:, :]), v3(tkt[:, 0:512]), bc8(sc["wtk"], d * 64 + 8 * g), ALU.mult),
                 reads=[tkb, sc["B"]], writes=[xswB[k]])
            if swap:
                S.op(DVE, lambda: nc.vector.tensor_tensor(v3(Sd[:, g, :]), v3(Sd[:, g, :]), bc8(sc["cdb"], d * 64 + 8 * g), ALU.mult),
                     reads=[sc["B"], SdB[g]], writes=[SdB[g]])
            else:
                S.op(POOL, lambda: nc.gpsimd.tensor_tensor(v3(Sd[:, g, :]), v3(Sd[:, g, :]), bc8(sc["cdb"], d * 64 + 8 * g), ALU.mult),
                     reads=[sc["B"], SdB[g]], writes=[SdB[g]])

        def state_fin(g, tkt, tkb, Sd, SdB, k, bank):
            S.op(PE, lambda: nc.tensor.matmul(pf[bank][:, :], tkt[:, 512:640], xsw[k][:, :], start=True, stop=True),
                 reads=[tkb, xswB[k]], writes=[pfB[bank]])
            S.op(DVE, lambda: nc.vector.tensor_tensor(Sd[:, g, :], Sd[:, g, :], pf[bank][:, :], ALU.add),
                 reads=[pfB[bank], SdB[g]], writes=[SdB[g]])

        def state_update(par, d, g, tkt, tkb, Sd, SdB, k):
            state_prep(par, d, g, tkt, tkb, Sd, SdB, k, swap=True)
            state_fin(g, tkt, tkb, Sd, SdB, k, 5 - k)

        aT = {"own": dtraw_own, "oth": dtraw_oth}
        ph_s1 = ExitStack()
        ldT = {"own": sb("ldT_own", [128, T], F32, ph), "oth": sb("ldT_oth", [128, OW], F32, ph_s1)}
        dt2B = Buf("dt2")
        for nm, raw, wdt in (("own", dtraw_own, T), ("oth", dtraw_oth, OW)):
            S.op(ACT, lambda: nc.scalar.activation(raw[:, 0:wdt], raw[:, 0:wdt], AF.Exp, bias=dtp[:, 0:1]),
                 reads=[dtB, pB], writes=[dtB])
            S.op(ACT, lambda: nc.scalar.activation(raw[:, 0:wdt], raw[:, 0:wdt], AF.Ln, bias=1.0),
                 reads=[dtB], writes=[dtB])
            S.op(ACT, lambda: nc.scalar.activation(ldT[nm][:, 0:wdt], raw[:, 0:wdt], AF.Ln),
                 reads=[dtB], writes=[dt2B])
            S.op(DVE, lambda: nc.vector.tensor_scalar(raw[:, 0:wdt], raw[:, 0:wdt], acol[:, 0:1], None, ALU.mult),
                 reads=[dtB, pB, dt2B], writes=[dt2B, dtB])

        sbsave = sb("sbsave", [128, 8, 512], BF16, ph_s1)
        sbsB = Buf("sbsave")
        visits = []
        visits += [("oth", CTX0 + c * 128, tok_oth, CTX0 + c * 128, 0, None) for c in (0, 1)]
        visits += [("oth", CTX0 + c * 128, tok_oth, CTX0 + c * 128, 1, None) for c in (1, 0)]
        visits += [("oth", 3 + c * 128, tok_oth, 3 + c * 128, 1, None) for c in range(15, -1, -1)]
        visits += [("own", c * 128, tok_own, c * 128, 1, c) for c in range(15, -1, -1)]
        items = [(vi, g) for vi in range(len(visits)) for g in range(8)]

        def s1_load(n):
            vi, g = items[n]
            nm, col0, tdr, row0, d, save = visits[vi]
            S.dma(SP, tk[n % 2][:, :], tdr[row0:row0 + 128, g, :], writes=[tkB[n % 2]])
        s1_load(0)
        for n, (vi, g) in enumerate(items):
            nm, col0, tdr, row0, d, save = visits[vi]
            par = vi % 2
            if n + 1 < len(items):
                s1_load(n + 1)
            if g == 0:
                if vi == 0:
                    chunk_scalars(par, nm, col0, full=False)
                if vi + 1 < len(visits):
                    chunk_scalars((vi + 1) % 2, visits[vi + 1][0], visits[vi + 1][1], full=False)
                if save is not None:
                    S.op(ACT, lambda: nc.scalar.copy(sbsave[:, :, :], S_b[:, :, :]), reads=SbB, writes=[sbsB])
                    S.dma(SP, sbin[save], sbsave[:, :, :], reads=[sbsB])
            if d == 0:
                state_update(par, 0, g, tk[n % 2], tkB[n % 2], S_f, SfB, n % 2)
            else:
                state_update(par, 1, g, tk[n % 2], tkB[n % 2], S_b, SbB, n % 2)
        S.op(ACT, lambda: nc.scalar.copy(Sbf_f[:, :, :], S_f[:, :, :]), reads=SfB, writes=SbfB)
        S.barrier()
        ph_s1.close()
        if dbg and stage == 3:
            S.dma(SP, dbg_d[:, 0:4096], S_f[:, :, :].rearrange("p g e -> p (g e)"), reads=SfB)
            S.barrier()
        if stage == 3:
            return finish(nc, S, es, out_d)

        NL = 3
        tk2 = [sb(f"tk2_{i}", [128, 640], BF16, ph) for i in range(NL)]
        tk2B = [Buf(f"tk2_{i}") for i in range(NL)]
        bct = [sb(f"bct{i}", [128, 2, 128], BF16, ph) for i in range(NL)]
        zt = [sb(f"zt{i}", [128, 4, 128], BF16, ph) for i in range(NL)]
        rg = [sb(f"rg{i}", [3, 2, 1024], BF16, ph) for i in range(NL)]
        sbl = [sb(f"sbl{i}", [128, 512], BF16, ph) for i in range(NL)]
        ldB = [Buf(f"ld{i}") for i in range(NL)]
        cbm = [sb(f"cbm{i}", [128, 2, 128], BF16, ph) for i in range(2)]
        cbmB = [Buf("cbm0"), Buf("cbm1")]
        Lt = [sb(f"Lt{i}", [128, 16, 128], BF16, ph) for i in range(2)]
        LtB = [[Buf(f"Lt{i}_{q}") for q in range(4)] for i in range(2)]
        Gt = [sb(f"Gt{i}", [128, 16, 128], BF16, ph) for i in range(2)]
        GtB = [[Buf(f"Gt{i}_{q}") for q in range(4)] for i in range(2)]
        xsD = [sb(f"xsD{i}", [128, 512], BF16, ph) for i in range(2)]
        xsDB = [Buf("xsD0"), Buf("xsD1")]
        yo = sb("yo", [128, 2, 512], BF16, ph)
        yoB = Buf("yo")
        ytot = sb("ytot", [128, 512], BF16, ph)
        ytB = Buf("ytot")
        ygp = sb("ygp", [128, 4, 128], BF16, ph)
        ygpB = Buf("ygp")
        ygs = sb("ygs", [128, 4, 128], BF16, ph)
        ygsB = Buf("ygs")
        gtmp = sb("gtmp", [128, 128], F32, ph)
        gtB = Buf("gtmp")
        items2 = [(c, g) for c in range(NCH) for g in range(8)]
        N2 = len(items2)

        def s2_load(n):
            c, g = items2[n]
            i = n % NL
            if g == 0 and c > 0:
                chunk_scalars(c % 2, "own", c * 128, rchunk=c)
            S.dma(SP, tk2[i][:, :], tok_own[c * 128:(c + 1) * 128, g, :], writes=[tk2B[i]])
            S.dma(SP, bct[i][:, :, :], featBC[g, :, :, c * 128:(c + 1) * 128].rearrange("w n t -> n w t"), writes=[ldB[i]])
            S.dma(SP, zt[i][:, :, :], zT_s[g * 512:(g + 1) * 512, c * 128:(c + 1) * 128].rearrange("(j p) t -> p j t", p=128),
                  writes=[ldB[i]])
            S.dma(SP, rg[i][:, :, :], rscr[c, :, :, :].rearrange("j (d h) q -> j d h q", d=2)[:, :, 8 * g:8 * g + 8, :]
                  .rearrange("j d h q -> j d (h q)"), reads=[rscrB[c]], writes=[ldB[i]])
            S.dma(SP, sbl[i][:, :], sbin[c, :, g, :], writes=[ldB[i]])

        bk = {"n": 0}

        def a_prep(n):
            c, g = items2[n]
            i = n % NL
            a = n % 2
            S.op(PE, lambda: nc.tensor.matmul(pf[0][:, 0:128], bct[i][:, 0, :], bct[i][:, 1, :], start=True, stop=True),
                 reads=[ldB[i]], writes=[p0aB])
            S.op(DVE, lambda: nc.vector.tensor_tensor(cbm[a][:, 0, :], pf[0][:, 0:128], tri_f, ALU.mult),
                 reads=[p0aB, cB], writes=[cbmB[a]])
            S.op(DVE, lambda: nc.vector.tensor_tensor(cbm[a][:, 1, :], pf[0][:, 0:128], tri_b, ALU.mult),
                 reads=[p0aB, cB], writes=[cbmB[a]])

        def a_pool(n):
            c, g = items2[n]
            i = n % NL
            a = n % 2
            S.op(POOL, lambda: nc.gpsimd.tensor_tensor(xsD[a][:, :], tk2[i][:, 0:512], drep[:, g * 512:(g + 1) * 512], ALU.mult),
                 reads=[tk2B[i], pB], writes=[xsDB[a]])
            state_prep(c % 2, 0, g, tk2[i], tk2B[i], S_f, SfB, a)

        def a_quarter(n, qd):
            c, g = items2[n]
            i = n % NL
            a = n % 2
            sc = scs[c % 2]
            d, half = qd // 2, qd % 2
            bank = 1 + (bk["n"] % 2)
            bk["n"] += 1
            S.op(PE, lambda: nc.tensor.matmul(pf[bank][:, :], ones3[:, :], rg[i][0:3, d, half * 512:(half + 1) * 512],
                                              start=True, stop=True),
                 reads=[ldB[i], pB], writes=[pfB[bank]])
            for hh in range(4):
                idx = d * 8 + half * 4 + hh
                h = 8 * g + half * 4 + hh
                S.op(ACT, lambda: nc.scalar.activation(Lt[a][:, idx, :], pf[bank][:, hh * 128:(hh + 1) * 128], AF.Exp,
                                                       bias=sc["biasL"][:, d * 64 + h:d * 64 + h + 1]),
                     reads=[pfB[bank], sc["B"]], writes=[LtB[a][qd]])
            i0 = d * 8 + half * 4
            S.op(DVE, lambda: nc.vector.scalar_tensor_tensor(Gt[a][:, i0:i0 + 4, :], Lt[a][:, i0:i0 + 4, :], 3.0e38,
                                                             cbm[a][:, d, :].unsqueeze(1).to_broadcast([128, 4, 128]),
                                                             ALU.min, ALU.mult),
                 reads=[LtB[a][qd], cbmB[a]], writes=[GtB[a][qd]])

        def b1(n):
            c, g = items2[n]
            i = n % NL
            a = n % 2
            sc = scs[c % 2]
            S.op(PE, lambda: nc.tensor.matmul(pf[3][:, :], ident_b[:, :], xsD[a][:, :], start=True, stop=False),
                 reads=[xsDB[a], cB], writes=[pfB[3]], inc=False)
            for hh8 in range(8):
                for d in range(2):
                    last = (hh8 == 7 and d == 1)
                    qd = d * 2 + hh8 // 4
                    S.op(PE, lambda: nc.tensor.matmul(pf[3][:, hh8 * 64:(hh8 + 1) * 64], Gt[a][:, d * 8 + hh8, :],
                                                      tk2[i][:, hh8 * 64:(hh8 + 1) * 64], start=False, stop=(d == 1),
                                                      skip_group_check=True),
                         reads=[GtB[a][qd], tk2B[i]], writes=[pfB[3]], inc=last)
            yoff(n, 0)

        def yoff(n, d):
            c, g = items2[n]
            i = n % NL
            sc = scs[c % 2]
            rhs = Sbf_f[:, g, :] if d == 0 else sbl[i][:, :]
            S.op(PE, lambda: nc.tensor.matmul(pf[4][:, :], bct[i][:, 1, :], rhs, start=True, stop=True),
                 reads=[ldB[i], SbfB[g]], writes=[pfB[4]])
            S.op(DVE, lambda: nc.vector.tensor_tensor(v3(yo[:, d, :]), v3(pf[4][:, :]), bc8(sc["dec"], d * 64 + 8 * g), ALU.mult),
                 reads=[pfB[4], sc["B"]], writes=[yoB])

        def b2(n):
            yoff(n, 1)
            S.op(DVE, lambda: nc.vector.tensor_tensor(yo[:, 0, :], yo[:, 0, :], yo[:, 1, :], ALU.add), reads=[yoB], writes=[yoB])
            S.op(DVE, lambda: nc.vector.tensor_tensor(ytot[:, :], pf[3][:, :], yo[:, 0, :], ALU.add),
                 reads=[pfB[3], yoB], writes=[ytB])

        def b3(n):
            c, g = items2[n]
            i = n % NL
            a = n % 2
            for j in range(4):
                S.op(PE, lambda: nc.tensor.transpose(pb[0][:, j * 128:(j + 1) * 128], ytot[:, j * 128:(j + 1) * 128], ident_b[:]),
                     reads=[ytB, cB], writes=[pbB[0]], inc=(j == 3))
            S.op(DVE, lambda: nc.vector.tensor_tensor(ygp[:, :, :].rearrange("p j q -> p (j q)"), pb[0][:, 0:512],
                                                      zt[i][:, :, :].rearrange("p j q -> p (j q)"), ALU.mult),
                 reads=[pbB[0], ldB[i]], writes=[ygpB])
            for j in range(4):
                S.op(PE, lambda: nc.tensor.matmul(gram_ps, ygp[:, j, :], ygp[:, j, :],
                                                  start=(g == 0 and j == 0), stop=(g == 7 and j == 3), skip_group_check=True),
                     reads=[ygpB], writes=[gramB], inc=(j == 3))
            S.op(POOL, lambda: nc.gpsimd.tensor_tensor(ygs[:, :, :], ygp[:, :, :],
                                                       ng[:, g * 4:g * 4 + 4].unsqueeze(2).to_broadcast([128, 4, 128]), ALU.mult),
                 reads=[ygpB, pB], writes=[ygsB])
            S.dma(SP, ygT_s[g * 512:(g + 1) * 512, c * 128:(c + 1) * 128].rearrange("(j p) q -> p j q", p=128), ygs[:, :, :],
                  reads=[ygsB])
            state_fin(g, tk2[i], tk2B[i], S_f, SfB, a, 5)
            S.op(ACT, lambda: nc.scalar.copy(Sbf_f[:, g, :], S_f[:, g, :]), reads=[SfB[g]], writes=[SbfB[g]])
            if g == 7:
                S.op(DVE, lambda: nc.vector.tensor_tensor(gtmp[:, :], gram_ps, ident_f, ALU.mult),
                     reads=[gramB, cB], writes=[gtB])
                S.op(DVE, lambda: nc.vector.reduce_sum(small[:, 0:1], gtmp[:, :], axis=AX.X), reads=[gtB], writes=[smB])
                S.op(ACT, lambda: nc.scalar.activation(small[:, 1:2], small[:, 0:1], AF.Sqrt, bias=EPS, scale=1.0 / DI),
                     reads=[smB], writes=[smB])
                S.op(DVE, lambda: nc.vector.reciprocal(rstd_y[:, c:c + 1], small[:, 1:2]), reads=[smB], writes=[ryB])

        chunk_scalars(0, "own", 0, rchunk=0)
        s2_load(0)
        s2_load(1)
        a_prep(0)
        for qd in range(4):
            a_quarter(0, qd)
        a_pool(0)
        for n in range(N2):
            nx = n + 1 < N2
            if nx:
                a_prep(n + 1)
                a_quarter(n + 1, 0)
            if n > 0:
                b3(n - 1)
            if n + 2 < N2:
                s2_load(n + 2)
            if nx:
                a_pool(n + 1)
                a_quarter(n + 1, 1)
            b1(n)
            if nx:
                a_quarter(n + 1, 2)
            b2(n)
            if nx:
                a_quarter(n + 1, 3)
        b3(N2 - 1)
        S.barrier()
        if dbg and stage == 4:
            S.dma(SP, dbg_d[:, 0:16], rstd_y[:, :], reads=[ryB])
            S.dma(POOL, dbg_d[:, 128:128 + 2048], ygT_s[0:128, :])
            S.dma(POOL, dbg_d[:, 2176:2176 + 1024], ygT_s[DI - 128:DI, 0:1024])
            S.barrier()
    if stage == 4:
        return finish(nc, S, es, out_d)

    def out_proj(srcT, w_dram, resid, layer, use_rstd, final):
        with ExitStack() as ph:
            nyb = 1 if final else 2
            yblk = [sb(f"yblk{i}", [128, 32, 512], BF16, ph) for i in range(nyb)]
            yblkB = [Buf(f"yblk{i}") for i in range(nyb)]
            wb = [sb(f"wb{i}", [128, 32, 512], BF16, ph) for i in range(2)]
            wbB = [Buf("wb0"), Buf("wb1")]
            xr = [sb(f"xr{i}", [128, 512], F32, ph) for i in range(2)]
            xrB = [Buf("xr0"), Buf("xr1")]
            x2 = sb("x2", [128, 4, D], F32, ph) if final else None
            x2B = [Buf(f"x2_{i}") for i in range(4)]
            ot = [sb(f"ot{i}", [128, 512], F32, ph) for i in range(2)]
            otB = [Buf("ot0"), Buf("ot1")]
            jk = sb("jk", [128, D], BF16, ph) if final else None
            jkB = Buf("jk")
            seq = [(tb, dblk) for tb in range(4) for dblk in range(4)] if final else \
                  [(tb, dblk) for dblk in range(4) for tb in range(4)]
            wi = {"n": 0, "cur": None, "slot": None}
            yi = {"n": 0, "cur": None, "slot": None}

            def get_w(dblk):
                if wi["cur"] == dblk:
                    return wi["slot"]
                i = wi["n"] % 2
                wi["n"] += 1
                for hf in range(2):
                    S.dma(POOL, wb[i][:, hf * 16:(hf + 1) * 16, :], w_dram[dblk, :, hf * 16:(hf + 1) * 16, :], writes=[wbB[i]])
                wi["cur"], wi["slot"] = dblk, i
                return i

            def get_y(tb):
                if yi["cur"] == tb:
                    return yi["slot"]
                i = yi["n"] % nyb
                yi["n"] += 1
                S.dma(SP, yblk[i][:, :, :], srcT[:, tb * 512:(tb + 1) * 512].rearrange("(j p) t -> p j t", p=128), writes=[yblkB[i]])
                yi["cur"], yi["slot"] = tb, i
                return i
            wnext = get_w(seq[0][1])
            ynext = get_y(seq[0][0])
            cnt = 0
            for si, (tb, dblk) in enumerate(seq):
                wcur, ycur = wnext, (ynext if nyb == 2 else get_y(tb))
                if si + 1 < len(seq):
                    wnext = get_w(seq[si + 1][1])
                    if nyb == 2:
                        ynext = get_y(seq[si + 1][0])
                for tt in range(4):
                    tok0 = tb * 512 + tt * 128
                    ch = tok0 // 128
                    k = cnt % 2
                    cnt += 1
                    S.dma(SP, xr[k][:, :], resid[tok0:tok0 + 128, dblk * 512:(dblk + 1) * 512], writes=[xrB[k]])
                    for j in range(32):
                        S.op(PE, lambda: nc.tensor.matmul(pf[k][:, :], yblk[ycur][:, j, tt * 128:(tt + 1) * 128], wb[wcur][:, j, :],
                                                          start=(j == 0), stop=(j == 31)),
                             reads=[yblkB[ycur], wbB[wcur]], writes=[pfB[k]], inc=(j == 31))
                    dst = x2[:, tt, dblk * 512:(dblk + 1) * 512] if final else ot[k][:, :]
                    dB = x2B[tt] if final else otB[k]
                    if use_rstd:
                        S.op(DVE, lambda: nc.vector.scalar_tensor_tensor(ot[k][:, :], pf[k][:, :], rstd_y[:, ch:ch + 1],
                                                                         gate_rep[:, layer, dblk * 512:(dblk + 1) * 512],
                                                                         ALU.mult, ALU.mult),
                             reads=[pfB[k], ryB, grB], writes=[otB[k]])
                    else:
                        S.op(DVE, lambda: nc.vector.tensor_tensor(ot[k][:, :], pf[k][:, :],
                                                                  gate_rep[:, layer, dblk * 512:(dblk + 1) * 512], ALU.mult),
                             reads=[pfB[k], grB], writes=[otB[k]])
                    S.op(DVE, lambda: nc.vector.tensor_tensor(dst, ot[k][:, :], xr[k][:, :], ALU.add),
                         reads=[otB[k], xrB[k]], writes=[dB] if final else [otB[k]])
                    if not final:
                        S.dma(SP, x1_s[tok0:tok0 + 128, dblk * 512:(dblk + 1) * 512], ot[k][:, :], reads=[otB[k]])
                    elif dblk == 3:
                        S.op(ACT, lambda: nc.scalar.activation(jk[:, :], x2[:, tt, :], AF.Square, accum_out=small[:, 8 + tt:9 + tt]),
                             reads=[x2B[tt]], writes=[jkB, smB])
                        S.op(ACT, lambda: nc.scalar.activation(small[:, 16 + tt:17 + tt], small[:, 8 + tt:9 + tt], AF.Sqrt,
                                                               bias=EPS, scale=1.0 / D), reads=[smB], writes=[smB])
                        S.op(DVE, lambda: nc.vector.reciprocal(small[:, 16 + tt:17 + tt], small[:, 16 + tt:17 + tt]),
                             reads=[smB], writes=[smB])
                        S.op(DVE, lambda: nc.vector.scalar_tensor_tensor(x2[:, tt, :], x2[:, tt, :], small[:, 16 + tt:17 + tt],
                                                                         fng_rep[:, :], ALU.mult, ALU.mult),
                             reads=[x2B[tt], smB, grB], writes=[x2B[tt]])
                        S.dma(SP, out_d[tok0:tok0 + 128, :], x2[:, tt, :], reads=[x2B[tt]])
            S.barrier()

    out_proj(ygT_s, w_out0, x_own, 0, True, False)
    if dbg and stage >= 5:
        S.dma(SP, dbg_d[:, 0:2048], x1_s[0:128, :])
        S.dma(SP, dbg_d[:, 2048:4096], x1_s[T - 128:T, :])
        S.barrier()
    if stage == 5:
        return finish(nc, S, es, out_d)

    blocks4 = [(0, 512), (512, 512), (1024, 512), (1536, 512)]
    with ExitStack() as ph:
        hlt = sb("hlt1", [128, KT, T], BF16, ph)
        hltB = [Buf(f"hlt1_{k}") for k in range(KT)]
        with ExitStack() as ph2:
            jobs = [(x1_s[i * 128:(i + 1) * 128, :], 0, i * 128, 0, 128) for i in range(16)]
            token_prep(ph2, hlt, hltB, jobs, 1)
            S.barrier()
        ft = [sb(f"ft1_{i}", [128, T], BF16, ph) for i in range(2)]
        ftB = [Buf("ft1_0"), Buf("ft1_1")]
        with ExitStack() as ph2:
            vtok = sb("vtok", [128, 16, 512], BF16, ph2)
            vtokB = Buf("vtok")
            nxt = load_w(w_in1[32])
            for j in range(32):
                fi = j % 2
                nsrc = w_in1[32 + j + 1] if j + 1 < 32 else None

                def ev(ps, bB, c0, n, fi=fi):
                    S.op(ACT, lambda: nc.scalar.activation(ft[fi][:, c0:c0 + n], ps, AF.Gelu), reads=[bB], writes=[ftB[fi]])
                nxt = inproj(None, hlt, hltB, blocks4, ev, nsrc, pre=nxt)
                for c8 in range(0, 16, 8):
                    bank = (c8 // 8) % 2
                    for ci in range(8):
                        col0 = (c8 + ci) * 128
                        S.op(PE, lambda: nc.tensor.transpose(pb[bank][:, ci * 128:(ci + 1) * 128], ft[fi][:, col0:col0 + 128], ident_b[:]),
                             reads=[ftB[fi], cB], writes=[pbB[bank]], inc=(ci == 7))
                    evac_copy(vtok[:, c8:c8 + 8, (j % 4) * 128:(j % 4) * 128 + 128],
                              pb[bank][:, 0:1024].rearrange("p (c q) -> p c q", q=128), [pbB[bank]], [vtokB])
                if j % 4 == 3:
                    S.dma(SP, gv_s[:, (j // 4) * 512:(j // 4 + 1) * 512].rearrange("(c p) e -> p c e", p=128), vtok[:, :, :],
                          reads=[vtokB])
            S.barrier()
        with ExitStack() as ph2:
            gr = [sb(f"gr{i}", [128, DI], BF16, ph2) for i in range(2)]
            grB_ = [Buf("gr0"), Buf("gr1")]
            jk2 = sb("jk2", [128, DI], BF16, ph2)
            jk2B = Buf("jk2")
            lst = sb("lst", [128, 2, 8], F32, ph2)
            lstB = [Buf("lst0"), Buf("lst1")]
            S.dma(SP, gr[0][:, :], gv_s[0:128, :], writes=[grB_[0]])
            for c in range(16):
                i = c % 2
                if c + 1 < 16:
                    S.dma(SP, gr[1 - i][:, :], gv_s[(c + 1) * 128:(c + 2) * 128, :], writes=[grB_[1 - i]])
                l = lst[:, i, :]
                S.op(ACT, lambda: nc.scalar.activation(jk2[:, :], gr[i][:, :], AF.Square, accum_out=l[:, 0:1]),
                     reads=[grB_[i]], writes=[jk2B, lstB[i]])
                S.op(DVE, lambda: nc.vector.reduce_sum(l[:, 1:2], gr[i][:, :], axis=AX.X), reads=[grB_[i]], writes=[lstB[i]])
                S.op(DVE, lambda: nc.vector.tensor_scalar(l[:, 2:3], l[:, 1:2], 1.0 / DI, None, ALU.mult), reads=[lstB[i]], writes=[lstB[i]])
                S.op(DVE, lambda: nc.vector.tensor_tensor(l[:, 3:4], l[:, 2:3], l[:, 2:3], ALU.mult), reads=[lstB[i]], writes=[lstB[i]])
                S.op(DVE, lambda: nc.vector.scalar_tensor_tensor(l[:, 4:5], l[:, 0:1], 1.0 / DI, l[:, 3:4], ALU.mult, ALU.subtract),
                     reads=[lstB[i]], writes=[lstB[i]])
                S.op(ACT, lambda: nc.scalar.activation(l[:, 5:6], l[:, 4:5], AF.Sqrt, bias=EPS, scale=1.0), reads=[lstB[i]], writes=[lstB[i]])
                S.op(DVE, lambda: nc.vector.reciprocal(l[:, 5:6], l[:, 5:6]), reads=[lstB[i]], writes=[lstB[i]])
                S.op(DVE, lambda: nc.vector.tensor_scalar(gr[i][:, :], gr[i][:, :], l[:, 2:3], l[:, 5:6], ALU.subtract, ALU.mult),
                     reads=[lstB[i], grB_[i]], writes=[grB_[i]])
                S.dma(SP, gv_s[c * 128:(c + 1) * 128, :], gr[i][:, :], reads=[grB_[i]])
            S.barrier()
        with ExitStack() as ph2:
            wsf = sb("wsf", [128, 16, 128], F32, ph2)
            wsb = sb("wsb", [128, 16, 128], BF16, ph2)
            bsr = sb("bsr", [1, 16 * 128], F32, ph2)
            lng = sb("lng", [128, 32], F32, ph2)
            lnb = sb("lnb", [128, 32], F32, ph2)
            bbt = sb("bbt", [128, 32, 128], F32, ph2)
            bsrep = sb("bsrep", [128, 128], F32, ph2)
            p1B = Buf("l1params")
            bbB = Buf("bbt")
            bsrepB = Buf("bsrep")
            S.dma(SP, wsf[:, :, :], wsT_d[:, :, :], writes=[p1B])
            S.dma(SP, bsr[:, :], bs_d[:, :], writes=[p1B])
            S.dma(SP, lng[:, :], lng_d[:, :], writes=[p1B])
            S.dma(SP, lnb[:, :], lnb_d[:, :], writes=[p1B])
            S.op(DVE, lambda: nc.vector.tensor_copy(wsb[:, :, :], wsf[:, :, :]), reads=[p1B], writes=[p1B])
            for grp in range(16):
                S.op(PE, lambda: nc.tensor.matmul(pf[0][:, 0:128], ones_f[:, :], wsf[:, grp, :], start=True, stop=True),
                     reads=[p1B, cB], writes=[pfB[0]])
                S.op(PE, lambda: nc.tensor.matmul(pf[1][:, 0:128], ones_f[0:1, :], bsr[0:1, grp * 128:(grp + 1) * 128], start=True, stop=True),
                     reads=[p1B, cB], writes=[pfB[1]])
                S.op(ACT, lambda: nc.scalar.copy(bsrep[:, :], pf[1][:, 0:128]), reads=[pfB[1]], writes=[bsrepB])
                for jj in range(2):
                    j = grp * 2 + jj
                    S.op(DVE, lambda: nc.vector.scalar_tensor_tensor(bbt[:, j, :], pf[0][:, 0:128], lnb[:, j:j + 1], bsrep[:, :],
                                                                     ALU.mult, ALU.add),
                         reads=[pfB[0], p1B, bsrepB], writes=[bbB])
            ug = sb("ug", [128, T], BF16, ph2)
            ugB = Buf("ug")
            vnt = [sb(f"vnt{i}", [128, 16, 128], BF16, ph2) for i in range(2)]
            vntB = [Buf("vnt0"), Buf("vnt1")]
            sTt = [sb(f"sTt{i}", [128, T], BF16, ph2) for i in range(2)]
            sTtB = [Buf("sTt0"), Buf("sTt1")]
            tv = sb("tv", [128, 512], F32, ph2)
            tvB = Buf("tv")

            def usrc(j):
                return w_in1[j]

            def gsrc(j):
                return w_in1[64 + j]
            nxt = load_w(usrc(0))
            for j in range(32):
                i = j % 2
                grp = j // 2
                S.dma(SP, vnt[i][:, :, :], gv_s[:, j * 128:(j + 1) * 128].rearrange("(c p) e -> p c e", p=128), writes=[vntB[i]])

                def ev_u(ps, bB, c0, n):
                    S.op(ACT, lambda: nc.scalar.activation(ft[0][:, c0:c0 + n], ps, AF.Gelu), reads=[bB], writes=[ftB[0]])

                def ev_g(ps, bB, c0, n):
                    S.op(ACT, lambda: nc.scalar.activation(ft[1][:, c0:c0 + n], ps, AF.Silu), reads=[bB], writes=[ftB[1]])
                nxt = inproj(None, hlt, hltB, blocks4, ev_u, gsrc(j), pre=nxt)
                nxt = inproj(None, hlt, hltB, blocks4, ev_g, usrc(j + 1) if j + 1 < 32 else None, pre=nxt)
                S.op(DVE, lambda: nc.vector.tensor_tensor(ug[:, :], ft[0][:, :], ft[1][:, :], ALU.mult),
                     reads=[ftB[0], ftB[1]], writes=[ugB])
                for c4 in range(4):
                    bank = 4 + (c4 % 2)
                    for cc in range(4):
                        c = c4 * 4 + cc
                        S.op(PE, lambda: nc.tensor.matmul(pf[bank][:, cc * 128:(cc + 1) * 128], vnt[i][:, c, :], wsb[:, grp, :],
                                                          start=True, stop=True),
                             reads=[vntB[i], p1B], writes=[pfB[bank]], inc=(cc == 3))
                    S.op(DVE, lambda: nc.vector.scalar_tensor_tensor(tv[:, :].rearrange("p (a q) -> p a q", q=128),
                                                                     pf[bank][:, :].rearrange("p (a q) -> p a q", q=128),
                                                                     lng[:, j:j + 1],
                                                                     bbt[:, j, :].unsqueeze(1).to_broadcast([128, 4, 128]),
                                                                     ALU.mult, ALU.add),
                         reads=[pfB[bank], p1B, bbB], writes=[tvB])
                    S.op(DVE, lambda: nc.vector.tensor_tensor(sTt[i][:, c4 * 512:(c4 + 1) * 512], tv[:, :], ug[:, c4 * 512:(c4 + 1) * 512],
                                                              ALU.mult),
                         reads=[tvB, ugB], writes=[sTtB[i]])
                S.dma(SP, sT_s[j * 128:(j + 1) * 128, :], sTt[i][:, :], reads=[sTtB[i]])
            S.barrier()
    out_proj(sT_s, w_out1, x1_s, 1, False, True)
    return finish(nc, S, es, out_d)


def finish(nc, S, es, out_d):
    S.barrier()
    es.close()
    return nc


def make_consts():
    k = np.arange(128)
    ident = np.eye(128, dtype=np.float32)
    tri_f = (k[:, None] <= k[None, :]).astype(np.float32)
    tri_b = (k[:, None] >= k[None, :]).astype(np.float32)
    e_last = np.zeros((128, 128), np.float32)
    e_last[127, :] = 1.0
    e_first = np.zeros((128, 128), np.float32)
    e_first[0, :] = 1.0
    return np.ascontiguousarray(np.concatenate([ident, tri_f, tri_b, e_last, e_first], axis=1))


def pack_inputs(r, x, c, ctx, c_ctx, mod_w, mod_b, ssd_w_in, ssd_conv_w, ssd_conv_b, ssd_dt_bias,
                ssd_a_log, ssd_d, ssd_norm_g, ssd_w_out, smlp_w_in, smlp_ln_g, smlp_ln_b,
                smlp_w_s, smlp_b_s, smlp_w_out, final_norm_g, shared):
    b, half = r // 2, r % 2
    flip = half == 1
    f32 = np.float32
    xs_ = x[b][::-1] if flip else x[b]
    cx_ = ctx[b][::-1] if flip else ctx[b]
    m = {}
    m["x_own"] = np.ascontiguousarray(xs_[:T], dtype=f32)
    m["x_oth"] = np.ascontiguousarray(xs_[T:], dtype=f32)
    m["x_ctx"] = np.ascontiguousarray(cx_, dtype=f32)
    cv = np.stack([c[b], c_ctx], axis=0)
    m["c2"] = np.ascontiguousarray(cv.reshape(2, KT, 128).transpose(2, 1, 0), dtype=f32)
    key = "flip" if flip else "noflip"
    if key not in shared:
        s = {}
        d_order = [1, 0] if flip else [0, 1]
        wdt = ssd_w_in[0][:, 10240:10368].reshape(D, 2, H)[:, d_order, :].reshape(D, 128)
        s["w_dt"] = np.ascontiguousarray(wdt.reshape(KT, 128, 128).transpose(1, 0, 2), dtype=f32)
        dtb = ssd_dt_bias[0][d_order].reshape(128)
        alg = ssd_a_log[0][d_order].reshape(128)
        s["dtp"] = np.ascontiguousarray(np.stack([dtb, alg], axis=1), dtype=f32)
        cw = ssd_conv_w[0][::-1] if flip else ssd_conv_w[0]
        cp = np.concatenate([cw.T, ssd_conv_b[0][:, None]], axis=1)
        s["convp"] = np.ascontiguousarray(cp.reshape(48, 128, 8).transpose(1, 0, 2), dtype=f32)
        ws = smlp_w_s[0]
        bs = smlp_b_s[0]
        if flip:
            ws = ws[:, ::-1, ::-1]
            bs = bs[:, ::-1]
        s["wsT"] = np.ascontiguousarray(ws.transpose(2, 0, 1), dtype=f32)
        s["bs"] = np.ascontiguousarray(bs.reshape(1, 16 * 128), dtype=f32)
        shared[key] = s
    m.update(shared[key])
    if "common" not in shared:
        s = {}
        s["mod_w"] = np.ascontiguousarray(mod_w.reshape(2, KT, 128, 48, 128).transpose(0, 3, 2, 1, 4), dtype=f32)
        mb = mod_b[:, :4096].reshape(2, 32, 128).transpose(2, 0, 1)
        s["modb_col"] = np.ascontiguousarray(mb, dtype=f32)
        s["modb_gate"] = np.ascontiguousarray(mod_b[:, 4096:].reshape(1, 2, D), dtype=f32)
        s["w_in0"] = np.ascontiguousarray(ssd_w_in[0].reshape(KT, 128, 81, 128).transpose(2, 1, 0, 3), dtype=f32)
        s["drep"] = np.ascontiguousarray(np.broadcast_to(np.repeat(ssd_d[0], 64)[None, :], (128, DI)), dtype=f32)
        s["ng"] = np.ascontiguousarray(ssd_norm_g[0].reshape(32, 128).T, dtype=f32)
        s["w_out0"] = np.ascontiguousarray(ssd_w_out[0].reshape(32, 128, 4, 512).transpose(2, 1, 0, 3), dtype=f32)
        s["w_in1"] = np.ascontiguousarray(smlp_w_in[0].reshape(KT, 128, 96, 128).transpose(2, 1, 0, 3), dtype=f32)
        s["lng"] = np.ascontiguousarray(smlp_ln_g[0].reshape(32, 128).T, dtype=f32)
        s["lnb"] = np.ascontiguousarray(smlp_ln_b[0].reshape(32, 128).T, dtype=f32)
        s["w_out1"] = np.ascontiguousarray(smlp_w_out[0].reshape(32, 128, 4, 512).transpose(2, 1, 0, 3), dtype=f32)
        s["fng"] = np.ascontiguousarray(final_norm_g.reshape(1, D), dtype=f32)
        s["consts"] = make_consts()
        shared["common"] = s
    m.update(shared["common"])
    return m


def kernel(**inputs):
    inputs = {k: np.asarray(v) for k, v in inputs.items()}
    shared = {}
    in_maps = [pack_inputs(r, shared=shared, **inputs) for r in range(8)]
    nc = build()
    res = run_bass_kernel_spmd(nc, in_maps, core_ids=list(range(8)))
    out = np.zeros((4, 2 * T, D), np.float32)
    for r in range(8):
        b, half = r // 2, r % 2
        o = np.asarray(res.results[r]["out"], dtype=np.float32)
        if half == 0:
            out[b, :T] = o
        else:
            out[b, T:] = o[::-1]
    return out
```

```python
import numpy as np
import ml_dtypes
import concourse.bass as bass
import concourse.mybir as mybir
from concourse.bass_utils import run_bass_kernel_spmd

F32 = mybir.dt.float32
BF16 = mybir.dt.bfloat16
AF = mybir.ActivationFunctionType
ALU = mybir.AluOpType
AX = mybir.AxisListType

D = 2048
KT = 16
T = 2048
NCH = 16
DI = 4096
H = 64
EPS = 1e-6
W_IN0 = 10368
OW = 2320
CTX0 = 2056


class Buf:
    __slots__ = ("name", "w", "r")

    def __init__(self, name):
        self.name = name
        self.w = None
        self.r = {}


class Eng:
    def __init__(self, name, h, sem, same_engine_sync=True):
        self.name = name
        self.h = h
        self.sem = sem
        self.cnt = 0
        self.waited = {}
        self.same = same_engine_sync


class Sched:
    def __init__(self, nc, sems):
        self.nc = nc
        it = iter(sems)
        self.pe = Eng("pe", nc.tensor, next(it), same_engine_sync=False)
        self.act = Eng("act", nc.scalar, next(it))
        self.dve = Eng("dve", nc.vector, next(it))
        self.pool = Eng("pool", nc.gpsimd, next(it))
        self.sp = Eng("sp", nc.sync, next(it))
        self.engs = [self.pe, self.act, self.dve, self.pool, self.sp]
        self.rings = {}
        for q in (self.sp, self.pool):
            self.rings[q.name] = {"sems": [next(it) for _ in range(12)], "vals": [0] * 12, "i": 0}
        self.n_ins = 0

    def _need(self, eng, deps):
        out = []
        best = {}
        for (sem, val, src) in deps:
            if src is eng and not eng.same:
                continue
            k = id(sem)
            if eng.waited.get(k, 0) >= val:
                continue
            if k not in best or best[k][1] < val:
                best[k] = (sem, val)
        for k, (sem, val) in best.items():
            eng.waited[k] = val
            out.append((sem, val))
        return out

    def _deps(self, reads, writes):
        deps = []
        for b in reads:
            if b.w is not None:
                deps.append(b.w)
        for b in writes:
            if b.w is not None:
                deps.append(b.w)
            deps.extend(b.r.values())
        return deps

    def _emit(self, eng, fn, waits):
        for (sem, val) in waits[1:]:
            eng.h.wait_ge(sem, val)
            self.n_ins += 1
        ins = fn()
        self.n_ins += 1
        if waits:
            ins._wait_ge(waits[0][0], waits[0][1])
        return ins

    def op(self, eng, fn, reads=(), writes=(), inc=True):
        waits = self._need(eng, self._deps(reads, writes))
        ins = self._emit(eng, fn, waits)
        if inc:
            eng.cnt += 1
            ins.then_inc(eng.sem, 1)
            ev = (eng.sem, eng.cnt, eng)
        else:
            ev = (eng.sem, eng.cnt + 1, eng)
        for b in reads:
            b.r[id(eng.sem)] = ev
        for b in writes:
            b.w = ev
            b.r = {}
        return ins

    def dma(self, q, out, in_, reads=(), writes=()):
        ring = self.rings[q.name]
        i = ring["i"]
        ring["i"] = (i + 1) % len(ring["sems"])
        sem = ring["sems"][i]
        deps = self._deps(reads, writes)
        if ring["vals"][i] > 0:
            deps.append((sem, ring["vals"][i], None))
        waits = self._need(q, deps)
        ins = self._emit(q, lambda: q.h.dma_start(out=out, in_=in_), waits)
        ring["vals"][i] += 16
        ins.then_inc(sem, 16)
        ev = (sem, ring["vals"][i], None)
        for b in reads:
            b.r[id(sem)] = ev
        for b in writes:
            b.w = ev
            b.r = {}
        return ins

    def barrier(self):
        evs = []
        for e in self.engs:
            if e.cnt > 0:
                evs.append((e.sem, e.cnt, None))
        for r in self.rings.values():
            for s, v in zip(r["sems"], r["vals"]):
                if v > 0:
                    evs.append((s, v, None))
        for e in self.engs:
            for (sem, val) in self._need(e, evs):
                e.h.wait_ge(sem, val)
                self.n_ins += 1


def build(stage=99, dbg=False):
    nc = bass.Bass("TRN2", target_bir_lowering=False)
    from contextlib import ExitStack
    es = ExitStack()

    def din(name, shape, dt=F32):
        return nc.dram_tensor(name, list(shape), dt, kind="ExternalInput").ap()

    def dscr(name, shape, dt):
        return nc.dram_tensor(name, list(shape), dt, kind="Internal").ap()

    x_own = din("x_own", [T, D])
    x_oth = din("x_oth", [T, D])
    x_ctx = din("x_ctx", [256, D])
    c2_d = din("c2", [128, KT, 2])
    mod_w = din("mod_w", [2, 48, 128, KT, 128])
    modb_col_d = din("modb_col", [128, 2, 32])
    modb_gate_d = din("modb_gate", [1, 2, D])
    w_in0 = din("w_in0", [81, 128, KT, 128])
    w_dt = din("w_dt", [128, KT, 128])
    dtp_d = din("dtp", [128, 2])
    convp_d = din("convp", [128, 48, 8])
    drep_d = din("drep", [128, DI])
    ng_d = din("ng", [128, 32])
    w_out0 = din("w_out0", [4, 128, 32, 512])
    w_in1 = din("w_in1", [96, 128, KT, 128])
    lng_d = din("lng", [128, 32])
    lnb_d = din("lnb", [128, 32])
    wsT_d = din("wsT", [128, 16, 128])
    bs_d = din("bs", [1, 16 * 128])
    w_out1 = din("w_out1", [4, 128, 32, 512])
    fng_d = din("fng", [1, D])
    consts_d = din("consts", [128, 640])
    out_d = nc.dram_tensor("out", [T, D], F32, kind="ExternalOutput").ap()
    dbg_d = nc.dram_tensor("dbg", [128, 4096], F32, kind="ExternalOutput").ap() if dbg else None

    tok_own = dscr("tok_own", [T, 8, 640], BF16)
    tok_oth = dscr("tok_oth", [OW, 8, 640], BF16)
    featBC = dscr("featBC", [8, 2, 128, T], BF16)
    zT_s = dscr("zT_s", [DI, T], BF16)
    rscr = dscr("rscr", [NCH, 3, 128, 128], BF16)
    sbin = dscr("sbin", [NCH, 128, 8, 512], BF16)
    ygT_s = dscr("ygT_s", [DI, T], BF16)
    x1_s = dscr("x1_s", [T, D], F32)
    gv_s = dscr("gv_s", [T, DI], BF16)
    sT_s = dscr("sT_s", [DI, T], BF16)

    sems = [es.enter_context(nc.semaphore(f"s{i}")) for i in range(5 + 24)]
    S = Sched(nc, sems)
    PE, ACT, DVE, POOL, SP = S.pe, S.act, S.dve, S.pool, S.sp

    uniq = {"n": 0}

    def sb(name, shape, dt, stack=None):
        uniq["n"] += 1
        t = (stack or es).enter_context(nc.sbuf_tensor(f"sb{uniq['n']}_{name}", list(shape), dt))
        return t

    pf = [es.enter_context(nc.psum_tensor(f"pf{i}", [128, 512], F32)) for i in range(6)]
    pb = [es.enter_context(nc.psum_tensor(f"pb{i}", [128, 1024], BF16)) for i in range(2)]
    pfB = [Buf(f"pf{i}") for i in range(6)]
    pbB = [Buf(f"pb{i}") for i in range(2)]

    consts = sb("consts", [128, 640], F32)
    cB = Buf("consts")
    ident_f = consts[:, 0:128]
    tri_f = consts[:, 128:256]
    tri_b = consts[:, 256:384]
    e_last = consts[:, 384:512]
    e_first = consts[:, 512:640]
    ident_b = sb("ident_b", [128, 128], BF16)
    ones_b = sb("ones_b", [128, 128], BF16)
    ones_f = sb("ones_f", [128, 128], F32)
    modcol = sb("modcol", [128, 2, 32, 2], F32)
    mcB = Buf("modcol")
    gate_rep = sb("gate_rep", [128, 2, D], F32)
    grB = Buf("gate_rep")
    fng_rep = sb("fng_rep", [128, D], F32)
    rstd_y = sb("rstd_y", [128, NCH], F32)
    ryB = Buf("rstd_y")
    small = sb("small", [128, 64], F32)
    smB = Buf("small")

    S.dma(SP, consts[:], consts_d[:, :], writes=[cB])
    S.op(DVE, lambda: nc.vector.tensor_copy(ident_b[:], ident_f), reads=[cB], writes=[cB])
    S.op(DVE, lambda: nc.vector.memset(ones_b[:], 1.0), writes=[cB])
    S.op(DVE, lambda: nc.vector.memset(ones_f[:], 1.0), writes=[cB])

    rr = {"ev": 0}

    def evac_copy(out, in_, reads, writes, eng=None):
        rr["ev"] += 1
        if eng == "act" or (eng is None and rr["ev"] % 2 == 0):
            S.op(ACT, lambda: nc.scalar.copy(out, in_), reads=reads, writes=writes)
        else:
            S.op(DVE, lambda: nc.vector.tensor_copy(out, in_), reads=reads, writes=writes)

    NW = 5
    wt = [sb(f"wt{i}", [128, KT, 128], BF16) for i in range(NW)]
    wtB = [Buf(f"wt{i}") for i in range(NW)]
    wstate = {"i": 0}

    def load_w(src_ap):
        i = wstate["i"]
        wstate["i"] = (i + 1) % NW
        S.dma(POOL, wt[i][:], src_ap, writes=[wtB[i]])
        return wt[i], wtB[i]

    with ExitStack() as ph:
        c2 = sb("c2", [128, KT, 2], F32, ph)
        cs = sb("cs", [128, KT, 2], BF16, ph)
        modb_col = sb("modb_col", [128, 2, 32], F32, ph)
        grow = sb("grow", [1, 2, D], F32, ph)
        mbg = sb("mbg", [1, 2, D], F32, ph)
        fng = sb("fng", [1, D], F32, ph)
        aB = Buf("phA")
        growB = Buf("grow")
        S.dma(SP, c2[:], c2_d[:, :, :], writes=[aB])
        S.dma(SP, modb_col[:], modb_col_d[:, :, :], writes=[aB])
        S.dma(SP, mbg[:], modb_gate_d[:, :, :], writes=[aB])
        S.dma(SP, fng[:], fng_d[:, :], writes=[aB])
        csB = Buf("cs")
        S.op(ACT, lambda: nc.scalar.activation(cs[:], c2[:], AF.Silu), reads=[aB], writes=[csB])
        pend = None
        for layer in range(2):
            tiles = list(range(48))
            if layer == 0:
                nxt = load_w(mod_w[layer, 0])
            for nt in tiles:
                wtile, wB = nxt
                if nt + 1 < 48:
                    nxt = load_w(mod_w[layer, nt + 1])
                elif layer == 0:
                    nxt = load_w(mod_w[1, 0])
                bank = nt % 2
                if nt < 32:
                    for kt in range(KT):
                        S.op(PE, lambda kt=kt: nc.tensor.matmul(pf[bank][:, 0:2], wtile[:, kt, :], cs[:, kt, :],
                                                                  start=(kt == 0), stop=(kt == KT - 1)),
                             reads=[wB, csB], writes=[pfB[bank]], inc=(kt == KT - 1))
                    S.op(DVE, lambda: nc.vector.tensor_scalar(modcol[:, layer, nt, :], pf[bank][:, 0:2],
                                                              modb_col[:, layer, nt:nt + 1],
                                                              1.0 if nt >= 16 else 0.0, ALU.add, ALU.add),
                         reads=[pfB[bank], aB], writes=[mcB])
                else:
                    for kt in range(KT):
                        S.op(PE, lambda kt=kt: nc.tensor.matmul(pf[bank][0:2, 0:128], cs[:, kt, :], wtile[:, kt, :],
                                                                  start=(kt == 0), stop=(kt == KT - 1)),
                             reads=[wB, csB], writes=[pfB[bank]], inc=(kt == KT - 1))
                    c0 = (nt - 32) * 128
                    S.op(DVE, lambda: nc.vector.tensor_tensor(grow[0:1, layer, c0:c0 + 128], pf[bank][0:1, 0:128],
                                                              mbg[0:1, layer, c0:c0 + 128], ALU.add),
                         reads=[pfB[bank], aB], writes=[growB])
        for layer in range(2):
            for blk in range(4):
                bank = blk % 2
                S.op(PE, lambda: nc.tensor.matmul(pf[bank][:, :], ones_f[0:1, :], grow[0:1, layer, blk * 512:(blk + 1) * 512],
                                                  start=True, stop=True),
                     reads=[growB, cB], writes=[pfB[bank]])
                evac_copy(gate_rep[:, layer, blk * 512:(blk + 1) * 512], pf[bank][:, :], [pfB[bank]], [grB])
        for blk in range(4):
            bank = blk % 2
            S.op(PE, lambda: nc.tensor.matmul(pf[bank][:, :], ones_f[0:1, :], fng[0:1, blk * 512:(blk + 1) * 512],
                                              start=True, stop=True),
                 reads=[aB, cB], writes=[pfB[bank]])
            evac_copy(fng_rep[:, blk * 512:(blk + 1) * 512], pf[bank][:, :], [pfB[bank]], [grB])
        if dbg and stage == 1:
            S.barrier()
            S.dma(SP, dbg_d[:, 0:128], modcol[:].rearrange("p a b c -> p (a b c)"), reads=[mcB])
            S.dma(SP, dbg_d[:, 128:128 + 2048], gate_rep[:, 0, :], reads=[grB])
            S.dma(SP, dbg_d[:, 2176:2176 + 1920], gate_rep[:, 1, 0:1920], reads=[grB])
        S.barrier()
    if stage == 1:
        return finish(nc, S, es, out_d)

    def token_prep(ph, hlt, hltB, jobs, layer):
        xt = [sb(f"xt{i}", [128, D], F32, ph) for i in range(2)]
        xtB = [Buf(f"xt{i}") for i in range(2)]
        xn = [sb(f"xn{i}", [128, D], BF16, ph) for i in range(2)]
        xnB = [Buf(f"xn{i}") for i in range(2)]
        junk = sb("junk", [128, D], BF16, ph)
        jB = Buf("junk")
        st = sb("st", [128, 4, 2], F32, ph)
        stB = [Buf("st0"), Buf("st1")]
        nj = len(jobs)

        def load(n):
            S.dma(SP, xt[n % 2][:], jobs[n][0], writes=[xtB[n % 2]])

        def stage1(n):
            i = n % 2
            S.op(ACT, lambda: nc.scalar.activation(junk[:], xt[i][:], AF.Square, accum_out=st[:, i, 0:1]),
                 reads=[xtB[i]], writes=[jB, stB[i]])
            S.op(ACT, lambda: nc.scalar.activation(st[:, i, 1:2], st[:, i, 0:1], AF.Sqrt, bias=EPS, scale=1.0 / D),
                 reads=[stB[i]], writes=[stB[i]])
            S.op(DVE, lambda: nc.vector.reciprocal(st[:, i, 1:2], st[:, i, 1:2]),
                 reads=[stB[i]], writes=[stB[i]])
            S.op(ACT, lambda: nc.scalar.activation(xn[i][:], xt[i][:], AF.Copy, scale=st[:, i, 1:2]),
                 reads=[xtB[i], stB[i]], writes=[xnB[i]])

        def stage2(n):
            src, modj, dcol, t0, ntok = jobs[n]
            i = n % 2
            for half in range(2):
                for k8 in range(8):
                    kt = half * 8 + k8
                    S.op(PE, lambda: nc.tensor.transpose(pb[half][:, k8 * 128:(k8 + 1) * 128],
                                                         xn[i][:, kt * 128:(kt + 1) * 128], ident_b[:]),
                         reads=[xnB[i], cB], writes=[pbB[half]], inc=(k8 == 7))
                for k8 in range(8):
                    kt = half * 8 + k8
                    src_ps = pb[half][:, k8 * 128 + t0:k8 * 128 + t0 + ntok]
                    dst = hlt[:, kt, dcol:dcol + ntok]
                    sc = modcol[:, layer, 16 + kt, modj:modj + 1]
                    sh = modcol[:, layer, kt, modj:modj + 1]
                    S.op(DVE, lambda: nc.vector.tensor_scalar(dst, src_ps, sc, sh, ALU.mult, ALU.add),
                         reads=[pbB[half], mcB], writes=[hltB[kt]])

        if nj:
            load(0)
            if nj > 1:
                load(1)
            stage1(0)
        for n in range(nj):
            if n + 1 < nj:
                stage1(n + 1)
            if n + 2 < nj:
                load(n + 2)
            stage2(n)

    def inproj(wsrc, hlt, hltB, blocks, evac, nxt_src=None, pre=None):
        wtile, wB = pre if pre is not None else load_w(wsrc)
        nxt = load_w(nxt_src) if nxt_src is not None else None
        for bi, (c0, n) in enumerate(blocks):
            bank = bi % 4
            for kt in range(KT):
                S.op(PE, lambda kt=kt: nc.tensor.matmul(pf[bank][:, 0:n], wtile[:, kt, :], hlt[:, kt, c0:c0 + n],
                                                          start=(kt == 0), stop=(kt == KT - 1)),
                     reads=[wB, hltB[kt]], writes=[pfB[bank]], inc=(kt == KT - 1))
            evac(pf[bank][:, 0:n], pfB[bank], c0, n)
        return nxt

    dtraw_own = sb("dtraw_own", [128, T], F32)
    dtraw_oth = sb("dtraw_oth", [128, OW], F32)
    dtB = Buf("dtraw")
    with ExitStack() as ph:
        hlt = sb("hlt", [128, KT, OW], BF16, ph)
        hltB = [Buf(f"hlt{k}") for k in range(KT)]
        convp = sb("convp", [128, 48, 8], F32, ph)
        cvB = Buf("convp")
        S.dma(SP, convp[:], convp_d[:, :, :], writes=[cvB])
        tcnt = {"n": 0}

        for pas in ("oth", "own"):
            with ExitStack() as ph2:
                if pas == "oth":
                    jobs = [(x_own[1920:2048, :], 0, 0, 125, 3)]
                    jobs += [(x_oth[i * 128:(i + 1) * 128, :], 0, 3 + i * 128, 0, 128) for i in range(16)]
                    jobs += [(x_ctx[i * 128:(i + 1) * 128, :], 1, CTX0 + i * 128, 0, 128) for i in range(2)]
                    blocks = [(0, 512), (512, 512), (1024, 512), (1536, 512), (2048, 3), (CTX0, 256)]
                    shift = 0
                    chunks = [("oth", c, 3 + c * 128) for c in range(16)] + [("ctx", c, CTX0 + c * 128) for c in range(2)]
                    conv_lo, conv_n = 3, 2312 - 3
                else:
                    jobs = [(x_own[i * 128:(i + 1) * 128, :], 0, i * 128, 0, 128) for i in range(16)]
                    jobs += [(x_oth[0:128, :], 0, 2048, 0, 3)]
                    blocks = [(0, 512), (512, 512), (1024, 512), (1536, 512), (2048, 3)]
                    shift = 3
                    chunks = [("own", c, c * 128) for c in range(16)]
                    conv_lo, conv_n = 3, 2048
                token_prep(ph2, hlt, hltB, jobs, 0)
                S.barrier()
            ph3 = ExitStack()
            pre_t = [sb(f"pre{i}", [128, OW], F32, ph3) for i in range(2)]
            preB = [Buf("pre0"), Buf("pre1")]
            acc = sb("acc", [128, OW], F32, ph3)
            accB = Buf("acc")
            acc2 = sb("acc2", [128, OW], F32, ph3)
            acc2B = Buf("acc2")
            ft = [sb(f"ft{i}", [128, OW], BF16, ph3) for i in range(2)]
            ftB = [Buf("ft0"), Buf("ft1")]
            tokg = sb("tokg", [128, 18, 640], BF16, ph3)
            tokgB = Buf("tokg")
            for i in range(2):
                S.op(DVE, lambda: nc.vector.memset(pre_t[i][:], 0.0), writes=[preB[i]])

            def cols_of(kind, g, j=0):
                if kind == "z":
                    return g * 512 + j * 128
                if kind == "xs":
                    return 4096 + g * 512 + j * 128
                if kind == "B":
                    return 8192 + g * 128
                if kind == "C":
                    return 9216 + g * 128
            tl = [("dt", 0, 0)]
            for g in range(8):
                tl.append(("B", g, 0))
                if pas == "own":
                    tl.append(("C", g, 0))
                for j in range(4):
                    tl.append(("xs", g, j))
                if pas == "own":
                    for j in range(4):
                        tl.append(("z", g, j))

            def src_of(t):
                kind, g, j = t
                if kind == "dt":
                    return w_dt[:, :, :]
                c0 = cols_of(kind, g, j)
                return w_in0[c0 // 128]

            nxt = load_w(src_of(tl[0]))
            pend = {"silu": None, "T": None}

            def flush():
                if pend["T"] is not None:
                    f_ = pend["T"]
                    pend["T"] = None
                    f_()

            def flush_silu():
                if pend["silu"] is not None:
                    f_, g_ = pend["silu"]
                    pend["silu"] = None
                    f_()
                    assert pend["T"] is None
                    pend["T"] = g_
            for ti, t in enumerate(tl):
                kind, g, j = t
                nsrc = src_of(tl[ti + 1]) if ti + 1 < len(tl) else None
                if kind == "dt":
                    dst = dtraw_oth if pas == "oth" else dtraw_own

                    def ev(ps, bB, c0, n, dst=dst):
                        if pas == "own" and c0 >= 2048:
                            return
                        evac_copy(dst[:, c0:c0 + n], ps, [bB], [dtB], eng="act")
                    nxt = inproj(None, hlt, hltB, blocks, ev, nsrc, pre=nxt)
                    flush()
                    flush_silu()
                    continue
                if kind == "z":
                    zi = tcnt["n"] % 2
                    tcnt["n"] += 1

                    def ev(ps, bB, c0, n, zi=zi):
                        if c0 >= 2048:
                            return
                        S.op(ACT, lambda: nc.scalar.activation(ft[zi][:, c0:c0 + n], ps, AF.Silu),
                             reads=[bB], writes=[ftB[zi]])
                    flush()
                    nxt = inproj(None, hlt, hltB, blocks[:4], ev, nsrc, pre=nxt)
                    flush_silu()
                    r0 = (g * 4 + j) * 128
                    S.dma(SP, zT_s[r0:r0 + 128, :], ft[zi][:, 0:T], reads=[ftB[zi]])
                    continue
                pi = tcnt["n"] % 2
                tcnt["n"] += 1
                cidx = {"xs": 0, "B": 32, "C": 40}[kind] + (g * 4 + j if kind == "xs" else g)

                def ev(ps, bB, c0, n, pi=pi):
                    evac_copy(pre_t[pi][:, c0 + shift:c0 + shift + n], ps, [bB], [preB[pi]], eng="act")
                nxt = inproj(None, hlt, hltB, blocks, ev, nsrc, pre=nxt)
                lo, n = conv_lo, conv_n
                def tap(k):
                    return pre_t[pi][:, lo - 3 + k:lo - 3 + k + n]
                accs = (acc, acc2)[pi]
                accsB = (accB, acc2B)[pi]
                flush()
                S.op(POOL, lambda: nc.gpsimd.tensor_tensor(accs[:, lo:lo + n], tap(0),
                                                           convp[:, cidx, 0:1].to_broadcast([128, n]), ALU.mult),
                     reads=[preB[pi], cvB], writes=[accsB])
                for k in range(1, 7):
                    S.op(DVE, lambda k=k: nc.vector.scalar_tensor_tensor(accs[:, lo:lo + n], tap(k), convp[:, cidx, k:k + 1],
                                                                         accs[:, lo:lo + n], ALU.mult, ALU.add),
                         reads=[preB[pi], cvB, accsB], writes=[accsB])
                fo = 0 if pas == "own" else lo

                def post_silu(pi=pi, accs=accs, accsB=accsB, cidx=cidx, fo=fo, lo=lo, n=n):
                    S.op(ACT, lambda: nc.scalar.activation(ft[pi][:, fo:fo + n], accs[:, lo:lo + n], AF.Silu,
                                                           bias=convp[:, cidx, 7:8]),
                         reads=[accsB, cvB], writes=[ftB[pi]])

                def post(kind=kind, g=g, j=j, pi=pi):
                    if kind in ("B", "C") and pas == "own":
                        S.dma(SP, featBC[g, 0 if kind == "B" else 1, :, :], ft[pi][:, 0:T], reads=[ftB[pi]])
                    if kind == "C":
                        return
                    dcol = 512 if kind == "B" else j * 128
                    for c8 in range(0, len(chunks), 8):
                        grp = chunks[c8:c8 + 8]
                        bank = (c8 // 8) % 2
                        for ci, (_, _, col0) in enumerate(grp):
                            S.op(PE, lambda: nc.tensor.transpose(pb[bank][:, ci * 128:(ci + 1) * 128],
                                                                 ft[pi][:, col0:col0 + 128], ident_b[:]),
                                 reads=[ftB[pi], cB], writes=[pbB[bank]], inc=(ci == len(grp) - 1))
                        ng = len(grp)
                        evac_copy(tokg[:, c8:c8 + ng, dcol:dcol + 128],
                                  pb[bank][:, 0:ng * 128].rearrange("p (c q) -> p c q", q=128),
                                  [pbB[bank]], [tokgB], eng="act")
                    if kind == "xs" and j == 3:
                        if pas == "own":
                            S.dma(SP, tok_own[:, g, :].rearrange("(c p) e -> p c e", p=128), tokg[:, 0:16, :], reads=[tokgB])
                        else:
                            S.dma(SP, tok_oth[3:3 + 2048, g, :].rearrange("(c p) e -> p c e", p=128), tokg[:, 0:16, :],
                                  reads=[tokgB])
                            S.dma(SP, tok_oth[CTX0:CTX0 + 256, g, :].rearrange("(c p) e -> p c e", p=128), tokg[:, 16:18, :],
                                  reads=[tokgB])
                flush_silu()
                pend["silu"] = (post_silu, post)
            flush()
            flush_silu()
            flush()
            S.barrier()
            ph3.close()
        if dbg and stage == 2:
            S.dma(POOL, dbg_d[:, 0:640], tok_own[0:128, 0, :])
            S.dma(POOL, dbg_d[:, 640:1280], tok_own[1920:2048, 7, :])
            S.dma(POOL, dbg_d[:, 1280:1920], tok_oth[3:131, 0, :])
            S.dma(POOL, dbg_d[:, 1920:2560], tok_oth[CTX0 + 128:CTX0 + 256, 3, :])
            S.dma(POOL, dbg_d[:, 2560:2688], featBC[2, 1, :, 0:128])
            S.dma(POOL, dbg_d[:, 2688:2816], zT_s[5 * 128:6 * 128, 128:256])
            S.dma(SP, dbg_d[:, 2816:3328], dtraw_own[:, 0:512], reads=[dtB])
            S.dma(SP, dbg_d[:, 3328:3840], dtraw_oth[:, 1808:2320], reads=[dtB])
            S.barrier()
    if stage == 2:
        return finish(nc, S, es, out_d)

    with ExitStack() as ph:
        dtp = sb("dtp", [128, 2], F32, ph)
        acol = sb("acol", [128, 1], F32, ph)
        drep = sb("drep", [128, DI], BF16, ph)
        ng = sb("ng", [128, 32], F32, ph)
        ones3 = sb("ones3", [3, 128], BF16, ph)
        pB = Buf("ssdparams")
        S.dma(SP, dtp[:], dtp_d[:, :], writes=[pB])
        S.dma(SP, ng[:], ng_d[:, :], writes=[pB])
        S.dma(POOL, drep[:], drep_d[:, :], writes=[pB])
        S.op(DVE, lambda: nc.vector.memset(ones3[:], 1.0), writes=[pB])
        S.op(ACT, lambda: nc.scalar.activation(acol[:], dtp[:, 1:2], AF.Exp), reads=[pB], writes=[pB])
        S.op(DVE, lambda: nc.vector.tensor_scalar(acol[:], acol[:], -1.0, None, ALU.mult), reads=[pB], writes=[pB])
        S_f = sb("S_f", [128, 8, 512], F32, ph)
        S_b = sb("S_b", [128, 8, 512], F32, ph)
        Sbf_f = sb("Sbf_f", [128, 8, 512], BF16, ph)
        SfB = [Buf(f"S_f{g}") for g in range(8)]
        SbB = [Buf(f"S_b{g}") for g in range(8)]
        SbfB = [Buf(f"Sbf_f{g}") for g in range(8)]
        S.op(DVE, lambda: nc.vector.memset(S_f[:], 0.0), writes=SfB)
        S.op(DVE, lambda: nc.vector.memset(S_b[:], 0.0), writes=SbB)
        S.op(DVE, lambda: nc.vector.memset(Sbf_f[:], 0.0), writes=SbfB)

        scs = []
        for i in range(2):
            scs.append(dict(
                at_lt=sb(f"at_lt{i}", [128, 256], F32, ph), ac=sb(f"ac{i}", [128, 128], F32, ph),
                acT=sb(f"acT{i}", [128, 128], F32, ph), cdb=sb(f"cdb{i}", [128, 128], F32, ph),
                wtk=sb(f"wtk{i}", [128, 128], F32, ph), biasL=sb(f"biasL{i}", [128, 128], F32, ph),
                dec=sb(f"dec{i}", [128, 128], F32, ph), r3=sb(f"r3{i}", [128, 3, 128], BF16, ph),
                tmpa=sb(f"tmpa{i}", [128, 128], F32, ph), B=Buf(f"sc{i}")))
        p0aB = Buf("pf0a")
        gramB = pbB[1]
        gram_ps = pb[1][:, 0:256].bitcast(F32)
        rscrB = [Buf(f"rscr{c}") for c in range(NCH)]

        def chunk_scalars(par, nm, col0, rchunk=None, full=True):
            sc = scs[par]
            B_ = sc["B"]
            a_src, l_src = aT[nm], ldT[nm]
            S.op(PE, lambda: nc.tensor.transpose(pf[0][:, 0:128], a_src[:, col0:col0 + 128], ident_f),
                 reads=[dt2B, cB], writes=[p0aB], inc=False)
            S.op(PE, lambda: nc.tensor.transpose(pf[0][:, 128:256], l_src[:, col0:col0 + 128], ident_f),
                 reads=[dt2B, cB], writes=[p0aB])
            S.op(DVE, lambda: nc.vector.tensor_copy(sc["at_lt"][:, :], pf[0][:, 0:256]), reads=[p0aB], writes=[B_])
            at = sc["at_lt"]
            S.op(PE, lambda: nc.tensor.matmul(pf[1][:, 0:64], tri_f, at[:, 0:64], start=True, stop=True),
                 reads=[B_, cB], writes=[pfB[1]], inc=False)
            S.op(PE, lambda: nc.tensor.matmul(pf[1][:, 64:128], tri_b, at[:, 64:128], start=True, stop=True),
                 reads=[B_, cB], writes=[pfB[1]], inc=False)
            S.op(PE, lambda: nc.tensor.matmul(pf[1][:, 128:256], at[:, 0:128], tri_f, start=True, stop=True),
                 reads=[B_, cB], writes=[pfB[1]], inc=False)
            S.op(PE, lambda: nc.tensor.matmul(pf[1][:, 256:384], at[:, 0:128], tri_b, start=True, stop=True),
                 reads=[B_, cB], writes=[pfB[1]])
            S.op(DVE, lambda: nc.vector.tensor_copy(sc["ac"][:, :], pf[1][:, 0:128]), reads=[pfB[1]], writes=[B_])
            if full:
                S.op(DVE, lambda: nc.vector.tensor_copy(sc["acT"][0:64, :], pf[1][0:64, 128:256]), reads=[pfB[1]], writes=[B_])
                S.op(DVE, lambda: nc.vector.tensor_copy(sc["acT"][64:128, :], pf[1][64:128, 256:384]), reads=[pfB[1]], writes=[B_])
            S.op(PE, lambda: nc.tensor.matmul(pf[1][:, 384:448], e_last, sc["ac"][:, 0:64], start=True, stop=True),
                 reads=[B_, cB], writes=[pfB[1]], inc=False)
            S.op(PE, lambda: nc.tensor.matmul(pf[1][:, 448:512], e_first, sc["ac"][:, 64:128], start=True, stop=True),
                 reads=[B_, cB], writes=[pfB[1]])
            S.op(ACT, lambda: nc.scalar.activation(sc["cdb"][:, :], pf[1][:, 384:512], AF.Exp), reads=[pfB[1]], writes=[B_])
            S.op(DVE, lambda: nc.vector.tensor_tensor(sc["tmpa"][:, :], pf[1][:, 384:512], sc["ac"][:, :], ALU.subtract),
                 reads=[pfB[1], B_], writes=[B_])
            S.op(DVE, lambda: nc.vector.tensor_tensor(sc["tmpa"][:, :], sc["tmpa"][:, :], at[:, 128:256], ALU.add),
                 reads=[B_], writes=[B_])
            S.op(ACT, lambda: nc.scalar.activation(sc["wtk"][:, :], sc["tmpa"][:, :], AF.Exp), reads=[B_], writes=[B_])
            if full:
                S.op(DVE, lambda: nc.vector.tensor_tensor(sc["biasL"][:, :], at[:, 128:256], sc["ac"][:, :], ALU.subtract),
                     reads=[B_], writes=[B_])
                S.op(ACT, lambda: nc.scalar.activation(sc["dec"][:, :], sc["ac"][:, :], AF.Exp), reads=[B_], writes=[B_])
            if rchunk is not None:
                r3 = sc["r3"]
                S.op(DVE, lambda: nc.vector.tensor_copy(r3[:, 0, :], sc["acT"][:, :]), reads=[B_], writes=[B_])
                S.op(DVE, lambda: nc.vector.tensor_tensor(sc["tmpa"][:, :], sc["acT"][:, :], r3[:, 0, :], ALU.subtract),
                     reads=[B_], writes=[B_])
                S.op(DVE, lambda: nc.vector.tensor_copy(r3[:, 1, :], sc["tmpa"][:, :]), reads=[B_], writes=[B_])
                S.op(DVE, lambda: nc.vector.tensor_tensor(sc["tmpa"][:, :], sc["tmpa"][:, :], r3[:, 1, :], ALU.subtract),
                     reads=[B_], writes=[B_])
                S.op(DVE, lambda: nc.vector.tensor_copy(r3[:, 2, :], sc["tmpa"][:, :]), reads=[B_], writes=[B_])
                S.dma(SP, rscr[rchunk].rearrange("j p q -> p j q"), r3[:, :, :], reads=[B_], writes=[rscrB[rchunk]])

        def bc8(tile_ap, c0):
            return tile_ap[:, c0:c0 + 8].unsqueeze(2).to_broadcast([128, 8, 64])

        def v3(ap2d):
            return ap2d.rearrange("p (a b) -> p a b", b=64)

        tk = [sb(f"tk{i}", [128, 640], BF16, ph) for i in range(2)]
        tkB = [Buf("tk0"), Buf("tk1")]
        xsw = [sb(f"xsw{i}", [128, 512], BF16, ph) for i in range(2)]
        xswB = [Buf("xsw0"), Buf("xsw1")]

        def state_prep(par, d, g, tkt, tkb, Sd, SdB, k, swap=False):
            sc = scs[par]
            S.op(POOL, lambda: nc.gpsimd.tensor_tensor(v3(xsw[k][:, :]), v3(tkt[:, 0:512]), bc8(sc["wtk"], d * 64 + 8 * g), ALU.mult),
                 reads=[tkb, sc["B"]], writes=[xswB[k]])
            if swap:
                S.op(DVE, lambda: nc.vector.tensor_tensor(v3(Sd[:, g, :]), v3(Sd[:, g, :]), bc8(sc["cdb"], d * 64 + 8 * g), ALU.mult),
                     reads=[sc["B"], SdB[g]], writes=[SdB[g]])
            else:
                S.op(POOL, lambda: nc.gpsimd.tensor_tensor(v3(Sd[:, g, :]), v3(Sd[:, g, :]), bc8(sc["cdb"], d * 64 + 8 * g), ALU.mult),
                     reads=[sc["B"], SdB[g]], writes=[SdB[g]])

        def state_fin(g, tkt, tkb, Sd, SdB, k, bank):
            S.op(PE, lambda: nc.tensor.matmul(pf[bank][:, :], tkt[:, 512:640], xsw[k][:, :], start=True, stop=True),
                 reads=[tkb, xswB[k]], writes=[pfB[bank]])
            S.op(DVE, lambda: nc.vector.tensor_tensor(Sd[:, g, :], Sd[:, g, :], pf[bank][:, :], ALU.add),
                 reads=[pfB[bank], SdB[g]], writes=[SdB[g]])

        def state_update(par, d, g, tkt, tkb, Sd, SdB, k):
            state_prep(par, d, g, tkt, tkb, Sd, SdB, k, swap=True)
            state_fin(g, tkt, tkb, Sd, SdB, k, 5 - k)

        aT = {"own": dtraw_own, "oth": dtraw_oth}
        ph_s1 = ExitStack()
        ldT = {"own": sb("ldT_own", [128, T], F32, ph), "oth": sb("ldT_oth", [128, OW], F32, ph_s1)}
        dt2B = Buf("dt2")
        for nm, raw, wdt in (("own", dtraw_own, T), ("oth", dtraw_oth, OW)):
            S.op(ACT, lambda: nc.scalar.activation(raw[:, 0:wdt], raw[:, 0:wdt], AF.Exp, bias=dtp[:, 0:1]),
                 reads=[dtB, pB], writes=[dtB])
            S.op(ACT, lambda: nc.scalar.activation(raw[:, 0:wdt], raw[:, 0:wdt], AF.Ln, bias=1.0),
                 reads=[dtB], writes=[dtB])
            S.op(ACT, lambda: nc.scalar.activation(ldT[nm][:, 0:wdt], raw[:, 0:wdt], AF.Ln),
                 reads=[dtB], writes=[dt2B])
            S.op(DVE, lambda: nc.vector.tensor_scalar(raw[:, 0:wdt], raw[:, 0:wdt], acol[:, 0:1], None, ALU.mult),
                 reads=[dtB, pB, dt2B], writes=[dt2B, dtB])

        sbsave = sb("sbsave", [128, 8, 512], BF16, ph_s1)
        sbsB = Buf("sbsave")
        visits = []
        visits += [("oth", CTX0 + c * 128, tok_oth, CTX0 + c * 128, 0, None) for c in (0, 1)]
        visits += [("oth", CTX0 + c * 128, tok_oth, CTX0 + c * 128, 1, None) for c in (1, 0)]
        visits += [("oth", 3 + c * 128, tok_oth, 3 + c * 128, 1, None) for c in range(15, -1, -1)]
        visits += [("own", c * 128, tok_own, c * 128, 1, c) for c in range(15, -1, -1)]
        items = [(vi, g) for vi in range(len(visits)) for g in range(8)]

        def s1_load(n):
            vi, g = items[n]
            nm, col0, tdr, row0, d, save = visits[vi]
            S.dma(SP, tk[n % 2][:, :], tdr[row0:row0 + 128, g, :], writes=[tkB[n % 2]])
        s1_load(0)
        for n, (vi, g) in enumerate(items):
            nm, col0, tdr, row0, d, save = visits[vi]
            par = vi % 2
            if n + 1 < len(items):
                s1_load(n + 1)
            if g == 0:
                if vi == 0:
                    chunk_scalars(par, nm, col0, full=False)
                if vi + 1 < len(visits):
                    chunk_scalars((vi + 1) % 2, visits[vi + 1][0], visits[vi + 1][1], full=False)
                if save is not None:
                    S.op(ACT, lambda: nc.scalar.copy(sbsave[:, :, :], S_b[:, :, :]), reads=SbB, writes=[sbsB])
                    S.dma(SP, sbin[save], sbsave[:, :, :], reads=[sbsB])
            if d == 0:
                state_update(par, 0, g, tk[n % 2], tkB[n % 2], S_f, SfB, n % 2)
            else:
                state_update(par, 1, g, tk[n % 2], tkB[n % 2], S_b, SbB, n % 2)
        S.op(ACT, lambda: nc.scalar.copy(Sbf_f[:, :, :], S_f[:, :, :]), reads=SfB, writes=SbfB)
        S.barrier()
        ph_s1.close()
        if dbg and stage == 3:
            S.dma(SP, dbg_d[:, 0:4096], S_f[:, :, :].rearrange("p g e -> p (g e)"), reads=SfB)
            S.barrier()
        if stage == 3:
            return finish(nc, S, es, out_d)

        NL = 3
        tk2 = [sb(f"tk2_{i}", [128, 640], BF16, ph) for i in range(NL)]
        tk2B = [Buf(f"tk2_{i}") for i in range(NL)]
        bct = [sb(f"bct{i}", [128, 2, 128], BF16, ph) for i in range(NL)]
        zt = [sb(f"zt{i}", [128, 4, 128], BF16, ph) for i in range(NL)]
        rg = [sb(f"rg{i}", [3, 2, 1024], BF16, ph) for i in range(NL)]
        sbl = [sb(f"sbl{i}", [128, 512], BF16, ph) for i in range(NL)]
        ldB = [Buf(f"ld{i}") for i in range(NL)]
        cbm = [sb(f"cbm{i}", [128, 2, 128], BF16, ph) for i in range(2)]
        cbmB = [Buf("cbm0"), Buf("cbm1")]
        Lt = [sb(f"Lt{i}", [128, 16, 128], BF16, ph) for i in range(2)]
        LtB = [[Buf(f"Lt{i}_{q}") for q in range(4)] for i in range(2)]
        Gt = [sb(f"Gt{i}", [128, 16, 128], BF16, ph) for i in range(2)]
        GtB = [[Buf(f"Gt{i}_{q}") for q in range(4)] for i in range(2)]
        xsD = [sb(f"xsD{i}", [128, 512], BF16, ph) for i in range(2)]
        xsDB = [Buf("xsD0"), Buf("xsD1")]
        yo = sb("yo", [128, 2, 512], BF16, ph)
        yoB = Buf("yo")
        ytot = sb("ytot", [128, 512], BF16, ph)
        ytB = Buf("ytot")
        ygp = sb("ygp", [128, 4, 128], BF16, ph)
        ygpB = Buf("ygp")
        ygs = sb("ygs", [128, 4, 128], BF16, ph)
        ygsB = Buf("ygs")
        gtmp = sb("gtmp", [128, 128], F32, ph)
        gtB = Buf("gtmp")
        items2 = [(c, g) for c in range(NCH) for g in range(8)]
        N2 = len(items2)

        def s2_load(n):
            c, g = items2[n]
            i = n % NL
            if g == 0 and c > 0:
                chunk_scalars(c % 2, "own", c * 128, rchunk=c)
            S.dma(SP, tk2[i][:, :], tok_own[c * 128:(c + 1) * 128, g, :], writes=[tk2B[i]])
            S.dma(SP, bct[i][:, :, :], featBC[g, :, :, c * 128:(c + 1) * 128].rearrange("w n t -> n w t"), writes=[ldB[i]])
            S.dma(SP, zt[i][:, :, :], zT_s[g * 512:(g + 1) * 512, c * 128:(c + 1) * 128].rearrange("(j p) t -> p j t", p=128),
                  writes=[ldB[i]])
            S.dma(SP, rg[i][:, :, :], rscr[c, :, :, :].rearrange("j (d h) q -> j d h q", d=2)[:, :, 8 * g:8 * g + 8, :]
                  .rearrange("j d h q -> j d (h q)"), reads=[rscrB[c]], writes=[ldB[i]])
            S.dma(SP, sbl[i][:, :], sbin[c, :, g, :], writes=[ldB[i]])

        bk = {"n": 0}

        def a_prep(n):
            c, g = items2[n]
            i = n % NL
            a = n % 2
            S.op(PE, lambda: nc.tensor.matmul(pf[0][:, 0:128], bct[i][:, 0, :], bct[i][:, 1, :], start=True, stop=True),
                 reads=[ldB[i]], writes=[p0aB])
            S.op(DVE, lambda: nc.vector.tensor_tensor(cbm[a][:, 0, :], pf[0][:, 0:128], tri_f, ALU.mult),
                 reads=[p0aB, cB], writes=[cbmB[a]])
            S.op(DVE, lambda: nc.vector.tensor_tensor(cbm[a][:, 1, :], pf[0][:, 0:128], tri_b, ALU.mult),
                 reads=[p0aB, cB], writes=[cbmB[a]])
            S.op(POOL, lambda: nc.gpsimd.tensor_tensor(xsD[a][:, :], tk2[i][:, 0:512], drep[:, g * 512:(g + 1) * 512], ALU.mult),
                 reads=[tk2B[i], pB], writes=[xsDB[a]])
            state_prep(c % 2, 0, g, tk2[i], tk2B[i], S_f, SfB, a)

        def a_quarter(n, qd):
            c, g = items2[n]
            i = n % NL
            a = n % 2
            sc = scs[c % 2]
            d, half = qd // 2, qd % 2
            bank = 1 + (bk["n"] % 2)
            bk["n"] += 1
            S.op(PE, lambda: nc.tensor.matmul(pf[bank][:, :], ones3[:, :], rg[i][0:3, d, half * 512:(half + 1) * 512],
                                              start=True, stop=True),
                 reads=[ldB[i], pB], writes=[pfB[bank]])
            for hh in range(4):
                idx = d * 8 + half * 4 + hh
                h = 8 * g + half * 4 + hh
                S.op(ACT, lambda: nc.scalar.activation(Lt[a][:, idx, :], pf[bank][:, hh * 128:(hh + 1) * 128], AF.Exp,
                                                       bias=sc["biasL"][:, d * 64 + h:d * 64 + h + 1]),
                     reads=[pfB[bank], sc["B"]], writes=[LtB[a][qd]])
            i0 = d * 8 + half * 4
            S.op(DVE, lambda: nc.vector.scalar_tensor_tensor(Gt[a][:, i0:i0 + 4, :], Lt[a][:, i0:i0 + 4, :], 3.0e38,
                                                             cbm[a][:, d, :].unsqueeze(1).to_broadcast([128, 4, 128]),
                                                             ALU.min, ALU.mult),
                 reads=[LtB[a][qd], cbmB[a]], writes=[GtB[a][qd]])

        def b1(n):
            c, g = items2[n]
            i = n % NL
            a = n % 2
            sc = scs[c % 2]
            S.op(PE, lambda: nc.tensor.matmul(pf[3][:, :], ident_b[:, :], xsD[a][:, :], start=True, stop=False),
                 reads=[xsDB[a], cB], writes=[pfB[3]], inc=False)
            for hh8 in range(8):
                for d in range(2):
                    last = (hh8 == 7 and d == 1)
                    qd = d * 2 + hh8 // 4
                    S.op(PE, lambda: nc.tensor.matmul(pf[3][:, hh8 * 64:(hh8 + 1) * 64], Gt[a][:, d * 8 + hh8, :],
                                                      tk2[i][:, hh8 * 64:(hh8 + 1) * 64], start=False, stop=(d == 1),
                                                      skip_group_check=True),
                         reads=[GtB[a][qd], tk2B[i]], writes=[pfB[3]], inc=last)
            yoff(n, 0)

        def yoff(n, d):
            c, g = items2[n]
            i = n % NL
            sc = scs[c % 2]
            rhs = Sbf_f[:, g, :] if d == 0 else sbl[i][:, :]
            S.op(PE, lambda: nc.tensor.matmul(pf[4][:, :], bct[i][:, 1, :], rhs, start=True, stop=True),
                 reads=[ldB[i], SbfB[g]], writes=[pfB[4]])
            S.op(DVE, lambda: nc.vector.tensor_tensor(v3(yo[:, d, :]), v3(pf[4][:, :]), bc8(sc["dec"], d * 64 + 8 * g), ALU.mult),
                 reads=[pfB[4], sc["B"]], writes=[yoB])

        def b2(n):
            yoff(n, 1)
            S.op(DVE, lambda: nc.vector.tensor_tensor(yo[:, 0, :], yo[:, 0, :], yo[:, 1, :], ALU.add), reads=[yoB], writes=[yoB])
            S.op(DVE, lambda: nc.vector.tensor_tensor(ytot[:, :], pf[3][:, :], yo[:, 0, :], ALU.add),
                 reads=[pfB[3], yoB], writes=[ytB])

        def b3(n):
            c, g = items2[n]
            i = n % NL
            a = n % 2
            for j in range(4):
                S.op(PE, lambda: nc.tensor.transpose(pb[0][:, j * 128:(j + 1) * 128], ytot[:, j * 128:(j + 1) * 128], ident_b[:]),
                     reads=[ytB, cB], writes=[pbB[0]], inc=(j == 3))
            S.op(DVE, lambda: nc.vector.tensor_tensor(ygp[:, :, :].rearrange("p j q -> p (j q)"), pb[0][:, 0:512],
                                                      zt[i][:, :, :].rearrange("p j q -> p (j q)"), ALU.mult),
                 reads=[pbB[0], ldB[i]], writes=[ygpB])
            for j in range(4):
                S.op(PE, lambda: nc.tensor.matmul(gram_ps, ygp[:, j, :], ygp[:, j, :],
                                                  start=(g == 0 and j == 0), stop=(g == 7 and j == 3), skip_group_check=True),
                     reads=[ygpB], writes=[gramB], inc=(j == 3))
            S.op(POOL, lambda: nc.gpsimd.tensor_tensor(ygs[:, :, :], ygp[:, :, :],
                                                       ng[:, g * 4:g * 4 + 4].unsqueeze(2).to_broadcast([128, 4, 128]), ALU.mult),
                 reads=[ygpB, pB], writes=[ygsB])
            S.dma(SP, ygT_s[g * 512:(g + 1) * 512, c * 128:(c + 1) * 128].rearrange("(j p) q -> p j q", p=128), ygs[:, :, :],
                  reads=[ygsB])
            state_fin(g, tk2[i], tk2B[i], S_f, SfB, a, 5)
            S.op(ACT, lambda: nc.scalar.copy(Sbf_f[:, g, :], S_f[:, g, :]), reads=[SfB[g]], writes=[SbfB[g]])
            if g == 7:
                S.op(DVE, lambda: nc.vector.tensor_tensor(gtmp[:, :], gram_ps, ident_f, ALU.mult),
                     reads=[gramB, cB], writes=[gtB])
                S.op(DVE, lambda: nc.vector.reduce_sum(small[:, 0:1], gtmp[:, :], axis=AX.X), reads=[gtB], writes=[smB])
                S.op(ACT, lambda: nc.scalar.activation(small[:, 1:2], small[:, 0:1], AF.Sqrt, bias=EPS, scale=1.0 / DI),
                     reads=[smB], writes=[smB])
                S.op(DVE, lambda: nc.vector.reciprocal(rstd_y[:, c:c + 1], small[:, 1:2]), reads=[smB], writes=[ryB])

        chunk_scalars(0, "own", 0, rchunk=0)
        s2_load(0)
        s2_load(1)
        a_prep(0)
        for qd in range(4):
            a_quarter(0, qd)
        for n in range(N2):
            if n + 2 < N2:
                s2_load(n + 2)
            nx = n + 1 < N2
            if nx:
                a_prep(n + 1)
                a_quarter(n + 1, 0)
                a_quarter(n + 1, 1)
            b1(n)
            if nx:
                a_quarter(n + 1, 2)
            b2(n)
            if nx:
                a_quarter(n + 1, 3)
            b3(n)
        S.barrier()
        if dbg and stage == 4:
            S.dma(SP, dbg_d[:, 0:16], rstd_y[:, :], reads=[ryB])
            S.dma(POOL, dbg_d[:, 128:128 + 2048], ygT_s[0:128, :])
            S.dma(POOL, dbg_d[:, 2176:2176 + 1024], ygT_s[DI - 128:DI, 0:1024])
            S.barrier()
    if stage == 4:
        return finish(nc, S, es, out_d)

    def out_proj(srcT, w_dram, resid, layer, use_rstd, final):
        with ExitStack() as ph:
            nyb = 1 if final else 2
            yblk = [sb(f"yblk{i}", [128, 32, 512], BF16, ph) for i in range(nyb)]
            yblkB = [Buf(f"yblk{i}") for i in range(nyb)]
            wb = [sb(f"wb{i}", [128, 32, 512], BF16, ph) for i in range(2)]
            wbB = [Buf("wb0"), Buf("wb1")]
            xr = [sb(f"xr{i}", [128, 512], F32, ph) for i in range(2)]
            xrB = [Buf("xr0"), Buf("xr1")]
            x2 = sb("x2", [128, 4, D], F32, ph) if final else None
            x2B = [Buf(f"x2_{i}") for i in range(4)]
            ot = [sb(f"ot{i}", [128, 512], F32, ph) for i in range(2)]
            otB = [Buf("ot0"), Buf("ot1")]
            jk = sb("jk", [128, D], BF16, ph) if final else None
            jkB = Buf("jk")
            seq = [(tb, dblk) for tb in range(4) for dblk in range(4)] if final else \
                  [(tb, dblk) for dblk in range(4) for tb in range(4)]
            wi = {"n": 0, "cur": None, "slot": None}
            yi = {"n": 0, "cur": None, "slot": None}

            def get_w(dblk):
                if wi["cur"] == dblk:
                    return wi["slot"]
                i = wi["n"] % 2
                wi["n"] += 1
                for hf in range(2):
                    S.dma(POOL, wb[i][:, hf * 16:(hf + 1) * 16, :], w_dram[dblk, :, hf * 16:(hf + 1) * 16, :], writes=[wbB[i]])
                wi["cur"], wi["slot"] = dblk, i
                return i

            def get_y(tb):
                if yi["cur"] == tb:
                    return yi["slot"]
                i = yi["n"] % nyb
                yi["n"] += 1
                S.dma(SP, yblk[i][:, :, :], srcT[:, tb * 512:(tb + 1) * 512].rearrange("(j p) t -> p j t", p=128), writes=[yblkB[i]])
                yi["cur"], yi["slot"] = tb, i
                return i
            wnext = get_w(seq[0][1])
            ynext = get_y(seq[0][0])
            cnt = 0
            for si, (tb, dblk) in enumerate(seq):
                wcur, ycur = wnext, (ynext if nyb == 2 else get_y(tb))
                if si + 1 < len(seq):
                    wnext = get_w(seq[si + 1][1])
                    if nyb == 2:
                        ynext = get_y(seq[si + 1][0])
                for tt in range(4):
                    tok0 = tb * 512 + tt * 128
                    ch = tok0 // 128
                    k = cnt % 2
                    cnt += 1
                    S.dma(SP, xr[k][:, :], resid[tok0:tok0 + 128, dblk * 512:(dblk + 1) * 512], writes=[xrB[k]])
                    for j in range(32):
                        S.op(PE, lambda: nc.tensor.matmul(pf[k][:, :], yblk[ycur][:, j, tt * 128:(tt + 1) * 128], wb[wcur][:, j, :],
                                                          start=(j == 0), stop=(j == 31)),
                             reads=[yblkB[ycur], wbB[wcur]], writes=[pfB[k]], inc=(j == 31))
                    dst = x2[:, tt, dblk * 512:(dblk + 1) * 512] if final else ot[k][:, :]
                    dB = x2B[tt] if final else otB[k]
                    if use_rstd:
                        S.op(DVE, lambda: nc.vector.scalar_tensor_tensor(ot[k][:, :], pf[k][:, :], rstd_y[:, ch:ch + 1],
                                                                         gate_rep[:, layer, dblk * 512:(dblk + 1) * 512],
                                                                         ALU.mult, ALU.mult),
                             reads=[pfB[k], ryB, grB], writes=[otB[k]])
                    else:
                        S.op(DVE, lambda: nc.vector.tensor_tensor(ot[k][:, :], pf[k][:, :],
                                                                  gate_rep[:, layer, dblk * 512:(dblk + 1) * 512], ALU.mult),
                             reads=[pfB[k], grB], writes=[otB[k]])
                    S.op(DVE, lambda: nc.vector.tensor_tensor(dst, ot[k][:, :], xr[k][:, :], ALU.add),
                         reads=[otB[k], xrB[k]], writes=[dB] if final else [otB[k]])
                    if not final:
                        S.dma(SP, x1_s[tok0:tok0 + 128, dblk * 512:(dblk + 1) * 512], ot[k][:, :], reads=[otB[k]])
                    elif dblk == 3:
                        S.op(ACT, lambda: nc.scalar.activation(jk[:, :], x2[:, tt, :], AF.Square, accum_out=small[:, 8 + tt:9 + tt]),
                             reads=[x2B[tt]], writes=[jkB, smB])
                        S.op(ACT, lambda: nc.scalar.activation(small[:, 16 + tt:17 + tt], small[:, 8 + tt:9 + tt], AF.Sqrt,
                                                               bias=EPS, scale=1.0 / D), reads=[smB], writes=[smB])
                        S.op(DVE, lambda: nc.vector.reciprocal(small[:, 16 + tt:17 + tt], small[:, 16 + tt:17 + tt]),
                             reads=[smB], writes=[smB])
                        S.op(DVE, lambda: nc.vector.scalar_tensor_tensor(x2[:, tt, :], x2[:, tt, :], small[:, 16 + tt:17 + tt],
                                                                         fng_rep[:, :], ALU.mult, ALU.mult),
                             reads=[x2B[tt], smB, grB], writes=[x2B[tt]])
                        S.dma(SP, out_d[tok0:tok0 + 128, :], x2[:, tt, :], reads=[x2B[tt]])
            S.barrier()

    out_proj(ygT_s, w_out0, x_own, 0, True, False)
    if dbg and stage >= 5:
        S.dma(SP, dbg_d[:, 0:2048], x1_s[0:128, :])
        S.dma(SP, dbg_d[:, 2048:4096], x1_s[T - 128:T, :])
        S.barrier()
    if stage == 5:
        return finish(nc, S, es, out_d)

    blocks4 = [(0, 512), (512, 512), (1024, 512), (1536, 512)]
    with ExitStack() as ph:
        hlt = sb("hlt1", [128, KT, T], BF16, ph)
        hltB = [Buf(f"hlt1_{k}") for k in range(KT)]
        with ExitStack() as ph2:
            jobs = [(x1_s[i * 128:(i + 1) * 128, :], 0, i * 128, 0, 128) for i in range(16)]
            token_prep(ph2, hlt, hltB, jobs, 1)
            S.barrier()
        ft = [sb(f"ft1_{i}", [128, T], BF16, ph) for i in range(2)]
        ftB = [Buf("ft1_0"), Buf("ft1_1")]
        with ExitStack() as ph2:
            vtok = sb("vtok", [128, 16, 512], BF16, ph2)
            vtokB = Buf("vtok")
            nxt = load_w(w_in1[32])
            for j in range(32):
                fi = j % 2
                nsrc = w_in1[32 + j + 1] if j + 1 < 32 else None

                def ev(ps, bB, c0, n, fi=fi):
                    S.op(ACT, lambda: nc.scalar.activation(ft[fi][:, c0:c0 + n], ps, AF.Gelu), reads=[bB], writes=[ftB[fi]])
                nxt = inproj(None, hlt, hltB, blocks4, ev, nsrc, pre=nxt)
                for c8 in range(0, 16, 8):
                    bank = (c8 // 8) % 2
                    for ci in range(8):
                        col0 = (c8 + ci) * 128
                        S.op(PE, lambda: nc.tensor.transpose(pb[bank][:, ci * 128:(ci + 1) * 128], ft[fi][:, col0:col0 + 128], ident_b[:]),
                             reads=[ftB[fi], cB], writes=[pbB[bank]], inc=(ci == 7))
                    evac_copy(vtok[:, c8:c8 + 8, (j % 4) * 128:(j % 4) * 128 + 128],
                              pb[bank][:, 0:1024].rearrange("p (c q) -> p c q", q=128), [pbB[bank]], [vtokB])
                if j % 4 == 3:
                    S.dma(SP, gv_s[:, (j // 4) * 512:(j // 4 + 1) * 512].rearrange("(c p) e -> p c e", p=128), vtok[:, :, :],
                          reads=[vtokB])
            S.barrier()
        with ExitStack() as ph2:
            gr = [sb(f"gr{i}", [128, DI], BF16, ph2) for i in range(2)]
            grB_ = [Buf("gr0"), Buf("gr1")]
            jk2 = sb("jk2", [128, DI], BF16, ph2)
            jk2B = Buf("jk2")
            lst = sb("lst", [128, 2, 8], F32, ph2)
            lstB = [Buf("lst0"), Buf("lst1")]
            S.dma(SP, gr[0][:, :], gv_s[0:128, :], writes=[grB_[0]])
            for c in range(16):
                i = c % 2
                if c + 1 < 16:
                    S.dma(SP, gr[1 - i][:, :], gv_s[(c + 1) * 128:(c + 2) * 128, :], writes=[grB_[1 - i]])
                l = lst[:, i, :]
                S.op(ACT, lambda: nc.scalar.activation(jk2[:, :], gr[i][:, :], AF.Square, accum_out=l[:, 0:1]),
                     reads=[grB_[i]], writes=[jk2B, lstB[i]])
                S.op(DVE, lambda: nc.vector.reduce_sum(l[:, 1:2], gr[i][:, :], axis=AX.X), reads=[grB_[i]], writes=[lstB[i]])
                S.op(DVE, lambda: nc.vector.tensor_scalar(l[:, 2:3], l[:, 1:2], 1.0 / DI, None, ALU.mult), reads=[lstB[i]], writes=[lstB[i]])
                S.op(DVE, lambda: nc.vector.tensor_tensor(l[:, 3:4], l[:, 2:3], l[:, 2:3], ALU.mult), reads=[lstB[i]], writes=[lstB[i]])
                S.op(DVE, lambda: nc.vector.scalar_tensor_tensor(l[:, 4:5], l[:, 0:1], 1.0 / DI, l[:, 3:4], ALU.mult, ALU.subtract),
                     reads=[lstB[i]], writes=[lstB[i]])
                S.op(ACT, lambda: nc.scalar.activation(l[:, 5:6], l[:, 4:5], AF.Sqrt, bias=EPS, scale=1.0), reads=[lstB[i]], writes=[lstB[i]])
                S.op(DVE, lambda: nc.vector.reciprocal(l[:, 5:6], l[:, 5:6]), reads=[lstB[i]], writes=[lstB[i]])
                S.op(DVE, lambda: nc.vector.tensor_scalar(gr[i][:, :], gr[i][:, :], l[:, 2:3], l[:, 5:6], ALU.subtract, ALU.mult),
                     reads=[lstB[i], grB_[i]], writes=[grB_[i]])
                S.dma(SP, gv_s[c * 128:(c + 1) * 128, :], gr[i][:, :], reads=[grB_[i]])
            S.barrier()
        with ExitStack() as ph2:
            wsf = sb("wsf", [128, 16, 128], F32, ph2)
            wsb = sb("wsb", [128, 16, 128], BF16, ph2)
            bsr = sb("bsr", [1, 16 * 128], F32, ph2)
            lng = sb("lng", [128, 32], F32, ph2)
            lnb = sb("lnb", [128, 32], F32, ph2)
            bbt = sb("bbt", [128, 32, 128], F32, ph2)
            bsrep = sb("bsrep", [128, 128], F32, ph2)
            p1B = Buf("l1params")
            bbB = Buf("bbt")
            bsrepB = Buf("bsrep")
            S.dma(SP, wsf[:, :, :], wsT_d[:, :, :], writes=[p1B])
            S.dma(SP, bsr[:, :], bs_d[:, :], writes=[p1B])
            S.dma(SP, lng[:, :], lng_d[:, :], writes=[p1B])
            S.dma(SP, lnb[:, :], lnb_d[:, :], writes=[p1B])
            S.op(DVE, lambda: nc.vector.tensor_copy(wsb[:, :, :], wsf[:, :, :]), reads=[p1B], writes=[p1B])
            for grp in range(16):
                S.op(PE, lambda: nc.tensor.matmul(pf[0][:, 0:128], ones_f[:, :], wsf[:, grp, :], start=True, stop=True),
                     reads=[p1B, cB], writes=[pfB[0]])
                S.op(PE, lambda: nc.tensor.matmul(pf[1][:, 0:128], ones_f[0:1, :], bsr[0:1, grp * 128:(grp + 1) * 128], start=True, stop=True),
                     reads=[p1B, cB], writes=[pfB[1]])
                S.op(ACT, lambda: nc.scalar.copy(bsrep[:, :], pf[1][:, 0:128]), reads=[pfB[1]], writes=[bsrepB])
                for jj in range(2):
                    j = grp * 2 + jj
                    S.op(DVE, lambda: nc.vector.scalar_tensor_tensor(bbt[:, j, :], pf[0][:, 0:128], lnb[:, j:j + 1], bsrep[:, :],
                                                                     ALU.mult, ALU.add),
                         reads=[pfB[0], p1B, bsrepB], writes=[bbB])
            ug = sb("ug", [128, T], BF16, ph2)
            ugB = Buf("ug")
            vnt = [sb(f"vnt{i}", [128, 16, 128], BF16, ph2) for i in range(2)]
            vntB = [Buf("vnt0"), Buf("vnt1")]
            sTt = [sb(f"sTt{i}", [128, T], BF16, ph2) for i in range(2)]
            sTtB = [Buf("sTt0"), Buf("sTt1")]
            tv = sb("tv", [128, 512], F32, ph2)
            tvB = Buf("tv")

            def usrc(j):
                return w_in1[j]

            def gsrc(j):
                return w_in1[64 + j]
            nxt = load_w(usrc(0))
            for j in range(32):
                i = j % 2
                grp = j // 2
                S.dma(SP, vnt[i][:, :, :], gv_s[:, j * 128:(j + 1) * 128].rearrange("(c p) e -> p c e", p=128), writes=[vntB[i]])

                def ev_u(ps, bB, c0, n):
                    S.op(ACT, lambda: nc.scalar.activation(ft[0][:, c0:c0 + n], ps, AF.Gelu), reads=[bB], writes=[ftB[0]])

                def ev_g(ps, bB, c0, n):
                    S.op(ACT, lambda: nc.scalar.activation(ft[1][:, c0:c0 + n], ps, AF.Silu), reads=[bB], writes=[ftB[1]])
                nxt = inproj(None, hlt, hltB, blocks4, ev_u, gsrc(j), pre=nxt)
                nxt = inproj(None, hlt, hltB, blocks4, ev_g, usrc(j + 1) if j + 1 < 32 else None, pre=nxt)
                S.op(DVE, lambda: nc.vector.tensor_tensor(ug[:, :], ft[0][:, :], ft[1][:, :], ALU.mult),
                     reads=[ftB[0], ftB[1]], writes=[ugB])
                for c4 in range(4):
                    bank = 4 + (c4 % 2)
                    for cc in range(4):
                        c = c4 * 4 + cc
                        S.op(PE, lambda: nc.tensor.matmul(pf[bank][:, cc * 128:(cc + 1) * 128], vnt[i][:, c, :], wsb[:, grp, :],
                                                          start=True, stop=True),
                             reads=[vntB[i], p1B], writes=[pfB[bank]], inc=(cc == 3))
                    S.op(DVE, lambda: nc.vector.scalar_tensor_tensor(tv[:, :].rearrange("p (a q) -> p a q", q=128),
                                                                     pf[bank][:, :].rearrange("p (a q) -> p a q", q=128),
                                                                     lng[:, j:j + 1],
                                                                     bbt[:, j, :].unsqueeze(1).to_broadcast([128, 4, 128]),
                                                                     ALU.mult, ALU.add),
                         reads=[pfB[bank], p1B, bbB], writes=[tvB])
                    S.op(DVE, lambda: nc.vector.tensor_tensor(sTt[i][:, c4 * 512:(c4 + 1) * 512], tv[:, :], ug[:, c4 * 512:(c4 + 1) * 512],
                                                              ALU.mult),
                         reads=[tvB, ugB], writes=[sTtB[i]])
                S.dma(SP, sT_s[j * 128:(j + 1) * 128, :], sTt[i][:, :], reads=[sTtB[i]])
            S.barrier()
    out_proj(sT_s, w_out1, x1_s, 1, False, True)
    return finish(nc, S, es, out_d)


def finish(nc, S, es, out_d):
    S.barrier()
    es.close()
    return nc


def make_consts():
    k = np.arange(128)
    ident = np.eye(128, dtype=np.float32)
    tri_f = (k[:, None] <= k[None, :]).astype(np.float32)
    tri_b = (k[:, None] >= k[None, :]).astype(np.float32)
    e_last = np.zeros((128, 128), np.float32)
    e_last[127, :] = 1.0
    e_first = np.zeros((128, 128), np.float32)
    e_first[0, :] = 1.0
    return np.ascontiguousarray(np.concatenate([ident, tri_f, tri_b, e_last, e_first], axis=1))


def pack_inputs(r, x, c, ctx, c_ctx, mod_w, mod_b, ssd_w_in, ssd_conv_w, ssd_conv_b, ssd_dt_bias,
                ssd_a_log, ssd_d, ssd_norm_g, ssd_w_out, smlp_w_in, smlp_ln_g, smlp_ln_b,
                smlp_w_s, smlp_b_s, smlp_w_out, final_norm_g, shared):
    b, half = r // 2, r % 2
    flip = half == 1
    f32 = np.float32
    xs_ = x[b][::-1] if flip else x[b]
    cx_ = ctx[b][::-1] if flip else ctx[b]
    m = {}
    m["x_own"] = np.ascontiguousarray(xs_[:T], dtype=f32)
    m["x_oth"] = np.ascontiguousarray(xs_[T:], dtype=f32)
    m["x_ctx"] = np.ascontiguousarray(cx_, dtype=f32)
    cv = np.stack([c[b], c_ctx], axis=0)
    m["c2"] = np.ascontiguousarray(cv.reshape(2, KT, 128).transpose(2, 1, 0), dtype=f32)
    key = "flip" if flip else "noflip"
    if key not in shared:
        s = {}
        d_order = [1, 0] if flip else [0, 1]
        wdt = ssd_w_in[0][:, 10240:10368].reshape(D, 2, H)[:, d_order, :].reshape(D, 128)
        s["w_dt"] = np.ascontiguousarray(wdt.reshape(KT, 128, 128).transpose(1, 0, 2), dtype=f32)
        dtb = ssd_dt_bias[0][d_order].reshape(128)
        alg = ssd_a_log[0][d_order].reshape(128)
        s["dtp"] = np.ascontiguousarray(np.stack([dtb, alg], axis=1), dtype=f32)
        cw = ssd_conv_w[0][::-1] if flip else ssd_conv_w[0]
        cp = np.concatenate([cw.T, ssd_conv_b[0][:, None]], axis=1)
        s["convp"] = np.ascontiguousarray(cp.reshape(48, 128, 8).transpose(1, 0, 2), dtype=f32)
        ws = smlp_w_s[0]
        bs = smlp_b_s[0]
        if flip:
            ws = ws[:, ::-1, ::-1]
            bs = bs[:, ::-1]
        s["wsT"] = np.ascontiguousarray(ws.transpose(2, 0, 1), dtype=f32)
        s["bs"] = np.ascontiguousarray(bs.reshape(1, 16 * 128), dtype=f32)
        shared[key] = s
    m.update(shared[key])
    if "common" not in shared:
        s = {}
        s["mod_w"] = np.ascontiguousarray(mod_w.reshape(2, KT, 128, 48, 128).transpose(0, 3, 2, 1, 4), dtype=f32)
        mb = mod_b[:, :4096].reshape(2, 32, 128).transpose(2, 0, 1)
        s["modb_col"] = np.ascontiguousarray(mb, dtype=f32)
        s["modb_gate"] = np.ascontiguousarray(mod_b[:, 4096:].reshape(1, 2, D), dtype=f32)
        s["w_in0"] = np.ascontiguousarray(ssd_w_in[0].reshape(KT, 128, 81, 128).transpose(2, 1, 0, 3), dtype=f32)
        s["drep"] = np.ascontiguousarray(np.broadcast_to(np.repeat(ssd_d[0], 64)[None, :], (128, DI)), dtype=f32)
        s["ng"] = np.ascontiguousarray(ssd_norm_g[0].reshape(32, 128).T, dtype=f32)
        s["w_out0"] = np.ascontiguousarray(ssd_w_out[0].reshape(32, 128, 4, 512).transpose(2, 1, 0, 3), dtype=f32)
        s["w_in1"] = np.ascontiguousarray(smlp_w_in[0].reshape(KT, 128, 96, 128).transpose(2, 1, 0, 3), dtype=f32)
        s["lng"] = np.ascontiguousarray(smlp_ln_g[0].reshape(32, 128).T, dtype=f32)
        s["lnb"] = np.ascontiguousarray(smlp_ln_b[0].reshape(32, 128).T, dtype=f32)
        s["w_out1"] = np.ascontiguousarray(smlp_w_out[0].reshape(32, 128, 4, 512).transpose(2, 1, 0, 3), dtype=f32)
        s["fng"] = np.ascontiguousarray(final_norm_g.reshape(1, D), dtype=f32)
        s["consts"] = make_consts()
        shared["common"] = s
    m.update(shared["common"])
    return m


def kernel(**inputs):
    inputs = {k: np.asarray(v) for k, v in inputs.items()}
    shared = {}
    in_maps = [pack_inputs(r, shared=shared, **inputs) for r in range(8)]
    nc = build()
    res = run_bass_kernel_spmd(nc, in_maps, core_ids=list(range(8)))
    out = np.zeros((4, 2 * T, D), np.float32)
    for r in range(8):
        b, half = r // 2, r % 2
        o = np.asarray(res.results[r]["out"], dtype=np.float32)
        if half == 0:
            out[b, :T] = o
        else:
            out[b, T:] = o[::-1]
    return out
```

```python
import numpy as np
import ml_dtypes
import concourse.bass as bass
import concourse.mybir as mybir
from concourse.bass_utils import run_bass_kernel_spmd

F32 = mybir.dt.float32
BF16 = mybir.dt.bfloat16
AF = mybir.ActivationFunctionType
ALU = mybir.AluOpType
AX = mybir.AxisListType

D = 2048
KT = 16
T = 2048
NCH = 16
DI = 4096
H = 64
EPS = 1e-6
W_IN0 = 10368
OW = 2320
CTX0 = 2056


class Buf:
    __slots__ = ("name", "w", "r")

    def __init__(self, name):
        self.name = name
        self.w = None
        self.r = {}


class Eng:
    def __init__(self, name, h, sem, same_engine_sync=True):
        self.name = name
        self.h = h
        self.sem = sem
        self.cnt = 0
        self.waited = {}
        self.same = same_engine_sync


class Sched:
    def __init__(self, nc, sems):
        self.nc = nc
        it = iter(sems)
        self.pe = Eng("pe", nc.tensor, next(it), same_engine_sync=False)
        self.act = Eng("act", nc.scalar, next(it))
        self.dve = Eng("dve", nc.vector, next(it))
        self.pool = Eng("pool", nc.gpsimd, next(it))
        self.sp = Eng("sp", nc.sync, next(it))
        self.engs = [self.pe, self.act, self.dve, self.pool, self.sp]
        self.rings = {}
        for q in (self.sp, self.pool):
            self.rings[q.name] = {"sems": [next(it) for _ in range(12)], "vals": [0] * 12, "i": 0}
        self.n_ins = 0

    def _need(self, eng, deps):
        out = []
        best = {}
        for (sem, val, src) in deps:
            if src is eng and not eng.same:
                continue
            k = id(sem)
            if eng.waited.get(k, 0) >= val:
                continue
            if k not in best or best[k][1] < val:
                best[k] = (sem, val)
        for k, (sem, val) in best.items():
            eng.waited[k] = val
            out.append((sem, val))
        return out

    def _deps(self, reads, writes):
        deps = []
        for b in reads:
            if b.w is not None:
                deps.append(b.w)
        for b in writes:
            if b.w is not None:
                deps.append(b.w)
            deps.extend(b.r.values())
        return deps

    def _emit(self, eng, fn, waits):
        for (sem, val) in waits[1:]:
            eng.h.wait_ge(sem, val)
            self.n_ins += 1
        ins = fn()
        self.n_ins += 1
        if waits:
            ins._wait_ge(waits[0][0], waits[0][1])
        return ins

    def op(self, eng, fn, reads=(), writes=(), inc=True):
        waits = self._need(eng, self._deps(reads, writes))
        ins = self._emit(eng, fn, waits)
        if inc:
            eng.cnt += 1
            ins.then_inc(eng.sem, 1)
            ev = (eng.sem, eng.cnt, eng)
        else:
            ev = (eng.sem, eng.cnt + 1, eng)
        for b in reads:
            b.r[id(eng.sem)] = ev
        for b in writes:
            b.w = ev
            b.r = {}
        return ins

    def dma(self, q, out, in_, reads=(), writes=()):
        ring = self.rings[q.name]
        i = ring["i"]
        ring["i"] = (i + 1) % len(ring["sems"])
        sem = ring["sems"][i]
        deps = self._deps(reads, writes)
        if ring["vals"][i] > 0:
            deps.append((sem, ring["vals"][i], None))
        waits = self._need(q, deps)
        ins = self._emit(q, lambda: q.h.dma_start(out=out, in_=in_), waits)
        ring["vals"][i] += 16
        ins.then_inc(sem, 16)
        ev = (sem, ring["vals"][i], None)
        for b in reads:
            b.r[id(sem)] = ev
        for b in writes:
            b.w = ev
            b.r = {}
        return ins

    def barrier(self):
        evs = []
        for e in self.engs:
            if e.cnt > 0:
                evs.append((e.sem, e.cnt, None))
        for r in self.rings.values():
            for s, v in zip(r["sems"], r["vals"]):
                if v > 0:
                    evs.append((s, v, None))
        for e in self.engs:
            for (sem, val) in self._need(e, evs):
                e.h.wait_ge(sem, val)
                self.n_ins += 1


def build(stage=99, dbg=False):
    nc = bass.Bass("TRN2", target_bir_lowering=False)
    from contextlib import ExitStack
    es = ExitStack()

    def din(name, shape, dt=F32):
        return nc.dram_tensor(name, list(shape), dt, kind="ExternalInput").ap()

    def dscr(name, shape, dt):
        return nc.dram_tensor(name, list(shape), dt, kind="Internal").ap()

    x_own = din("x_own", [T, D])
    x_oth = din("x_oth", [T, D])
    x_ctx = din("x_ctx", [256, D])
    c2_d = din("c2", [128, KT, 2])
    mod_w = din("mod_w", [2, 48, 128, KT, 128])
    modb_col_d = din("modb_col", [128, 2, 32])
    modb_gate_d = din("modb_gate", [1, 2, D])
    w_in0 = din("w_in0", [81, 128, KT, 128])
    w_dt = din("w_dt", [128, KT, 128])
    dtp_d = din("dtp", [128, 2])
    convp_d = din("convp", [128, 48, 8])
    drep_d = din("drep", [128, DI])
    ng_d = din("ng", [128, 32])
    w_out0 = din("w_out0", [4, 128, 32, 512])
    w_in1 = din("w_in1", [96, 128, KT, 128])
    lng_d = din("lng", [128, 32])
    lnb_d = din("lnb", [128, 32])
    wsT_d = din("wsT", [128, 16, 128])
    bs_d = din("bs", [1, 16 * 128])
    w_out1 = din("w_out1", [4, 128, 32, 512])
    fng_d = din("fng", [1, D])
    consts_d = din("consts", [128, 640])
    out_d = nc.dram_tensor("out", [T, D], F32, kind="ExternalOutput").ap()
    dbg_d = nc.dram_tensor("dbg", [128, 4096], F32, kind="ExternalOutput").ap() if dbg else None

    tok_own = dscr("tok_own", [T, 8, 640], BF16)
    tok_oth = dscr("tok_oth", [OW, 8, 640], BF16)
    featBC = dscr("featBC", [8, 2, 128, T], BF16)
    zT_s = dscr("zT_s", [DI, T], BF16)
    rscr = dscr("rscr", [NCH, 3, 128, 128], BF16)
    sbin = dscr("sbin", [NCH, 128, 8, 512], BF16)
    ygT_s = dscr("ygT_s", [DI, T], BF16)
    x1_s = dscr("x1_s", [T, D], F32)
    gv_s = dscr("gv_s", [T, DI], BF16)
    sT_s = dscr("sT_s", [DI, T], BF16)

    sems = [es.enter_context(nc.semaphore(f"s{i}")) for i in range(5 + 24)]
    S = Sched(nc, sems)
    PE, ACT, DVE, POOL, SP = S.pe, S.act, S.dve, S.pool, S.sp

    uniq = {"n": 0}

    def sb(name, shape, dt, stack=None):
        uniq["n"] += 1
        t = (stack or es).enter_context(nc.sbuf_tensor(f"sb{uniq['n']}_{name}", list(shape), dt))
        return t

    pf = [es.enter_context(nc.psum_tensor(f"pf{i}", [128, 512], F32)) for i in range(6)]
    pb = [es.enter_context(nc.psum_tensor(f"pb{i}", [128, 1024], BF16)) for i in range(2)]
    pfB = [Buf(f"pf{i}") for i in range(6)]
    pbB = [Buf(f"pb{i}") for i in range(2)]

    consts = sb("consts", [128, 640], F32)
    cB = Buf("consts")
    ident_f = consts[:, 0:128]
    tri_f = consts[:, 128:256]
    tri_b = consts[:, 256:384]
    e_last = consts[:, 384:512]
    e_first = consts[:, 512:640]
    ident_b = sb("ident_b", [128, 128], BF16)
    ones_b = sb("ones_b", [128, 128], BF16)
    ones_f = sb("ones_f", [128, 128], F32)
    modcol = sb("modcol", [128, 2, 32, 2], F32)
    mcB = Buf("modcol")
    gate_rep = sb("gate_rep", [128, 2, D], F32)
    grB = Buf("gate_rep")
    fng_rep = sb("fng_rep", [128, D], F32)
    rstd_y = sb("rstd_y", [128, NCH], F32)
    ryB = Buf("rstd_y")
    small = sb("small", [128, 64], F32)
    smB = Buf("small")

    S.dma(SP, consts[:], consts_d[:, :], writes=[cB])
    S.op(DVE, lambda: nc.vector.tensor_copy(ident_b[:], ident_f), reads=[cB], writes=[cB])
    S.op(DVE, lambda: nc.vector.memset(ones_b[:], 1.0), writes=[cB])
    S.op(DVE, lambda: nc.vector.memset(ones_f[:], 1.0), writes=[cB])

    rr = {"ev": 0}

    def evac_copy(out, in_, reads, writes, eng=None):
        rr["ev"] += 1
        if eng == "act" or (eng is None and rr["ev"] % 2 == 0):
            S.op(ACT, lambda: nc.scalar.copy(out, in_), reads=reads, writes=writes)
        else:
            S.op(DVE, lambda: nc.vector.tensor_copy(out, in_), reads=reads, writes=writes)

    NW = 5
    es_wt = ExitStack()
    wt = [sb(f"wt{i}", [128, KT, 128], BF16, es_wt) for i in range(NW)]
    wtB = [Buf(f"wt{i}") for i in range(NW)]
    wstate = {"i": 0}

    def load_w(src_ap):
        i = wstate["i"]
        wstate["i"] = (i + 1) % NW
        S.dma(POOL, wt[i][:], src_ap, writes=[wtB[i]])
        return wt[i], wtB[i]

    with ExitStack() as ph:
        c2 = sb("c2", [128, KT, 2], F32, ph)
        cs = sb("cs", [128, KT, 2], BF16, ph)
        modb_col = sb("modb_col", [128, 2, 32], F32, ph)
        grow = sb("grow", [1, 2, D], F32, ph)
        mbg = sb("mbg", [1, 2, D], F32, ph)
        fng = sb("fng", [1, D], F32, ph)
        aB = Buf("phA")
        growB = Buf("grow")
        S.dma(SP, c2[:], c2_d[:, :, :], writes=[aB])
        S.dma(SP, modb_col[:], modb_col_d[:, :, :], writes=[aB])
        S.dma(SP, mbg[:], modb_gate_d[:, :, :], writes=[aB])
        S.dma(SP, fng[:], fng_d[:, :], writes=[aB])
        csB = Buf("cs")
        S.op(ACT, lambda: nc.scalar.activation(cs[:], c2[:], AF.Silu), reads=[aB], writes=[csB])
        pend = None
        for layer in range(2):
            tiles = list(range(48))
            if layer == 0:
                nxt = load_w(mod_w[layer, 0])
            for nt in tiles:
                wtile, wB = nxt
                if nt + 1 < 48:
                    nxt = load_w(mod_w[layer, nt + 1])
                elif layer == 0:
                    nxt = load_w(mod_w[1, 0])
                bank = nt % 2
                if nt < 32:
                    for kt in range(KT):
                        S.op(PE, lambda kt=kt: nc.tensor.matmul(pf[bank][:, 0:2], wtile[:, kt, :], cs[:, kt, :],
                                                                  start=(kt == 0), stop=(kt == KT - 1)),
                             reads=[wB, csB], writes=[pfB[bank]], inc=(kt == KT - 1))
                    S.op(DVE, lambda: nc.vector.tensor_scalar(modcol[:, layer, nt, :], pf[bank][:, 0:2],
                                                              modb_col[:, layer, nt:nt + 1],
                                                              1.0 if nt >= 16 else 0.0, ALU.add, ALU.add),
                         reads=[pfB[bank], aB], writes=[mcB])
                else:
                    for kt in range(KT):
                        S.op(PE, lambda kt=kt: nc.tensor.matmul(pf[bank][0:2, 0:128], cs[:, kt, :], wtile[:, kt, :],
                                                                  start=(kt == 0), stop=(kt == KT - 1)),
                             reads=[wB, csB], writes=[pfB[bank]], inc=(kt == KT - 1))
                    c0 = (nt - 32) * 128
                    S.op(DVE, lambda: nc.vector.tensor_tensor(grow[0:1, layer, c0:c0 + 128], pf[bank][0:1, 0:128],
                                                              mbg[0:1, layer, c0:c0 + 128], ALU.add),
                         reads=[pfB[bank], aB], writes=[growB])
        for layer in range(2):
            for blk in range(4):
                bank = blk % 2
                S.op(PE, lambda: nc.tensor.matmul(pf[bank][:, :], ones_f[0:1, :], grow[0:1, layer, blk * 512:(blk + 1) * 512],
                                                  start=True, stop=True),
                     reads=[growB, cB], writes=[pfB[bank]])
                evac_copy(gate_rep[:, layer, blk * 512:(blk + 1) * 512], pf[bank][:, :], [pfB[bank]], [grB])
        for blk in range(4):
            bank = blk % 2
            S.op(PE, lambda: nc.tensor.matmul(pf[bank][:, :], ones_f[0:1, :], fng[0:1, blk * 512:(blk + 1) * 512],
                                              start=True, stop=True),
                 reads=[aB, cB], writes=[pfB[bank]])
            evac_copy(fng_rep[:, blk * 512:(blk + 1) * 512], pf[bank][:, :], [pfB[bank]], [grB])
        if dbg and stage == 1:
            S.barrier()
            S.dma(SP, dbg_d[:, 0:128], modcol[:].rearrange("p a b c -> p (a b c)"), reads=[mcB])
            S.dma(SP, dbg_d[:, 128:128 + 2048], gate_rep[:, 0, :], reads=[grB])
            S.dma(SP, dbg_d[:, 2176:2176 + 1920], gate_rep[:, 1, 0:1920], reads=[grB])
        S.barrier()
    if stage == 1:
        return finish(nc, S, es, out_d)

    def token_prep(ph, hlt, hltB, jobs, layer):
        xt = [sb(f"xt{i}", [128, D], F32, ph) for i in range(2)]
        xtB = [Buf(f"xt{i}") for i in range(2)]
        xn = [sb(f"xn{i}", [128, D], BF16, ph) for i in range(2)]
        xnB = [Buf(f"xn{i}") for i in range(2)]
        junk = sb("junk", [128, D], BF16, ph)
        jB = Buf("junk")
        st = sb("st", [128, 4, 2], F32, ph)
        stB = [Buf("st0"), Buf("st1")]
        nj = len(jobs)

        def load(n):
            S.dma(SP, xt[n % 2][:], jobs[n][0], writes=[xtB[n % 2]])

        def stage1(n):
            i = n % 2
            S.op(ACT, lambda: nc.scalar.activation(junk[:], xt[i][:], AF.Square, accum_out=st[:, i, 0:1]),
                 reads=[xtB[i]], writes=[jB, stB[i]])
            S.op(ACT, lambda: nc.scalar.activation(st[:, i, 1:2], st[:, i, 0:1], AF.Sqrt, bias=EPS, scale=1.0 / D),
                 reads=[stB[i]], writes=[stB[i]])
            S.op(DVE, lambda: nc.vector.reciprocal(st[:, i, 1:2], st[:, i, 1:2]),
                 reads=[stB[i]], writes=[stB[i]])
            S.op(ACT, lambda: nc.scalar.activation(xn[i][:], xt[i][:], AF.Copy, scale=st[:, i, 1:2]),
                 reads=[xtB[i], stB[i]], writes=[xnB[i]])

        def stage2(n):
            src, modj, dcol, t0, ntok = jobs[n]
            i = n % 2
            for half in range(2):
                for k8 in range(8):
                    kt = half * 8 + k8
                    S.op(PE, lambda: nc.tensor.transpose(pb[half][:, k8 * 128:(k8 + 1) * 128],
                                                         xn[i][:, kt * 128:(kt + 1) * 128], ident_b[:]),
                         reads=[xnB[i], cB], writes=[pbB[half]], inc=(k8 == 7))
                for k8 in range(8):
                    kt = half * 8 + k8
                    src_ps = pb[half][:, k8 * 128 + t0:k8 * 128 + t0 + ntok]
                    dst = hlt[:, kt, dcol:dcol + ntok]
                    sc = modcol[:, layer, 16 + kt, modj:modj + 1]
                    sh = modcol[:, layer, kt, modj:modj + 1]
                    S.op(DVE, lambda: nc.vector.tensor_scalar(dst, src_ps, sc, sh, ALU.mult, ALU.add),
                         reads=[pbB[half], mcB], writes=[hltB[kt]])

        if nj:
            load(0)
            if nj > 1:
                load(1)
            stage1(0)
        for n in range(nj):
            if n + 1 < nj:
                stage1(n + 1)
            if n + 2 < nj:
                load(n + 2)
            stage2(n)

    def inproj(wsrc, hlt, hltB, blocks, evac, nxt_src=None, pre=None):
        wtile, wB = pre if pre is not None else load_w(wsrc)
        nxt = load_w(nxt_src) if nxt_src is not None else None
        for bi, (c0, n) in enumerate(blocks):
            bank = bi % 4
            for kt in range(KT):
                S.op(PE, lambda kt=kt: nc.tensor.matmul(pf[bank][:, 0:n], wtile[:, kt, :], hlt[:, kt, c0:c0 + n],
                                                          start=(kt == 0), stop=(kt == KT - 1)),
                     reads=[wB, hltB[kt]], writes=[pfB[bank]], inc=(kt == KT - 1))
            evac(pf[bank][:, 0:n], pfB[bank], c0, n)
        return nxt

    es_dt = ExitStack()
    dtraw_own = sb("dtraw_own", [128, T], F32, es_dt)
    dtraw_oth = sb("dtraw_oth", [128, OW], F32, es_dt)
    dtB = Buf("dtraw")
    with ExitStack() as ph:
        hlt = sb("hlt", [128, KT, OW], BF16, ph)
        hltB = [Buf(f"hlt{k}") for k in range(KT)]
        convp = sb("convp", [128, 48, 8], F32, ph)
        cvB = Buf("convp")
        S.dma(SP, convp[:], convp_d[:, :, :], writes=[cvB])
        tcnt = {"n": 0}

        for pas in ("oth", "own"):
            with ExitStack() as ph2:
                if pas == "oth":
                    jobs = [(x_own[1920:2048, :], 0, 0, 125, 3)]
                    jobs += [(x_oth[i * 128:(i + 1) * 128, :], 0, 3 + i * 128, 0, 128) for i in range(16)]
                    jobs += [(x_ctx[i * 128:(i + 1) * 128, :], 1, CTX0 + i * 128, 0, 128) for i in range(2)]
                    blocks = [(0, 512), (512, 512), (1024, 512), (1536, 512), (2048, 3), (CTX0, 256)]
                    shift = 0
                    chunks = [("oth", c, 3 + c * 128) for c in range(16)] + [("ctx", c, CTX0 + c * 128) for c in range(2)]
                    conv_lo, conv_n = 3, 2312 - 3
                else:
                    jobs = [(x_own[i * 128:(i + 1) * 128, :], 0, i * 128, 0, 128) for i in range(16)]
                    jobs += [(x_oth[0:128, :], 0, 2048, 0, 3)]
                    blocks = [(0, 512), (512, 512), (1024, 512), (1536, 512), (2048, 3)]
                    shift = 3
                    chunks = [("own", c, c * 128) for c in range(16)]
                    conv_lo, conv_n = 3, 2048
                token_prep(ph2, hlt, hltB, jobs, 0)
                S.barrier()
            ph3 = ExitStack()
            pre_t = [sb(f"pre{i}", [128, OW], F32, ph3) for i in range(2)]
            preB = [Buf("pre0"), Buf("pre1")]
            acc = sb("acc", [128, OW], F32, ph3)
            accB = Buf("acc")
            acc2 = sb("acc2", [128, OW], F32, ph3)
            acc2B = Buf("acc2")
            ft = [sb(f"ft{i}", [128, OW], BF16, ph3) for i in range(2)]
            ftB = [Buf("ft0"), Buf("ft1")]
            tokg = sb("tokg", [128, 18, 640], BF16, ph3)
            tokgB = Buf("tokg")
            for i in range(2):
                S.op(DVE, lambda: nc.vector.memset(pre_t[i][:], 0.0), writes=[preB[i]])

            def cols_of(kind, g, j=0):
                if kind == "z":
                    return g * 512 + j * 128
                if kind == "xs":
                    return 4096 + g * 512 + j * 128
                if kind == "B":
                    return 8192 + g * 128
                if kind == "C":
                    return 9216 + g * 128
            tl = [("dt", 0, 0)]
            for g in range(8):
                tl.append(("B", g, 0))
                if pas == "own":
                    tl.append(("C", g, 0))
                for j in range(4):
                    tl.append(("xs", g, j))
                if pas == "own":
                    for j in range(4):
                        tl.append(("z", g, j))

            def src_of(t):
                kind, g, j = t
                if kind == "dt":
                    return w_dt[:, :, :]
                c0 = cols_of(kind, g, j)
                return w_in0[c0 // 128]

            nxt = load_w(src_of(tl[0]))
            pend = {"silu": None, "T": None}

            def flush():
                if pend["T"] is not None:
                    f_ = pend["T"]
                    pend["T"] = None
                    f_()

            def flush_silu():
                if pend["silu"] is not None:
                    f_, g_ = pend["silu"]
                    pend["silu"] = None
                    f_()
                    assert pend["T"] is None
                    pend["T"] = g_
            for ti, t in enumerate(tl):
                kind, g, j = t
                nsrc = src_of(tl[ti + 1]) if ti + 1 < len(tl) else None
                if kind == "dt":
                    dst = dtraw_oth if pas == "oth" else dtraw_own

                    def ev(ps, bB, c0, n, dst=dst):
                        if pas == "own" and c0 >= 2048:
                            return
                        evac_copy(dst[:, c0:c0 + n], ps, [bB], [dtB], eng="act")
                    nxt = inproj(None, hlt, hltB, blocks, ev, nsrc, pre=nxt)
                    flush()
                    flush_silu()
                    continue
                if kind == "z":
                    zi = tcnt["n"] % 2
                    tcnt["n"] += 1

                    def ev(ps, bB, c0, n, zi=zi):
                        if c0 >= 2048:
                            return
                        S.op(ACT, lambda: nc.scalar.activation(ft[zi][:, c0:c0 + n], ps, AF.Silu),
                             reads=[bB], writes=[ftB[zi]])
                    flush()
                    nxt = inproj(None, hlt, hltB, blocks[:4], ev, nsrc, pre=nxt)
                    flush_silu()
                    r0 = (g * 4 + j) * 128
                    S.dma(SP, zT_s[r0:r0 + 128, :], ft[zi][:, 0:T], reads=[ftB[zi]])
                    continue
                pi = tcnt["n"] % 2
                tcnt["n"] += 1
                cidx = {"xs": 0, "B": 32, "C": 40}[kind] + (g * 4 + j if kind == "xs" else g)

                def ev(ps, bB, c0, n, pi=pi):
                    evac_copy(pre_t[pi][:, c0 + shift:c0 + shift + n], ps, [bB], [preB[pi]], eng="act")
                nxt = inproj(None, hlt, hltB, blocks, ev, nsrc, pre=nxt)
                lo, n = conv_lo, conv_n
                def tap(k):
                    return pre_t[pi][:, lo - 3 + k:lo - 3 + k + n]
                accs = (acc, acc2)[pi]
                accsB = (accB, acc2B)[pi]
                flush()
                S.op(POOL, lambda: nc.gpsimd.tensor_tensor(accs[:, lo:lo + n], tap(0),
                                                           convp[:, cidx, 0:1].to_broadcast([128, n]), ALU.mult),
                     reads=[preB[pi], cvB], writes=[accsB])
                for k in range(1, 7):
                    S.op(DVE, lambda k=k: nc.vector.scalar_tensor_tensor(accs[:, lo:lo + n], tap(k), convp[:, cidx, k:k + 1],
                                                                         accs[:, lo:lo + n], ALU.mult, ALU.add),
                         reads=[preB[pi], cvB, accsB], writes=[accsB])
                fo = 0 if pas == "own" else lo

                def post_silu(pi=pi, accs=accs, accsB=accsB, cidx=cidx, fo=fo, lo=lo, n=n):
                    S.op(ACT, lambda: nc.scalar.activation(ft[pi][:, fo:fo + n], accs[:, lo:lo + n], AF.Silu,
                                                           bias=convp[:, cidx, 7:8]),
                         reads=[accsB, cvB], writes=[ftB[pi]])

                def post(kind=kind, g=g, j=j, pi=pi):
                    if kind in ("B", "C") and pas == "own":
                        S.dma(SP, featBC[g, 0 if kind == "B" else 1, :, :], ft[pi][:, 0:T], reads=[ftB[pi]])
                    if kind == "C":
                        return
                    dcol = 512 if kind == "B" else j * 128
                    for c8 in range(0, len(chunks), 8):
                        grp = chunks[c8:c8 + 8]
                        bank = (c8 // 8) % 2
                        for ci, (_, _, col0) in enumerate(grp):
                            S.op(PE, lambda: nc.tensor.transpose(pb[bank][:, ci * 128:(ci + 1) * 128],
                                                                 ft[pi][:, col0:col0 + 128], ident_b[:]),
                                 reads=[ftB[pi], cB], writes=[pbB[bank]], inc=(ci == len(grp) - 1))
                        ng = len(grp)
                        evac_copy(tokg[:, c8:c8 + ng, dcol:dcol + 128],
                                  pb[bank][:, 0:ng * 128].rearrange("p (c q) -> p c q", q=128),
                                  [pbB[bank]], [tokgB], eng="act")
                    if kind == "xs" and j == 3:
                        if pas == "own":
                            S.dma(SP, tok_own[:, g, :].rearrange("(c p) e -> p c e", p=128), tokg[:, 0:16, :], reads=[tokgB])
                        else:
                            S.dma(SP, tok_oth[3:3 + 2048, g, :].rearrange("(c p) e -> p c e", p=128), tokg[:, 0:16, :],
                                  reads=[tokgB])
                            S.dma(SP, tok_oth[CTX0:CTX0 + 256, g, :].rearrange("(c p) e -> p c e", p=128), tokg[:, 16:18, :],
                                  reads=[tokgB])
                flush_silu()
                pend["silu"] = (post_silu, post)
            flush()
            flush_silu()
            flush()
            S.barrier()
            ph3.close()
        if dbg and stage == 2:
            S.dma(POOL, dbg_d[:, 0:640], tok_own[0:128, 0, :])
            S.dma(POOL, dbg_d[:, 640:1280], tok_own[1920:2048, 7, :])
            S.dma(POOL, dbg_d[:, 1280:1920], tok_oth[3:131, 0, :])
            S.dma(POOL, dbg_d[:, 1920:2560], tok_oth[CTX0 + 128:CTX0 + 256, 3, :])
            S.dma(POOL, dbg_d[:, 2560:2688], featBC[2, 1, :, 0:128])
            S.dma(POOL, dbg_d[:, 2688:2816], zT_s[5 * 128:6 * 128, 128:256])
            S.dma(SP, dbg_d[:, 2816:3328], dtraw_own[:, 0:512], reads=[dtB])
            S.dma(SP, dbg_d[:, 3328:3840], dtraw_oth[:, 1808:2320], reads=[dtB])
            S.barrier()
    if stage == 2:
        return finish(nc, S, es, out_d)

    with ExitStack() as ph:
        dtp = sb("dtp", [128, 2], F32, ph)
        acol = sb("acol", [128, 1], F32, ph)
        drep = sb("drep", [128, DI], BF16, ph)
        ng = sb("ng", [128, 32], F32, ph)
        ones3 = sb("ones3", [3, 128], BF16, ph)
        pB = Buf("ssdparams")
        S.dma(SP, dtp[:], dtp_d[:, :], writes=[pB])
        S.dma(SP, ng[:], ng_d[:, :], writes=[pB])
        S.dma(POOL, drep[:], drep_d[:, :], writes=[pB])
        S.op(DVE, lambda: nc.vector.memset(ones3[:], 1.0), writes=[pB])
        S.op(ACT, lambda: nc.scalar.activation(acol[:], dtp[:, 1:2], AF.Exp), reads=[pB], writes=[pB])
        S.op(DVE, lambda: nc.vector.tensor_scalar(acol[:], acol[:], -1.0, None, ALU.mult), reads=[pB], writes=[pB])
        S_f = sb("S_f", [128, 8, 512], F32, ph)
        S_b = sb("S_b", [128, 8, 512], F32, ph)
        Sbf_f = sb("Sbf_f", [128, 8, 512], BF16, ph)
        SfB = [Buf(f"S_f{g}") for g in range(8)]
        SbB = [Buf(f"S_b{g}") for g in range(8)]
        SbfB = [Buf(f"Sbf_f{g}") for g in range(8)]
        S.op(DVE, lambda: nc.vector.memset(S_f[:], 0.0), writes=SfB)
        S.op(DVE, lambda: nc.vector.memset(S_b[:], 0.0), writes=SbB)
        S.op(DVE, lambda: nc.vector.memset(Sbf_f[:], 0.0), writes=SbfB)

        scs = []
        for i in range(2):
            scs.append(dict(
                at_lt=sb(f"at_lt{i}", [128, 256], F32, ph), ac=sb(f"ac{i}", [128, 128], F32, ph),
                acT=sb(f"acT{i}", [128, 128], F32, ph), cdb=sb(f"cdb{i}", [128, 128], F32, ph),
                wtk=sb(f"wtk{i}", [128, 128], F32, ph), biasL=sb(f"biasL{i}", [128, 128], F32, ph),
                dec=sb(f"dec{i}", [128, 128], F32, ph), r3=sb(f"r3{i}", [128, 3, 128], BF16, ph),
                tmpa=sb(f"tmpa{i}", [128, 128], F32, ph), B=Buf(f"sc{i}")))
        p0aB = Buf("pf0a")
        gramB = pbB[1]
        gram_ps = pb[1][:, 0:256].bitcast(F32)
        rscrB = [Buf(f"rscr{c}") for c in range(NCH)]

        def chunk_scalar_steps(par, nm, col0, rchunk=None, full=True):
            sc = scs[par]
            B_ = sc["B"]
            a_src, l_src = aT[nm], ldT[nm]
            at = sc["at_lt"]

            def s0():
                S.op(PE, lambda: nc.tensor.transpose(pf[0][:, 0:128], a_src[:, col0:col0 + 128], ident_f),
                     reads=[dt2B, cB], writes=[p0aB], inc=False)
                S.op(PE, lambda: nc.tensor.transpose(pf[0][:, 128:256], l_src[:, col0:col0 + 128], ident_f),
                     reads=[dt2B, cB], writes=[p0aB])

            def s1():
                S.op(DVE, lambda: nc.vector.tensor_copy(sc["at_lt"][:, :], pf[0][:, 0:256]), reads=[p0aB], writes=[B_])

            def s2():
                S.op(PE, lambda: nc.tensor.matmul(pf[1][:, 0:64], tri_f, at[:, 0:64], start=True, stop=True),
                     reads=[B_, cB], writes=[pfB[1]], inc=False)
                S.op(PE, lambda: nc.tensor.matmul(pf[1][:, 64:128], tri_b, at[:, 64:128], start=True, stop=True),
                     reads=[B_, cB], writes=[pfB[1]], inc=False)
                S.op(PE, lambda: nc.tensor.matmul(pf[1][:, 128:256], at[:, 0:128], tri_f, start=True, stop=True),
                     reads=[B_, cB], writes=[pfB[1]], inc=False)
                S.op(PE, lambda: nc.tensor.matmul(pf[1][:, 256:384], at[:, 0:128], tri_b, start=True, stop=True),
                     reads=[B_, cB], writes=[pfB[1]])

            def s3():
                S.op(DVE, lambda: nc.vector.tensor_copy(sc["ac"][:, :], pf[1][:, 0:128]), reads=[pfB[1]], writes=[B_])
                if full:
                    S.op(DVE, lambda: nc.vector.tensor_copy(sc["acT"][0:64, :], pf[1][0:64, 128:256]), reads=[pfB[1]], writes=[B_])
                    S.op(DVE, lambda: nc.vector.tensor_copy(sc["acT"][64:128, :], pf[1][64:128, 256:384]), reads=[pfB[1]], writes=[B_])

            def s4():
                S.op(PE, lambda: nc.tensor.matmul(pf[1][:, 384:448], e_last, sc["ac"][:, 0:64], start=True, stop=True),
                     reads=[B_, cB], writes=[pfB[1]], inc=False)
                S.op(PE, lambda: nc.tensor.matmul(pf[1][:, 448:512], e_first, sc["ac"][:, 64:128], start=True, stop=True),
                     reads=[B_, cB], writes=[pfB[1]])

            def s5():
                S.op(ACT, lambda: nc.scalar.activation(sc["cdb"][:, :], pf[1][:, 384:512], AF.Exp), reads=[pfB[1]], writes=[B_])
                S.op(DVE, lambda: nc.vector.tensor_tensor(sc["tmpa"][:, :], pf[1][:, 384:512], sc["ac"][:, :], ALU.subtract),
                     reads=[pfB[1], B_], writes=[B_])
                S.op(DVE, lambda: nc.vector.tensor_tensor(sc["tmpa"][:, :], sc["tmpa"][:, :], at[:, 128:256], ALU.add),
                     reads=[B_], writes=[B_])

            def s6():
                S.op(ACT, lambda: nc.scalar.activation(sc["wtk"][:, :], sc["tmpa"][:, :], AF.Exp), reads=[B_], writes=[B_])
                if full:
                    S.op(DVE, lambda: nc.vector.tensor_tensor(sc["biasL"][:, :], at[:, 128:256], sc["ac"][:, :], ALU.subtract),
                         reads=[B_], writes=[B_])
                    S.op(ACT, lambda: nc.scalar.activation(sc["dec"][:, :], sc["ac"][:, :], AF.Exp), reads=[B_], writes=[B_])
                if rchunk is not None:
                    r3 = sc["r3"]
                    S.op(DVE, lambda: nc.vector.tensor_copy(r3[:, 0, :], sc["acT"][:, :]), reads=[B_], writes=[B_])
                    S.op(DVE, lambda: nc.vector.tensor_tensor(sc["tmpa"][:, :], sc["acT"][:, :], r3[:, 0, :], ALU.subtract),
                         reads=[B_], writes=[B_])
                    S.op(DVE, lambda: nc.vector.tensor_copy(r3[:, 1, :], sc["tmpa"][:, :]), reads=[B_], writes=[B_])
                    S.op(DVE, lambda: nc.vector.tensor_tensor(sc["tmpa"][:, :], sc["tmpa"][:, :], r3[:, 1, :], ALU.subtract),
                         reads=[B_], writes=[B_])
                    S.op(DVE, lambda: nc.vector.tensor_copy(r3[:, 2, :], sc["tmpa"][:, :]), reads=[B_], writes=[B_])
                    S.dma(SP, rscr[rchunk].rearrange("j p q -> p j q"), r3[:, :, :], reads=[B_], writes=[rscrB[rchunk]])
            return [s0, s1, s2, s3, s4, s5, s6]

        def chunk_scalars(par, nm, col0, rchunk=None, full=True):
            for f_ in chunk_scalar_steps(par, nm, col0, rchunk, full):
                f_()

        def bc8(tile_ap, c0):
            return tile_ap[:, c0:c0 + 8].unsqueeze(2).to_broadcast([128, 8, 64])

        def v3(ap2d):
            return ap2d.rearrange("p (a b) -> p a b", b=64)

        tk = [sb(f"tk{i}", [128, 640], BF16, ph) for i in range(2)]
        tkB = [Buf("tk0"), Buf("tk1")]
        xsw = [sb(f"xsw{i}", [128, 512], BF16, ph) for i in range(2)]
        xswB = [Buf("xsw0"), Buf("xsw1")]

        def state_prep(par, d, g, tkt, tkb, Sd, SdB, k, swap=False):
            sc = scs[par]
            S.op(POOL, lambda: nc.gpsimd.tensor_tensor(v3(xsw[k][:, :]), v3(tkt[:, 0:512]), bc8(sc["wtk"], d * 64 + 8 * g), ALU.mult),
                 reads=[tkb, sc["B"]], writes=[xswB[k]])
            if swap:
                S.op(DVE, lambda: nc.vector.tensor_tensor(v3(Sd[:, g, :]), v3(Sd[:, g, :]), bc8(sc["cdb"], d * 64 + 8 * g), ALU.mult),
                     reads=[sc["B"], SdB[g]], writes=[SdB[g]])
            else:
                S.op(POOL, lambda: nc.gpsimd.tensor_tensor(v3(Sd[:, g, :]), v3(Sd[:, g, :]), bc8(sc["cdb"], d * 64 + 8 * g), ALU.mult),
                     reads=[sc["B"], SdB[g]], writes=[SdB[g]])

        def state_fin(g, tkt, tkb, Sd, SdB, k, bank):
            S.op(PE, lambda: nc.tensor.matmul(pf[bank][:, :], tkt[:, 512:640], xsw[k][:, :], start=True, stop=True),
                 reads=[tkb, xswB[k]], writes=[pfB[bank]])
            S.op(DVE, lambda: nc.vector.tensor_tensor(Sd[:, g, :], Sd[:, g, :], pf[bank][:, :], ALU.add),
                 reads=[pfB[bank], SdB[g]], writes=[SdB[g]])

        def state_update(par, d, g, tkt, tkb, Sd, SdB, k):
            state_prep(par, d, g, tkt, tkb, Sd, SdB, k, swap=True)
            state_fin(g, tkt, tkb, Sd, SdB, k, 5 - k)

        aT = {"own": dtraw_own, "oth": dtraw_oth}
        ph_s1 = ExitStack()
        ldT = {"own": sb("ldT_own", [128, T], F32, ph), "oth": sb("ldT_oth", [128, OW], F32, ph_s1)}
        dt2B = Buf("dt2")
        for nm, raw, wdt in (("own", dtraw_own, T), ("oth", dtraw_oth, OW)):
            S.op(ACT, lambda: nc.scalar.activation(raw[:, 0:wdt], raw[:, 0:wdt], AF.Exp, bias=dtp[:, 0:1]),
                 reads=[dtB, pB], writes=[dtB])
            S.op(ACT, lambda: nc.scalar.activation(raw[:, 0:wdt], raw[:, 0:wdt], AF.Ln, bias=1.0),
                 reads=[dtB], writes=[dtB])
            S.op(ACT, lambda: nc.scalar.activation(ldT[nm][:, 0:wdt], raw[:, 0:wdt], AF.Ln),
                 reads=[dtB], writes=[dt2B])
            S.op(DVE, lambda: nc.vector.tensor_scalar(raw[:, 0:wdt], raw[:, 0:wdt], acol[:, 0:1], None, ALU.mult),
                 reads=[dtB, pB, dt2B], writes=[dt2B, dtB])

        sbsave = sb("sbsave", [128, 8, 512], BF16, ph_s1)
        sbsB = Buf("sbsave")
        visits = []
        visits += [("oth", CTX0 + c * 128, tok_oth, CTX0 + c * 128, 0, None) for c in (0, 1)]
        visits += [("oth", CTX0 + c * 128, tok_oth, CTX0 + c * 128, 1, None) for c in (1, 0)]
        visits += [("oth", 3 + c * 128, tok_oth, 3 + c * 128, 1, None) for c in range(15, -1, -1)]
        visits += [("own", c * 128, tok_own, c * 128, 1, c) for c in range(15, -1, -1)]
        items = [(vi, g) for vi in range(len(visits)) for g in range(8)]

        def s1_load(n):
            vi, g = items[n]
            nm, col0, tdr, row0, d, save = visits[vi]
            S.dma(SP, tk[n % 2][:, :], tdr[row0:row0 + 128, g, :], writes=[tkB[n % 2]])
        s1_load(0)
        for n, (vi, g) in enumerate(items):
            nm, col0, tdr, row0, d, save = visits[vi]
            par = vi % 2
            if n + 1 < len(items):
                s1_load(n + 1)
            if g == 0:
                if vi == 0:
                    chunk_scalars(par, nm, col0, full=False)
                nsteps = chunk_scalar_steps((vi + 1) % 2, visits[vi + 1][0], visits[vi + 1][1], full=False) \
                    if vi + 1 < len(visits) else []
                if save is not None:
                    S.op(ACT, lambda: nc.scalar.copy(sbsave[:, :, :], S_b[:, :, :]), reads=SbB, writes=[sbsB])
                    S.dma(SP, sbin[save], sbsave[:, :, :], reads=[sbsB])
            if d == 0:
                state_update(par, 0, g, tk[n % 2], tkB[n % 2], S_f, SfB, n % 2)
            else:
                state_update(par, 1, g, tk[n % 2], tkB[n % 2], S_b, SbB, n % 2)
            if g < len(nsteps):
                nsteps[g]()
        S.op(ACT, lambda: nc.scalar.copy(Sbf_f[:, :, :], S_f[:, :, :]), reads=SfB, writes=SbfB)
        S.barrier()
        ph_s1.close()
        if dbg and stage == 3:
            S.dma(SP, dbg_d[:, 0:4096], S_f[:, :, :].rearrange("p g e -> p (g e)"), reads=SfB)
            S.barrier()
        if stage == 3:
            return finish(nc, S, es, out_d)

        NL = 3
        tk2 = [sb(f"tk2_{i}", [128, 640], BF16, ph) for i in range(NL)]
        tk2B = [Buf(f"tk2_{i}") for i in range(NL)]
        bct = [sb(f"bct{i}", [128, 2, 128], BF16, ph) for i in range(NL)]
        zt = [sb(f"zt{i}", [128, 4, 128], BF16, ph) for i in range(NL)]
        rg = [sb(f"rg{i}", [3, 2, 1024], BF16, ph) for i in range(NL)]
        sbl = [sb(f"sbl{i}", [128, 512], BF16, ph) for i in range(NL)]
        ldB = [Buf(f"ld{i}") for i in range(NL)]
        cbm = [sb(f"cbm{i}", [128, 2, 128], BF16, ph) for i in range(2)]
        cbmB = [Buf("cbm0"), Buf("cbm1")]
        Lt = [sb(f"Lt{i}", [128, 16, 128], BF16, ph) for i in range(2)]
        LtB = [[Buf(f"Lt{i}_{q}") for q in range(4)] for i in range(2)]
        Gt = [sb(f"Gt{i}", [128, 16, 128], BF16, ph) for i in range(2)]
        GtB = [[Buf(f"Gt{i}_{q}") for q in range(4)] for i in range(2)]
        xsD = [sb(f"xsD{i}", [128, 512], BF16, ph) for i in range(2)]
        xsDB = [Buf("xsD0"), Buf("xsD1")]
        yo = sb("yo", [128, 2, 512], BF16, ph)
        yoB = Buf("yo")
        ytot = sb("ytot", [128, 512], BF16, ph)
        ytB = Buf("ytot")
        ygp = sb("ygp", [128, 4, 128], BF16, ph)
        ygpB = Buf("ygp")
        ygs = sb("ygs", [128, 4, 128], BF16, ph)
        ygsB = Buf("ygs")
        gtmp = sb("gtmp", [128, 128], F32, ph)
        gtB = Buf("gtmp")
        items2 = [(c, g) for c in range(NCH) for g in range(8)]
        N2 = len(items2)

        def s2_load(n):
            c, g = items2[n]
            i = n % NL
            if g == 0 and c > 0:
                chunk_scalars(c % 2, "own", c * 128, rchunk=c)
            S.dma(SP, tk2[i][:, :], tok_own[c * 128:(c + 1) * 128, g, :], writes=[tk2B[i]])
            S.dma(SP, bct[i][:, :, :], featBC[g, :, :, c * 128:(c + 1) * 128].rearrange("w n t -> n w t"), writes=[ldB[i]])
            S.dma(SP, zt[i][:, :, :], zT_s[g * 512:(g + 1) * 512, c * 128:(c + 1) * 128].rearrange("(j p) t -> p j t", p=128),
                  writes=[ldB[i]])
            S.dma(SP, rg[i][:, :, :], rscr[c, :, :, :].rearrange("j (d h) q -> j d h q", d=2)[:, :, 8 * g:8 * g + 8, :]
                  .rearrange("j d h q -> j d (h q)"), reads=[rscrB[c]], writes=[ldB[i]])
            S.dma(SP, sbl[i][:, :], sbin[c, :, g, :], writes=[ldB[i]])

        bk = {"n": 0}

        def a_prep(n):
            c, g = items2[n]
            i = n % NL
            a = n % 2
            S.op(PE, lambda: nc.tensor.matmul(pf[0][:, 0:128], bct[i][:, 0, :], bct[i][:, 1, :], start=True, stop=True),
                 reads=[ldB[i]], writes=[p0aB])
            S.op(DVE, lambda: nc.vector.tensor_tensor(cbm[a][:, 0, :], pf[0][:, 0:128], tri_f, ALU.mult),
                 reads=[p0aB, cB], writes=[cbmB[a]])
            S.op(DVE, lambda: nc.vector.tensor_tensor(cbm[a][:, 1, :], pf[0][:, 0:128], tri_b, ALU.mult),
                 reads=[p0aB, cB], writes=[cbmB[a]])
            S.op(POOL, lambda: nc.gpsimd.tensor_tensor(xsD[a][:, :], tk2[i][:, 0:512], drep[:, g * 512:(g + 1) * 512], ALU.mult),
                 reads=[tk2B[i], pB], writes=[xsDB[a]])
            state_prep(c % 2, 0, g, tk2[i], tk2B[i], S_f, SfB, a)

        def a_quarter(n, qd):
            c, g = items2[n]
            i = n % NL
            a = n % 2
            sc = scs[c % 2]
            d, half = qd // 2, qd % 2
            bank = 1 + (bk["n"] % 2)
            bk["n"] += 1
            S.op(PE, lambda: nc.tensor.matmul(pf[bank][:, :], ones3[:, :], rg[i][0:3, d, half * 512:(half + 1) * 512],
                                              start=True, stop=True),
                 reads=[ldB[i], pB], writes=[pfB[bank]])
            for hh in range(4):
                idx = d * 8 + half * 4 + hh
                h = 8 * g + half * 4 + hh
                S.op(ACT, lambda: nc.scalar.activation(Lt[a][:, idx, :], pf[bank][:, hh * 128:(hh + 1) * 128], AF.Exp,
                                                       bias=sc["biasL"][:, d * 64 + h:d * 64 + h + 1]),
                     reads=[pfB[bank], sc["B"]], writes=[LtB[a][qd]])
            i0 = d * 8 + half * 4
            S.op(DVE, lambda: nc.vector.scalar_tensor_tensor(Gt[a][:, i0:i0 + 4, :], Lt[a][:, i0:i0 + 4, :], 3.0e38,
                                                             cbm[a][:, d, :].unsqueeze(1).to_broadcast([128, 4, 128]),
                                                             ALU.min, ALU.mult),
                 reads=[LtB[a][qd], cbmB[a]], writes=[GtB[a][qd]])

        def b1(n):
            c, g = items2[n]
            i = n % NL
            a = n % 2
            sc = scs[c % 2]
            S.op(PE, lambda: nc.tensor.matmul(pf[3][:, :], ident_b[:, :], xsD[a][:, :], start=True, stop=False),
                 reads=[xsDB[a], cB], writes=[pfB[3]], inc=False)
            for hh8 in range(8):
                for d in range(2):
                    last = (hh8 == 7 and d == 1)
                    qd = d * 2 + hh8 // 4
                    S.op(PE, lambda: nc.tensor.matmul(pf[3][:, hh8 * 64:(hh8 + 1) * 64], Gt[a][:, d * 8 + hh8, :],
                                                      tk2[i][:, hh8 * 64:(hh8 + 1) * 64], start=False, stop=(d == 1),
                                                      skip_group_check=True),
                         reads=[GtB[a][qd], tk2B[i]], writes=[pfB[3]], inc=last)
            yoff(n, 0)

        def yoff(n, d):
            c, g = items2[n]
            i = n % NL
            sc = scs[c % 2]
            rhs = Sbf_f[:, g, :] if d == 0 else sbl[i][:, :]
            S.op(PE, lambda: nc.tensor.matmul(pf[4][:, :], bct[i][:, 1, :], rhs, start=True, stop=True),
                 reads=[ldB[i], SbfB[g]], writes=[pfB[4]])
            S.op(DVE, lambda: nc.vector.tensor_tensor(v3(yo[:, d, :]), v3(pf[4][:, :]), bc8(sc["dec"], d * 64 + 8 * g), ALU.mult),
                 reads=[pfB[4], sc["B"]], writes=[yoB])

        def b2(n):
            yoff(n, 1)
            S.op(DVE, lambda: nc.vector.tensor_tensor(yo[:, 0, :], yo[:, 0, :], yo[:, 1, :], ALU.add), reads=[yoB], writes=[yoB])
            S.op(DVE, lambda: nc.vector.tensor_tensor(ytot[:, :], pf[3][:, :], yo[:, 0, :], ALU.add),
                 reads=[pfB[3], yoB], writes=[ytB])

        def b3(n):
            c, g = items2[n]
            i = n % NL
            a = n % 2
            for j in range(4):
                S.op(PE, lambda: nc.tensor.transpose(pb[0][:, j * 128:(j + 1) * 128], ytot[:, j * 128:(j + 1) * 128], ident_b[:]),
                     reads=[ytB, cB], writes=[pbB[0]], inc=(j == 3))
            S.op(DVE, lambda: nc.vector.tensor_tensor(ygp[:, :, :].rearrange("p j q -> p (j q)"), pb[0][:, 0:512],
                                                      zt[i][:, :, :].rearrange("p j q -> p (j q)"), ALU.mult),
                 reads=[pbB[0], ldB[i]], writes=[ygpB])
            for j in range(4):
                S.op(PE, lambda: nc.tensor.matmul(gram_ps, ygp[:, j, :], ygp[:, j, :],
                                                  start=(g == 0 and j == 0), stop=(g == 7 and j == 3), skip_group_check=True),
                     reads=[ygpB], writes=[gramB], inc=(j == 3))
            S.op(POOL, lambda: nc.gpsimd.tensor_tensor(ygs[:, :, :], ygp[:, :, :],
                                                       ng[:, g * 4:g * 4 + 4].unsqueeze(2).to_broadcast([128, 4, 128]), ALU.mult),
                 reads=[ygpB, pB], writes=[ygsB])
            S.dma(SP, ygT_s[g * 512:(g + 1) * 512, c * 128:(c + 1) * 128].rearrange("(j p) q -> p j q", p=128), ygs[:, :, :],
                  reads=[ygsB])
            state_fin(g, tk2[i], tk2B[i], S_f, SfB, a, 5)
            S.op(ACT, lambda: nc.scalar.copy(Sbf_f[:, g, :], S_f[:, g, :]), reads=[SfB[g]], writes=[SbfB[g]])
            if g == 7:
                S.op(DVE, lambda: nc.vector.tensor_tensor(gtmp[:, :], gram_ps, ident_f, ALU.mult),
                     reads=[gramB, cB], writes=[gtB])
                S.op(DVE, lambda: nc.vector.reduce_sum(small[:, 0:1], gtmp[:, :], axis=AX.X), reads=[gtB], writes=[smB])
                S.op(ACT, lambda: nc.scalar.activation(small[:, 1:2], small[:, 0:1], AF.Sqrt, bias=EPS, scale=1.0 / DI),
                     reads=[smB], writes=[smB])
                S.op(DVE, lambda: nc.vector.reciprocal(rstd_y[:, c:c + 1], small[:, 1:2]), reads=[smB], writes=[ryB])

        chunk_scalars(0, "own", 0, rchunk=0)
        s2_load(0)
        s2_load(1)
        a_prep(0)
        for qd in range(4):
            a_quarter(0, qd)
        for n in range(N2):
            if n + 2 < N2:
                s2_load(n + 2)
            nx = n + 1 < N2
            if nx:
                a_prep(n + 1)
                a_quarter(n + 1, 0)
                a_quarter(n + 1, 1)
            b1(n)
            if nx:
                a_quarter(n + 1, 2)
            b2(n)
            if nx:
                a_quarter(n + 1, 3)
            b3(n)
        S.barrier()
        if dbg and stage == 4:
            S.dma(SP, dbg_d[:, 0:16], rstd_y[:, :], reads=[ryB])
            S.dma(POOL, dbg_d[:, 128:128 + 2048], ygT_s[0:128, :])
            S.dma(POOL, dbg_d[:, 2176:2176 + 1024], ygT_s[DI - 128:DI, 0:1024])
            S.barrier()
    es_dt.close()
    if stage == 4:
        es_wt.close()
        return finish(nc, S, es, out_d)

    def out_proj(srcT, w_dram, resid, layer, use_rstd, final):
        with ExitStack() as ph:
            nyb = 2
            yblk = [sb(f"yblk{i}", [128, 32, 512], BF16, ph) for i in range(nyb)]
            yblkB = [Buf(f"yblk{i}") for i in range(nyb)]
            wb = [sb(f"wb{i}", [128, 32, 512], BF16, ph) for i in range(2)]
            wbB = [Buf("wb0"), Buf("wb1")]
            xr = [sb(f"xr{i}", [128, 512], F32, ph) for i in range(2)]
            xrB = [Buf("xr0"), Buf("xr1")]
            x2 = sb("x2", [128, 4, D], F32, ph) if final else None
            x2B = [Buf(f"x2_{i}") for i in range(4)]
            ot = [sb(f"ot{i}", [128, 512], F32, ph) for i in range(2)]
            otB = [Buf("ot0"), Buf("ot1")]
            jk = sb("jk", [128, D], BF16, ph) if final else None
            jkB = Buf("jk")
            seq = [(tb, dblk) for tb in range(4) for dblk in range(4)] if final else \
                  [(tb, dblk) for dblk in range(4) for tb in range(4)]
            wi = {"n": 0, "cur": None, "slot": None}
            yi = {"n": 0, "cur": None, "slot": None}

            def get_w(dblk):
                if wi["cur"] == dblk:
                    return wi["slot"]
                i = wi["n"] % 2
                wi["n"] += 1
                for hf in range(2):
                    S.dma(POOL, wb[i][:, hf * 16:(hf + 1) * 16, :], w_dram[dblk, :, hf * 16:(hf + 1) * 16, :], writes=[wbB[i]])
                wi["cur"], wi["slot"] = dblk, i
                return i

            def get_y(tb):
                if yi["cur"] == tb:
                    return yi["slot"]
                i = yi["n"] % nyb
                yi["n"] += 1
                S.dma(SP, yblk[i][:, :, :], srcT[:, tb * 512:(tb + 1) * 512].rearrange("(j p) t -> p j t", p=128), writes=[yblkB[i]])
                yi["cur"], yi["slot"] = tb, i
                return i
            wnext = get_w(seq[0][1])
            ynext = get_y(seq[0][0])
            cnt = 0
            for si, (tb, dblk) in enumerate(seq):
                wcur, ycur = wnext, (ynext if nyb == 2 else get_y(tb))
                if si + 1 < len(seq):
                    wnext = get_w(seq[si + 1][1])
                    if nyb == 2:
                        ynext = get_y(seq[si + 1][0])
                for tt in range(4):
                    tok0 = tb * 512 + tt * 128
                    ch = tok0 // 128
                    k = cnt % 2
                    cnt += 1
                    S.dma(SP, xr[k][:, :], resid[tok0:tok0 + 128, dblk * 512:(dblk + 1) * 512], writes=[xrB[k]])
                    for j in range(32):
                        S.op(PE, lambda: nc.tensor.matmul(pf[k][:, :], yblk[ycur][:, j, tt * 128:(tt + 1) * 128], wb[wcur][:, j, :],
                                                          start=(j == 0), stop=(j == 31)),
                             reads=[yblkB[ycur], wbB[wcur]], writes=[pfB[k]], inc=(j == 31))
                    dst = x2[:, tt, dblk * 512:(dblk + 1) * 512] if final else ot[k][:, :]
                    dB = x2B[tt] if final else otB[k]
                    if use_rstd:
                        S.op(DVE, lambda: nc.vector.scalar_tensor_tensor(ot[k][:, :], pf[k][:, :], rstd_y[:, ch:ch + 1],
                                                                         gate_rep[:, layer, dblk * 512:(dblk + 1) * 512],
                                                                         ALU.mult, ALU.mult),
                             reads=[pfB[k], ryB, grB], writes=[otB[k]])
                    else:
                        S.op(DVE, lambda: nc.vector.tensor_tensor(ot[k][:, :], pf[k][:, :],
                                                                  gate_rep[:, layer, dblk * 512:(dblk + 1) * 512], ALU.mult),
                             reads=[pfB[k], grB], writes=[otB[k]])
                    S.op(DVE, lambda: nc.vector.tensor_tensor(dst, ot[k][:, :], xr[k][:, :], ALU.add),
                         reads=[otB[k], xrB[k]], writes=[dB] if final else [otB[k]])
                    if not final:
                        S.dma(SP, x1_s[tok0:tok0 + 128, dblk * 512:(dblk + 1) * 512], ot[k][:, :], reads=[otB[k]])
                    elif dblk == 3:
                        S.op(ACT, lambda: nc.scalar.activation(jk[:, :], x2[:, tt, :], AF.Square, accum_out=small[:, 8 + tt:9 + tt]),
                             reads=[x2B[tt]], writes=[jkB, smB])
                        S.op(ACT, lambda: nc.scalar.activation(small[:, 16 + tt:17 + tt], small[:, 8 + tt:9 + tt], AF.Sqrt,
                                                               bias=EPS, scale=1.0 / D), reads=[smB], writes=[smB])
                        S.op(DVE, lambda: nc.vector.reciprocal(small[:, 16 + tt:17 + tt], small[:, 16 + tt:17 + tt]),
                             reads=[smB], writes=[smB])
                        S.op(DVE, lambda: nc.vector.scalar_tensor_tensor(x2[:, tt, :], x2[:, tt, :], small[:, 16 + tt:17 + tt],
                                                                         fng_rep[:, :], ALU.mult, ALU.mult),
                             reads=[x2B[tt], smB, grB], writes=[x2B[tt]])
                        S.dma(SP, out_d[tok0:tok0 + 128, :], x2[:, tt, :], reads=[x2B[tt]])
            S.barrier()

    out_proj(ygT_s, w_out0, x_own, 0, True, False)
    if dbg and stage >= 5:
        S.dma(SP, dbg_d[:, 0:2048], x1_s[0:128, :])
        S.dma(SP, dbg_d[:, 2048:4096], x1_s[T - 128:T, :])
        S.barrier()
    if stage == 5:
        return finish(nc, S, es, out_d)

    blocks4 = [(0, 512), (512, 512), (1024, 512), (1536, 512)]
    with ExitStack() as ph:
        hlt = sb("hlt1", [128, KT, T], BF16, ph)
        hltB = [Buf(f"hlt1_{k}") for k in range(KT)]
        with ExitStack() as ph2:
            jobs = [(x1_s[i * 128:(i + 1) * 128, :], 0, i * 128, 0, 128) for i in range(16)]
            token_prep(ph2, hlt, hltB, jobs, 1)
            S.barrier()
        ft = [sb(f"ft1_{i}", [128, T], BF16, ph) for i in range(2)]
        ftB = [Buf("ft1_0"), Buf("ft1_1")]
        with ExitStack() as ph2:
            vtok = sb("vtok", [128, 16, 512], BF16, ph2)
            vtokB = Buf("vtok")
            nxt = load_w(w_in1[32])
            for j in range(32):
                fi = j % 2
                nsrc = w_in1[32 + j + 1] if j + 1 < 32 else None

                def ev(ps, bB, c0, n, fi=fi):
                    S.op(ACT, lambda: nc.scalar.activation(ft[fi][:, c0:c0 + n], ps, AF.Gelu), reads=[bB], writes=[ftB[fi]])
                nxt = inproj(None, hlt, hltB, blocks4, ev, nsrc, pre=nxt)
                for c8 in range(0, 16, 8):
                    bank = (c8 // 8) % 2
                    for ci in range(8):
                        col0 = (c8 + ci) * 128
                        S.op(PE, lambda: nc.tensor.transpose(pb[bank][:, ci * 128:(ci + 1) * 128], ft[fi][:, col0:col0 + 128], ident_b[:]),
                             reads=[ftB[fi], cB], writes=[pbB[bank]], inc=(ci == 7))
                    evac_copy(vtok[:, c8:c8 + 8, (j % 4) * 128:(j % 4) * 128 + 128],
                              pb[bank][:, 0:1024].rearrange("p (c q) -> p c q", q=128), [pbB[bank]], [vtokB])
                if j % 4 == 3:
                    S.dma(SP, gv_s[:, (j // 4) * 512:(j // 4 + 1) * 512].rearrange("(c p) e -> p c e", p=128), vtok[:, :, :],
                          reads=[vtokB])
            S.barrier()
        with ExitStack() as ph2:
            gr = [sb(f"gr{i}", [128, DI], BF16, ph2) for i in range(2)]
            grB_ = [Buf("gr0"), Buf("gr1")]
            jk2 = sb("jk2", [128, DI], BF16, ph2)
            jk2B = Buf("jk2")
            lst = sb("lst", [128, 2, 8], F32, ph2)
            lstB = [Buf("lst0"), Buf("lst1")]
            S.dma(SP, gr[0][:, :], gv_s[0:128, :], writes=[grB_[0]])
            for c in range(16):
                i = c % 2
                if c + 1 < 16:
                    S.dma(SP, gr[1 - i][:, :], gv_s[(c + 1) * 128:(c + 2) * 128, :], writes=[grB_[1 - i]])
                l = lst[:, i, :]
                S.op(ACT, lambda: nc.scalar.activation(jk2[:, :], gr[i][:, :], AF.Square, accum_out=l[:, 0:1]),
                     reads=[grB_[i]], writes=[jk2B, lstB[i]])
                S.op(DVE, lambda: nc.vector.reduce_sum(l[:, 1:2], gr[i][:, :], axis=AX.X), reads=[grB_[i]], writes=[lstB[i]])
                S.op(DVE, lambda: nc.vector.tensor_scalar(l[:, 2:3], l[:, 1:2], 1.0 / DI, None, ALU.mult), reads=[lstB[i]], writes=[lstB[i]])
                S.op(DVE, lambda: nc.vector.tensor_tensor(l[:, 3:4], l[:, 2:3], l[:, 2:3], ALU.mult), reads=[lstB[i]], writes=[lstB[i]])
                S.op(DVE, lambda: nc.vector.scalar_tensor_tensor(l[:, 4:5], l[:, 0:1], 1.0 / DI, l[:, 3:4], ALU.mult, ALU.subtract),
                     reads=[lstB[i]], writes=[lstB[i]])
                S.op(ACT, lambda: nc.scalar.activation(l[:, 5:6], l[:, 4:5], AF.Sqrt, bias=EPS, scale=1.0), reads=[lstB[i]], writes=[lstB[i]])
                S.op(DVE, lambda: nc.vector.reciprocal(l[:, 5:6], l[:, 5:6]), reads=[lstB[i]], writes=[lstB[i]])
                S.op(DVE, lambda: nc.vector.tensor_scalar(gr[i][:, :], gr[i][:, :], l[:, 2:3], l[:, 5:6], ALU.subtract, ALU.mult),
                     reads=[lstB[i], grB_[i]], writes=[grB_[i]])
                S.dma(SP, gv_s[c * 128:(c + 1) * 128, :], gr[i][:, :], reads=[grB_[i]])
            S.barrier()
        with ExitStack() as ph2:
            wsf = sb("wsf", [128, 16, 128], F32, ph2)
            wsb = sb("wsb", [128, 16, 128], BF16, ph2)
            bsr = sb("bsr", [1, 16 * 128], F32, ph2)
            lng = sb("lng", [128, 32], F32, ph2)
            lnb = sb("lnb", [128, 32], F32, ph2)
            bbt = sb("bbt", [128, 32, 128], F32, ph2)
            bsrep = sb("bsrep", [128, 128], F32, ph2)
            p1B = Buf("l1params")
            bbB = Buf("bbt")
            bsrepB = Buf("bsrep")
            S.dma(SP, wsf[:, :, :], wsT_d[:, :, :], writes=[p1B])
            S.dma(SP, bsr[:, :], bs_d[:, :], writes=[p1B])
            S.dma(SP, lng[:, :], lng_d[:, :], writes=[p1B])
            S.dma(SP, lnb[:, :], lnb_d[:, :], writes=[p1B])
            S.op(DVE, lambda: nc.vector.tensor_copy(wsb[:, :, :], wsf[:, :, :]), reads=[p1B], writes=[p1B])
            for grp in range(16):
                S.op(PE, lambda: nc.tensor.matmul(pf[0][:, 0:128], ones_f[:, :], wsf[:, grp, :], start=True, stop=True),
                     reads=[p1B, cB], writes=[pfB[0]])
                S.op(PE, lambda: nc.tensor.matmul(pf[1][:, 0:128], ones_f[0:1, :], bsr[0:1, grp * 128:(grp + 1) * 128], start=True, stop=True),
                     reads=[p1B, cB], writes=[pfB[1]])
                S.op(ACT, lambda: nc.scalar.copy(bsrep[:, :], pf[1][:, 0:128]), reads=[pfB[1]], writes=[bsrepB])
                for jj in range(2):
                    j = grp * 2 + jj
                    S.op(DVE, lambda: nc.vector.scalar_tensor_tensor(bbt[:, j, :], pf[0][:, 0:128], lnb[:, j:j + 1], bsrep[:, :],
                                                                     ALU.mult, ALU.add),
                         reads=[pfB[0], p1B, bsrepB], writes=[bbB])
            ug = sb("ug", [128, T], BF16, ph2)
            ugB = Buf("ug")
            vnt = [sb(f"vnt{i}", [128, 16, 128], BF16, ph2) for i in range(2)]
            vntB = [Buf("vnt0"), Buf("vnt1")]
            sTt = [sb(f"sTt{i}", [128, T], BF16, ph2) for i in range(2)]
            sTtB = [Buf("sTt0"), Buf("sTt1")]
            tv = sb("tv", [128, 512], F32, ph2)
            tvB = Buf("tv")

            def usrc(j):
                return w_in1[j]

            def gsrc(j):
                return w_in1[64 + j]
            nxt = load_w(usrc(0))
            for j in range(32):
                i = j % 2
                grp = j // 2
                S.dma(SP, vnt[i][:, :, :], gv_s[:, j * 128:(j + 1) * 128].rearrange("(c p) e -> p c e", p=128), writes=[vntB[i]])

                def ev_u(ps, bB, c0, n):
                    S.op(ACT, lambda: nc.scalar.activation(ft[0][:, c0:c0 + n], ps, AF.Gelu), reads=[bB], writes=[ftB[0]])

                def ev_g(ps, bB, c0, n):
                    S.op(ACT, lambda: nc.scalar.activation(ft[1][:, c0:c0 + n], ps, AF.Silu), reads=[bB], writes=[ftB[1]])
                nxt = inproj(None, hlt, hltB, blocks4, ev_u, gsrc(j), pre=nxt)
                nxt = inproj(None, hlt, hltB, blocks4, ev_g, usrc(j + 1) if j + 1 < 32 else None, pre=nxt)
                S.op(DVE, lambda: nc.vector.tensor_tensor(ug[:, :], ft[0][:, :], ft[1][:, :], ALU.mult),
                     reads=[ftB[0], ftB[1]], writes=[ugB])
                for c4 in range(4):
                    bank = 4 + (c4 % 2)
                    for cc in range(4):
                        c = c4 * 4 + cc
                        S.op(PE, lambda: nc.tensor.matmul(pf[bank][:, cc * 128:(cc + 1) * 128], vnt[i][:, c, :], wsb[:, grp, :],
                                                          start=True, stop=True),
                             reads=[vntB[i], p1B], writes=[pfB[bank]], inc=(cc == 3))
                    S.op(DVE, lambda: nc.vector.scalar_tensor_tensor(tv[:, :].rearrange("p (a q) -> p a q", q=128),
                                                                     pf[bank][:, :].rearrange("p (a q) -> p a q", q=128),
                                                                     lng[:, j:j + 1],
                                                                     bbt[:, j, :].unsqueeze(1).to_broadcast([128, 4, 128]),
                                                                     ALU.mult, ALU.add),
                         reads=[pfB[bank], p1B, bbB], writes=[tvB])
                    S.op(DVE, lambda: nc.vector.tensor_tensor(sTt[i][:, c4 * 512:(c4 + 1) * 512], tv[:, :], ug[:, c4 * 512:(c4 + 1) * 512],
                                                              ALU.mult),
                         reads=[tvB, ugB], writes=[sTtB[i]])
                S.dma(SP, sT_s[j * 128:(j + 1) * 128, :], sTt[i][:, :], reads=[sTtB[i]])
            S.barrier()
    es_wt.close()
    out_proj(sT_s, w_out1, x1_s, 1, False, True)
    return finish(nc, S, es, out_d)


def finish(nc, S, es, out_d):
    S.barrier()
    es.close()
    return nc


def make_consts():
    k = np.arange(128)
    ident = np.eye(128, dtype=np.float32)
    tri_f = (k[:, None] <= k[None, :]).astype(np.float32)
    tri_b = (k[:, None] >= k[None, :]).astype(np.float32)
    e_last = np.zeros((128, 128), np.float32)
    e_last[127, :] = 1.0
    e_first = np.zeros((128, 128), np.float32)
    e_first[0, :] = 1.0
    return np.ascontiguousarray(np.concatenate([ident, tri_f, tri_b, e_last, e_first], axis=1))


def pack_inputs(r, x, c, ctx, c_ctx, mod_w, mod_b, ssd_w_in, ssd_conv_w, ssd_conv_b, ssd_dt_bias,
                ssd_a_log, ssd_d, ssd_norm_g, ssd_w_out, smlp_w_in, smlp_ln_g, smlp_ln_b,
                smlp_w_s, smlp_b_s, smlp_w_out, final_norm_g, shared):
    b, half = r // 2, r % 2
    flip = half == 1
    f32 = np.float32
    xs_ = x[b][::-1] if flip else x[b]
    cx_ = ctx[b][::-1] if flip else ctx[b]
    m = {}
    m["x_own"] = np.ascontiguousarray(xs_[:T], dtype=f32)
    m["x_oth"] = np.ascontiguousarray(xs_[T:], dtype=f32)
    m["x_ctx"] = np.ascontiguousarray(cx_, dtype=f32)
    cv = np.stack([c[b], c_ctx], axis=0)
    m["c2"] = np.ascontiguousarray(cv.reshape(2, KT, 128).transpose(2, 1, 0), dtype=f32)
    key = "flip" if flip else "noflip"
    if key not in shared:
        s = {}
        d_order = [1, 0] if flip else [0, 1]
        wdt = ssd_w_in[0][:, 10240:10368].reshape(D, 2, H)[:, d_order, :].reshape(D, 128)
        s["w_dt"] = np.ascontiguousarray(wdt.reshape(KT, 128, 128).transpose(1, 0, 2), dtype=f32)
        dtb = ssd_dt_bias[0][d_order].reshape(128)
        alg = ssd_a_log[0][d_order].reshape(128)
        s["dtp"] = np.ascontiguousarray(np.stack([dtb, alg], axis=1), dtype=f32)
        cw = ssd_conv_w[0][::-1] if flip else ssd_conv_w[0]
        cp = np.concatenate([cw.T, ssd_conv_b[0][:, None]], axis=1)
        s["convp"] = np.ascontiguousarray(cp.reshape(48, 128, 8).transpose(1, 0, 2), dtype=f32)
        ws = smlp_w_s[0]
        bs = smlp_b_s[0]
        if flip:
            ws = ws[:, ::-1, ::-1]
            bs = bs[:, ::-1]
        s["wsT"] = np.ascontiguousarray(ws.transpose(2, 0, 1), dtype=f32)
        s["bs"] = np.ascontiguousarray(bs.reshape(1, 16 * 128), dtype=f32)
        shared[key] = s
    m.update(shared[key])
    if "common" not in shared:
        s = {}
        s["mod_w"] = np.ascontiguousarray(mod_w.reshape(2, KT, 128, 48, 128).transpose(0, 3, 2, 1, 4), dtype=f32)
        mb = mod_b[:, :4096].reshape(2, 32, 128).transpose(2, 0, 1)
        s["modb_col"] = np.ascontiguousarray(mb, dtype=f32)
        s["modb_gate"] = np.ascontiguousarray(mod_b[:, 4096:].reshape(1, 2, D), dtype=f32)
        s["w_in0"] = np.ascontiguousarray(ssd_w_in[0].reshape(KT, 128, 81, 128).transpose(2, 1, 0, 3), dtype=f32)
        s["drep"] = np.ascontiguousarray(np.broadcast_to(np.repeat(ssd_d[0], 64)[None, :], (128, DI)), dtype=f32)
        s["ng"] = np.ascontiguousarray(ssd_norm_g[0].reshape(32, 128).T, dtype=f32)
        s["w_out0"] = np.ascontiguousarray(ssd_w_out[0].reshape(32, 128, 4, 512).transpose(2, 1, 0, 3), dtype=f32)
        s["w_in1"] = np.ascontiguousarray(smlp_w_in[0].reshape(KT, 128, 96, 128).transpose(2, 1, 0, 3), dtype=f32)
        s["lng"] = np.ascontiguousarray(smlp_ln_g[0].reshape(32, 128).T, dtype=f32)
        s["lnb"] = np.ascontiguousarray(smlp_ln_b[0].reshape(32, 128).T, dtype=f32)
        s["w_out1"] = np.ascontiguousarray(smlp_w_out[0].reshape(32, 128, 4, 512).transpose(2, 1, 0, 3), dtype=f32)
        s["fng"] = np.ascontiguousarray(final_norm_g.reshape(1, D), dtype=f32)
        s["consts"] = make_consts()
        shared["common"] = s
    m.update(shared["common"])
    return m


def kernel(**inputs):
    inputs = {k: np.asarray(v) for k, v in inputs.items()}
    shared = {}
    in_maps = [pack_inputs(r, shared=shared, **inputs) for r in range(8)]
    nc = build()
    res = run_bass_kernel_spmd(nc, in_maps, core_ids=list(range(8)))
    out = np.zeros((4, 2 * T, D), np.float32)
    for r in range(8):
        b, half = r // 2, r % 2
        o = np.asarray(res.results[r]["out"], dtype=np.float32)
        if half == 0:
            out[b, :T] = o
        else:
            out[b, T:] = o[::-1]
    return out
```
